# Optimizing a Trainium2 kernel written in Bass

```python
import math
import jax, jax.numpy as jnp
from jax import lax
import numpy as np

D_MODEL = 2048
BATCH = 1
SEQ = 8192
DEPTH = 2

CHUNK = 64
GDN_HEADS = 8
GDN_HEAD_DIM = 128
GDN_DIM = GDN_HEADS * GDN_HEAD_DIM
SHORT_CONV = 4
RET_HEADS = 4
RET_QK_DIM = 128
RET_V_DIM = 256
RET_QK = RET_HEADS * RET_QK_DIM
RET_V = RET_HEADS * RET_V_DIM
RET_DECAY_BASE = 5.0
ROPE_BASE = 10000.0
CONV_DIM = D_MODEL // 2
CONV_WIDTH = 31
N_BRANCH = 3
D_FF = 4 * D_MODEL
NORM_EPS = 1e-6
MAX_STREAM_OFFSET = 4096

IN_SIZES = (GDN_DIM, GDN_DIM, GDN_DIM, GDN_DIM, GDN_HEADS, GDN_HEADS,
            RET_QK, RET_QK, RET_V, RET_V, 2 * CONV_DIM, N_BRANCH * D_MODEL)
IN_WIDTH = 4 * GDN_DIM + 2 * GDN_HEADS + 2 * RET_QK + 2 * RET_V + 2 * CONV_DIM + N_BRANCH * D_MODEL

kernel_name = "hybrid_gdn_retnet_conformer_adaln"


def _split_points(sizes):
    pts, acc = [], 0
    for s in sizes[:-1]:
        acc += s
        pts.append(acc)
    return pts


def rms_norm(x, w, eps=NORM_EPS):
    xf = x.astype(jnp.float32)
    y = xf * lax.rsqrt(jnp.mean(xf * xf, axis=-1, keepdims=True) + eps)
    return (y * w.astype(jnp.float32)).astype(x.dtype)


def layer_norm_f32(x, eps=1e-5):
    mu = jnp.mean(x, axis=-1, keepdims=True)
    var = jnp.mean(jnp.square(x - mu), axis=-1, keepdims=True)
    return (x - mu) * lax.rsqrt(var + eps)


def l2_norm(x, eps=1e-6):
    return x * lax.rsqrt(jnp.sum(x * x, axis=-1, keepdims=True) + eps)


def causal_depthwise_conv(x, w):
    k_width, ch = w.shape
    return lax.conv_general_dilated(
        x, w[:, None, :].astype(x.dtype), window_strides=(1,),
        padding=[(k_width - 1, 0)], dimension_numbers=('NWC', 'WIO', 'NWC'),
        feature_group_count=ch)


def rotary(x, positions):
    half = x.shape[-1] // 2
    inv_freq = ROPE_BASE ** (-jnp.arange(half, dtype=jnp.float32) / half)
    ang = positions.astype(jnp.float32)[..., None] * inv_freq
    cos = jnp.cos(ang)[:, :, None, :]
    sin = jnp.sin(ang)[:, :, None, :]
    x1, x2 = x[..., :half], x[..., half:]
    return jnp.concatenate([x1 * cos - x2 * sin, x1 * sin + x2 * cos], axis=-1)


def _to_chunks(a):
    b, t, h = a.shape[:3]
    a = a.reshape((b, t // CHUNK, CHUNK, h) + a.shape[3:])
    return jnp.moveaxis(a, 3, 1)


def _from_chunks(a):
    b, h, n, c, d = a.shape
    return jnp.transpose(a, (0, 2, 3, 1, 4)).reshape(b, n * c, h, d)


def gated_delta_rule(q, k, v, g, beta):
    bsz, _, h, dk = q.shape
    dv = v.shape[-1]
    q = _to_chunks(q * dk ** -0.5)
    k = _to_chunks(k)
    v = _to_chunks(v)
    g = jnp.cumsum(_to_chunks(g), axis=-1)
    beta = _to_chunks(beta)
    idx = jnp.arange(CHUNK)
    causal = idx[:, None] >= idx[None, :]
    strict = idx[:, None] > idx[None, :]
    decay = jnp.exp(jnp.where(causal, g[..., :, None] - g[..., None, :], -jnp.inf))
    k_beta = k * beta[..., None]
    v_beta = v * beta[..., None]
    lower = jnp.where(strict, jnp.einsum('bhncd,bhnsd->bhncs', k_beta, k) * decay, 0.0)
    eye = jnp.eye(CHUNK, dtype=jnp.float32)
    t_inv = lax.linalg.triangular_solve(eye + lower, jnp.broadcast_to(eye, lower.shape),
                                        left_side=True, lower=True)
    u = jnp.einsum('bhncs,bhnse->bhnce', t_inv, v_beta)
    w = jnp.einsum('bhncs,bhnsd->bhncd', t_inv, k_beta * jnp.exp(g)[..., None])
    attn = jnp.where(causal, jnp.einsum('bhncd,bhnsd->bhncs', q, k) * decay, 0.0)
    q_dec = q * jnp.exp(g)[..., None]
    k_dec = k * jnp.exp(g[..., -1:] - g)[..., None]
    g_last = jnp.exp(g[..., -1])

    def step(state, xs):
        q_i, k_i, u_i, w_i, a_i, gl_i = xs
        v_new = u_i - jnp.einsum('bhcd,bhde->bhce', w_i, state)
        o = jnp.einsum('bhcd,bhde->bhce', q_i, state) + jnp.einsum('bhcs,bhse->bhce', a_i, v_new)
        state = state * gl_i[..., None, None] + jnp.einsum('bhcd,bhce->bhde', k_i, v_new)
        return state, o

    xs = tuple(jnp.moveaxis(a, 2, 0) for a in (q_dec, k_dec, u, w, attn, g_last))
    s0 = jnp.zeros((bsz, h, dk, dv), jnp.float32)
    _, o = lax.scan(step, s0, xs)
    return _from_chunks(jnp.moveaxis(o, 0, 2))


def chunkwise_retention(q, k, v):
    bsz, _, h, dk = q.shape
    dv = v.shape[-1]
    log_gamma = jnp.log(1.0 - jnp.exp2(-RET_DECAY_BASE - jnp.arange(h, dtype=jnp.float32)))
    q = _to_chunks(q)
    k = _to_chunks(k * dk ** -0.5)
    v = _to_chunks(v)
    idx = jnp.arange(CHUNK, dtype=jnp.float32)
    rel = idx[:, None] - idx[None, :]
    dmask = jnp.where(rel >= 0, jnp.exp(jnp.maximum(rel, 0.0) * log_gamma[:, None, None]), 0.0)
    scores = jnp.einsum('bhncd,bhnsd->bhncs', q, k) * dmask[None, :, None]
    o_intra = jnp.einsum('bhncs,bhnse->bhnce', scores, v)
    inner = jnp.exp((idx + 1.0) * log_gamma[:, None])
    tail = jnp.exp((CHUNK - 1.0 - idx) * log_gamma[:, None])
    chunk_decay = jnp.exp(CHUNK * log_gamma)
    q_in = q * inner[None, :, None, :, None]
    k_tail = k * tail[None, :, None, :, None]

    def step(state, xs):
        q_i, k_i, v_i = xs
        o = jnp.einsum('bhcd,bhde->bhce', q_i, state)
        state = state * chunk_decay[None, :, None, None] + jnp.einsum('bhcd,bhce->bhde', k_i, v_i)
        return state, o

    xs = tuple(jnp.moveaxis(a, 2, 0) for a in (q_in, k_tail, v))
    r0 = jnp.zeros((bsz, h, dk, dv), jnp.float32)
    _, o_inter = lax.scan(step, r0, xs)
    return _from_chunks(o_intra + jnp.moveaxis(o_inter, 0, 2))


def hybrid_mixer(h, positions, w_in, conv_qkv_w, a_log, dt_bias, gdn_norm_w,
                 conv_dw_w, conv_dw_b, conv_ln_w, conv_ln_b,
                 w_branch_a, w_branch_b, w_branch_c, w_out):
    bsz, t, _ = h.shape
    dt = h.dtype
    proj = h @ w_in
    (a_q, a_k, a_v, a_z, a_beta, a_alpha,
     b_q, b_k, b_v, b_g, c_glu, gate_raw) = jnp.split(proj, _split_points(IN_SIZES), axis=-1)

    qkv = jax.nn.silu(causal_depthwise_conv(jnp.concatenate([a_q, a_k, a_v], axis=-1), conv_qkv_w))
    a_q, a_k, a_v = jnp.split(qkv.astype(jnp.float32), 3, axis=-1)
    heads = lambda z, n, d: z.reshape(bsz, t, n, d)
    gq = l2_norm(heads(a_q, GDN_HEADS, GDN_HEAD_DIM))
    gk = l2_norm(heads(a_k, GDN_HEADS, GDN_HEAD_DIM))
    gv = heads(a_v, GDN_HEADS, GDN_HEAD_DIM)
    beta = jax.nn.sigmoid(a_beta.astype(jnp.float32))
    g = -jnp.exp(a_log.astype(jnp.float32)) * jax.nn.softplus(a_alpha.astype(jnp.float32) + dt_bias.astype(jnp.float32))
    o_a = gated_delta_rule(gq, gk, gv, g, beta)
    o_a = o_a * lax.rsqrt(jnp.mean(o_a * o_a, axis=-1, keepdims=True) + NORM_EPS) * gdn_norm_w.astype(jnp.float32)
    o_a = o_a * jax.nn.silu(heads(a_z.astype(jnp.float32), GDN_HEADS, GDN_HEAD_DIM))
    y_a = o_a.reshape(bsz, t, GDN_DIM).astype(dt) @ w_branch_a

    rq = rotary(heads(b_q.astype(jnp.float32), RET_HEADS, RET_QK_DIM), positions)
    rk = rotary(heads(b_k.astype(jnp.float32), RET_HEADS, RET_QK_DIM), positions)
    rv = heads(b_v.astype(jnp.float32), RET_HEADS, RET_V_DIM)
    o_b = layer_norm_f32(chunkwise_retention(rq, rk, rv))
    o_b = o_b.reshape(bsz, t, RET_V) * jax.nn.silu(b_g.astype(jnp.float32))
    y_b = o_b.astype(dt) @ w_branch_b

    c_a, c_b = jnp.split(c_glu, 2, axis=-1)
    u = c_a * jax.nn.sigmoid(c_b)
    u = causal_depthwise_conv(u, conv_dw_w) + conv_dw_b.astype(dt)
    u = layer_norm_f32(u.astype(jnp.float32)) * conv_ln_w.astype(jnp.float32) + conv_ln_b.astype(jnp.float32)
    y_c = jax.nn.silu(u).astype(dt) @ w_branch_c

    gates = jax.nn.sigmoid(gate_raw.reshape(bsz, t, N_BRANCH, D_MODEL))
    merged = gates[:, :, 0] * y_a + gates[:, :, 1] * y_b + gates[:, :, 2] * y_c
    return merged @ w_out


def setup_inputs(seed: int = 0) -> dict:
    key = jax.random.key(seed)
    ks = jax.random.split(key, 24)
    nrm = lambda k, shape, std: jax.random.normal(k, shape, jnp.float32) * std
    x = nrm(ks[0], (BATCH, SEQ, D_MODEL), 1.0)
    c = nrm(ks[1], (BATCH, D_MODEL), 1.0)
    start = jax.random.randint(ks[2], (BATCH, 1), 0, MAX_STREAM_OFFSET, dtype=jnp.int32)
    positions = (start + jnp.arange(SEQ, dtype=jnp.int32)[None, :]).astype(jnp.int32)
    w_ada = nrm(ks[3], (DEPTH, D_MODEL, 6 * D_MODEL), 0.5 * D_MODEL ** -0.5)
    b_ada = nrm(ks[4], (DEPTH, 6 * D_MODEL), 0.01)
    norm_mix_w = 1.0 + nrm(ks[5], (DEPTH, D_MODEL), 0.02)
    norm_mlp_w = 1.0 + nrm(ks[6], (DEPTH, D_MODEL), 0.02)
    w_in = nrm(ks[7], (DEPTH, D_MODEL, IN_WIDTH), D_MODEL ** -0.5)
    conv_qkv_w = nrm(ks[8], (DEPTH, SHORT_CONV, 3 * GDN_DIM), SHORT_CONV ** -0.5)
    gdn_a_log = jnp.log(jax.random.uniform(ks[9], (DEPTH, GDN_HEADS), jnp.float32, 1.0, 16.0))
    dt0 = jnp.exp(jax.random.uniform(ks[10], (DEPTH, GDN_HEADS), jnp.float32, math.log(1e-3), math.log(1e-1)))
    gdn_dt_bias = dt0 + jnp.log(-jnp.expm1(-dt0))
    gdn_norm_w = 1.0 + nrm(ks[11], (DEPTH, GDN_HEAD_DIM), 0.02)
    conv_dw_w = nrm(ks[12], (DEPTH, CONV_WIDTH, CONV_DIM), CONV_WIDTH ** -0.5)
    conv_dw_b = nrm(ks[13], (DEPTH, CONV_DIM), 0.01)
    conv_ln_w = 1.0 + nrm(ks[14], (DEPTH, CONV_DIM), 0.02)
    conv_ln_b = nrm(ks[15], (DEPTH, CONV_DIM), 0.01)
    w_branch_a = nrm(ks[16], (DEPTH, GDN_DIM, D_MODEL), GDN_DIM ** -0.5)
    w_branch_b = nrm(ks[17], (DEPTH, RET_V, D_MODEL), RET_V ** -0.5)
    w_branch_c = nrm(ks[18], (DEPTH, CONV_DIM, D_MODEL), CONV_DIM ** -0.5)
    w_out = nrm(ks[19], (DEPTH, D_MODEL, D_MODEL), D_MODEL ** -0.5)
    w_mlp_in = nrm(ks[20], (DEPTH, D_MODEL, D_FF), D_MODEL ** -0.5)
    w_mlp_out = nrm(ks[21], (DEPTH, D_FF, D_MODEL), D_FF ** -0.5)
    final_norm_w = 1.0 + nrm(ks[22], (D_MODEL,), 0.02)
    return {"x": x, "c": c, "positions": positions, "w_ada": w_ada, "b_ada": b_ada,
            "norm_mix_w": norm_mix_w, "norm_mlp_w": norm_mlp_w, "w_in": w_in,
            "conv_qkv_w": conv_qkv_w, "gdn_a_log": gdn_a_log, "gdn_dt_bias": gdn_dt_bias,
            "gdn_norm_w": gdn_norm_w, "conv_dw_w": conv_dw_w, "conv_dw_b": conv_dw_b,
            "conv_ln_w": conv_ln_w, "conv_ln_b": conv_ln_b, "w_branch_a": w_branch_a,
            "w_branch_b": w_branch_b, "w_branch_c": w_branch_c, "w_out": w_out,
            "w_mlp_in": w_mlp_in, "w_mlp_out": w_mlp_out, "final_norm_w": final_norm_w}


def reference(x, c, positions, w_ada, b_ada, norm_mix_w, norm_mlp_w, w_in, conv_qkv_w,
              gdn_a_log, gdn_dt_bias, gdn_norm_w, conv_dw_w, conv_dw_b, conv_ln_w, conv_ln_b,
              w_branch_a, w_branch_b, w_branch_c, w_out, w_mlp_in, w_mlp_out, final_norm_w):
    c_act = jax.nn.silu(c)
    for l in range(DEPTH):
        mod = c_act @ w_ada[l] + b_ada[l]
        shift1, scale1, gate1, shift2, scale2, gate2 = [m[:, None, :] for m in jnp.split(mod, 6, axis=-1)]
        h = rms_norm(x, norm_mix_w[l]) * (1.0 + scale1) + shift1
        x = x + gate1 * hybrid_mixer(h, positions, w_in[l], conv_qkv_w[l], gdn_a_log[l], gdn_dt_bias[l],
                                     gdn_norm_w[l], conv_dw_w[l], conv_dw_b[l], conv_ln_w[l], conv_ln_b[l],
                                     w_branch_a[l], w_branch_b[l], w_branch_c[l], w_out[l])
        h = rms_norm(x, norm_mlp_w[l]) * (1.0 + scale2) + shift2
        x = x + gate2 * (jnp.square(jax.nn.relu(h @ w_mlp_in[l])) @ w_mlp_out[l])
    return rms_norm(x, final_norm_w)
```

```python
import numpy as np
from contextlib import ExitStack
import concourse.bass as bass
import concourse.mybir as mybir
from concourse.bass_utils import run_bass_kernel_spmd

F32 = mybir.dt.float32
BF16 = mybir.dt.bfloat16
I32 = mybir.dt.int32
AF = mybir.ActivationFunctionType
ALU = mybir.AluOpType
AX = mybir.AxisListType


class Prog:
    ENGS = ("pe", "dve", "act", "pool", "sp")
    NDMA = 8

    def __init__(self, nc, es, same_engine_sync=True):
        self.nc = nc
        self.es = es
        self.same = same_engine_sync
        self.ops = {e: [] for e in self.ENGS}
        self.sems = {}
        self.cnt = {}
        for e in ("pe", "dve", "act", "pool"):
            self.sems[e] = es.enter_context(nc.semaphore("c_" + e))
            self.cnt[e] = 0
        for q in ("sp", "pool", "act"):
            for i in range(self.NDMA):
                k = ("dma", q, i)
                self.sems[k] = es.enter_context(nc.semaphore(f"d_{q}{i}"))
                self.cnt[k] = 0
        self.dma_i = {"sp": 0, "pool": 0, "act": 0}
        self.waited = {}
        self.lastw = {}
        self.readers = {}
        self.n_ops = 0

    def sb(self, name, shape, dt):
        return self.es.enter_context(self.nc.sbuf_tensor("s_" + name, list(shape), dt))

    def ps(self, name, shape, dt=F32):
        return self.es.enter_context(self.nc.psum_tensor("p_" + name, list(shape), dt))

    def _deps(self, reads, writes):
        deps = {}

        def add(tok):
            if tok is None:
                return
            k, v = tok
            if deps.get(k, 0) < v:
                deps[k] = v
        for b in reads:
            add(self.lastw.get(b))
        for b in writes:
            add(self.lastw.get(b))
            for k, v in self.readers.get(b, {}).items():
                add((k, v))
        return deps

    def _commit(self, tok, reads, writes):
        k, v = tok
        for b in reads:
            r = self.readers.setdefault(b, {})
            if r.get(k, 0) < v:
                r[k] = v
        for b in writes:
            self.lastw[b] = tok
            self.readers[b] = {}

    def _waits(self, eng, deps):
        waits = []
        for k, v in deps.items():
            if k == "pe" and eng == "pe":
                continue
            if (not self.same) and k == eng:
                continue
            if self.waited.get((eng, k), 0) >= v:
                continue
            self.waited[(eng, k)] = v
            waits.append((k, v))
        return waits

    @staticmethod
    def _excl(reads, writes):
        ex = [b for b in reads if isinstance(b, tuple) and b[0] == "bk"]
        if ex:
            writes = list(writes) + [b for b in ex if b not in writes]
        return reads, writes

    def op(self, eng, fn, reads=(), writes=()):
        reads, writes = self._excl(reads, writes)
        deps = self._deps(reads, writes)
        waits = self._waits(eng, deps)
        self.cnt[eng] += 1
        tok = (eng, self.cnt[eng])
        self.ops[eng].append((waits, fn, (eng, 1)))
        self._commit(tok, reads, writes)
        self.n_ops += 1
        return tok

    def dma(self, q, fn, reads=(), writes=()):
        i = self.dma_i[q]
        self.dma_i[q] += 1
        k = ("dma", q, i % self.NDMA)
        deps = self._deps(reads, writes)
        if self.cnt[k] > 0:
            if deps.get(k, 0) < self.cnt[k]:
                deps[k] = self.cnt[k]
        waits = self._waits(q, deps)
        self.cnt[k] += 16
        tok = (k, self.cnt[k])
        self.ops[q].append((waits, fn, (k, 16)))
        self._commit(tok, reads, writes)
        self.n_ops += 1
        return tok

    def emit(self):
        final = []
        for k, v in self.cnt.items():
            if v > 0 and self.waited.get(("sp", k), 0) < v:
                final.append((k, v))
        nc = self.nc
        sems = self.sems
        ops = self.ops

        def run(e, name, fin=False):
            for waits, fn, (sk, inc) in ops[name]:
                for k, v in waits:
                    e.wait_ge(sems[k], v)
                ins = fn(e)
                ins.then_inc(sems[sk], inc)
            if fin:
                for k, v in final:
                    e.wait_ge(sems[k], v)

        with nc.Block() as block:
            @block.tensor
            def _(e):
                run(e, "pe")

            @block.vector
            def _(e):
                run(e, "dve")

            @block.scalar
            def _(e):
                run(e, "act")

            @block.gpsimd
            def _(e):
                run(e, "pool")

            @block.sync
            def _(e):
                run(e, "sp", fin=True)


D = 2048
T = 1024
KC = D // 128
EPS = 1e-6
INW = 15376
V_SH1, V_SC1, V_G1, V_SH2, V_SC2, V_G2, V_NMIX, V_NMLP, V_NFIN = [i * 16 for i in range(9)]
NV = 9 * 16


def new_prog():
    nc = bass.Bass("TRN2", target_bir_lowering=False)
    es = ExitStack()
    P = Prog(nc, es)
    P.wctr = 0
    P.psctr = 0
    return nc, es, P


def gemm(P, w_dram, K, NC, CB, rhs, evac, wbufs, psb, ntb=T // 512, k0=0):
    kcn = K // 128
    nblocks = (NC + CB - 1) // CB
    for cb in range(nblocks):
        c0 = cb * CB
        cw = min(CB, NC - c0)
        b = P.wctr % 2
        P.wctr += 1
        wt = wbufs[b][:, 0:kcn * CB].rearrange("p (kc c) -> p kc c", c=CB)
        src = w_dram[k0:k0 + K, c0:c0 + cw].rearrange("(kc p) c -> p kc c", p=128)
        P.dma("pool", lambda e, wt=wt, src=src, cw=cw: e.dma_start(out=wt[:, :, 0:cw], in_=src),
              writes=[("wt", b)])
        for ci in range((cw + 127) // 128):
            m = min(128, cw - ci * 128)
            ct = (c0 + ci * 128) // 128
            slot = P.psctr % 2
            P.psctr += 1
            for kc in range(kcn):
                for tb in range(ntb):
                    pst = psb[slot * ntb + tb]
                    rap, rkeys = rhs(kc, tb)
                    P.op("pe", lambda e, pst=pst, wt=wt, kc=kc, ci=ci, m=m, rap=rap, st=(kc == 0), sp=(kc == kcn - 1):
                         e.matmul(pst[0:m, :], wt[:, kc, ci * 128:ci * 128 + m], rap, start=st, stop=sp),
                         reads=[("wt", b)] + rkeys, writes=[("ps", slot * ntb + tb)])
            for tb in range(ntb):
                evac(ct, m, tb, psb[slot * ntb + tb], ("ps", slot * ntb + tb))


def rms_to_bf16(P, x32, vec, c_scale, c_shift, c_nw, hb, ones32, sq, rstd, pss, avec, final_out=None):
    for kc in range(KC):
        s = sq[kc % 2]
        P.op("act", lambda e, s=s, kc=kc: e.activation(out=s[:], in_=x32[:, kc, :], func=AF.Square),
             reads=[("x32", kc)], writes=[("sq", kc % 2)])
        for tb in range(T // 512):
            P.op("pe", lambda e, s=s, tb=tb, kc=kc: e.matmul(pss[tb][:], ones32[:], s[:, tb * 512:(tb + 1) * 512],
                                                            start=(kc == 0), stop=(kc == KC - 1)),
                 reads=[("sq", kc % 2), "ones32"], writes=[("pss", tb)])
    for tb in range(T // 512):
        P.op("act", lambda e, tb=tb: e.activation(out=rstd[:, tb * 512:(tb + 1) * 512], in_=pss[tb][:], func=AF.Sqrt,
                                                  bias=epsb[0][:], scale=1.0 / D),
             reads=[("pss", tb), "epsb"], writes=[("rstd", tb)])
        P.op("dve", lambda e, tb=tb: e.reciprocal(out=rstd[:, tb * 512:(tb + 1) * 512], in_=rstd[:, tb * 512:(tb + 1) * 512]),
             reads=[("rstd", tb)], writes=[("rstd", tb)])
    if c_scale is not None:
        P.op("dve", lambda e: e.scalar_tensor_tensor(out=avec[:], in0=vec[:, c_scale:c_scale + 16], scalar=1.0,
                                                     in1=vec[:, c_nw:c_nw + 16], op0=ALU.add, op1=ALU.mult),
             reads=["vec"], writes=["avec"])
    else:
        P.op("dve", lambda e: e.tensor_copy(out=avec[:], in_=vec[:, c_nw:c_nw + 16]), reads=["vec"], writes=["avec"])
    for kc in range(KC):
        s = sq[kc % 2]
        if final_out is None:
            P.op("dve", lambda e, s=s, kc=kc: e.scalar_tensor_tensor(out=s[:], in0=x32[:, kc, :], scalar=avec[:, kc:kc + 1],
                                                                     in1=rstd[:], op0=ALU.mult, op1=ALU.mult),
                 reads=[("x32", kc), "avec", ("rstd", 0), ("rstd", 1)], writes=[("sq", kc % 2)])
            P.op("act", lambda e, s=s, kc=kc: e.activation(out=hb[:, kc, :], in_=s[:], func=AF.Identity,
                                                           bias=vec[:, c_shift + kc:c_shift + kc + 1], scale=1.0),
                 reads=[("sq", kc % 2), "vec"], writes=[("hb", kc)])
        else:
            final_out(kc, s)


epsb = [None]


def consts(P, need_ident=False):
    ones32 = P.sb("ones32", [128, 128], F32)
    P.op("dve", lambda e: e.memset(ones32[:], 1.0), writes=["ones32"])
    eb = P.sb("epsb", [128, 1], F32)
    P.op("dve", lambda e: e.memset(eb[:], EPS), writes=["epsb"])
    epsb[0] = eb
    return ones32


def load_x(P, xT, x32):
    for kc in range(KC):
        P.dma("sp", lambda e, kc=kc: e.dma_start(out=x32[:, kc, :], in_=xT[kc * 128:(kc + 1) * 128, :]),
              writes=[("x32", kc)])


def build_pre():
    nc, es, P = new_prog()
    xT = nc.dram_tensor("xT", [D, T], F32, kind="ExternalInput").ap()
    vecd = nc.dram_tensor("vec", [128, NV], F32, kind="ExternalInput").ap()
    w_in = nc.dram_tensor("w_in", [D, INW], F32, kind="ExternalInput").ap()
    projT = nc.dram_tensor("projT", [INW, T], F32, kind="ExternalOutput").ap()
    with es:
        x32 = P.sb("x32", [128, KC, T], F32)
        hb = P.sb("hb", [128, KC, T], BF16)
        vec = P.sb("vec", [128, NV], F32)
        avec = P.sb("avec", [128, 16], F32)
        sq = [P.sb(f"sq{i}", [128, T], F32) for i in range(2)]
        rstd = P.sb("rstd", [128, T], F32)
        wbufs = [P.sb(f"wb{i}", [128, 8192], BF16) for i in range(2)]
        ob = [P.sb(f"ob{i}", [128, T], F32) for i in range(2)]
        pss = [P.ps(f"pss{i}", [128, 512]) for i in range(2)]
        psb = [P.ps(f"psb{i}", [128, 512]) for i in range(4)]
        ones32 = consts(P)
        P.dma("sp", lambda e: e.dma_start(out=vec[:], in_=vecd), writes=["vec"])
        load_x(P, xT, x32)
        rms_to_bf16(P, x32, vec, V_SC1, V_SH1, V_NMIX, hb, ones32, sq, rstd, pss, avec)

        def rhs(kc, tb):
            return hb[:, kc, tb * 512:(tb + 1) * 512], [("hb", kc)]

        def evac(ct, m, tb, pst, pkey):
            o = ob[ct % 2]
            eng = "dve" if tb == 0 else "act"
            if eng == "dve":
                P.op("dve", lambda e: e.tensor_copy(out=o[0:m, tb * 512:(tb + 1) * 512], in_=pst[0:m, :]),
                     reads=[pkey], writes=[("ob", ct % 2, tb)])
            else:
                P.op("act", lambda e: e.activation(out=o[0:m, tb * 512:(tb + 1) * 512], in_=pst[0:m, :], func=AF.Identity),
                     reads=[pkey], writes=[("ob", ct % 2, tb)])
            if tb == T // 512 - 1:
                P.dma("sp", lambda e: e.dma_start(out=projT[ct * 128:ct * 128 + m, :], in_=o[0:m, :]),
                      reads=[("ob", ct % 2, 0), ("ob", ct % 2, 1)])
        gemm(P, w_in, D, INW, 512, rhs, evac, wbufs, psb)
        P.emit()
    return nc


def build_merge():
    nc, es, P = new_prog()
    xT = nc.dram_tensor("xT", [D, T], F32, kind="ExternalInput").ap()
    vecd = nc.dram_tensor("vec", [128, NV], F32, kind="ExternalInput").ap()
    oT = [nc.dram_tensor(n, [1024, T], F32, kind="ExternalInput").ap() for n in ("oaT", "obT", "ocT")]
    gT = nc.dram_tensor("gT", [3 * D, T], F32, kind="ExternalInput").ap()
    wbr = [nc.dram_tensor(n, [1024, D], F32, kind="ExternalInput").ap() for n in ("wba", "wbb", "wbc")]
    w_out = nc.dram_tensor("w_out", [D, D], F32, kind="ExternalInput").ap()
    x1T = nc.dram_tensor("x1T", [D, T], F32, kind="ExternalOutput").ap()
    with es:
        ob3 = [P.sb(f"o3_{i}", [128, 8, T], BF16) for i in range(3)]
        mb = P.sb("mb", [128, KC, T], BF16)
        vec = P.sb("vec", [128, NV], F32)
        wbufs = [P.sb(f"wb{i}", [128, 8192], BF16) for i in range(2)]
        wbt = [[P.sb(f"wbt{j}_{i}", [128, 8, 128], BF16) for i in range(3)] for j in range(2)]
        gt = [[P.sb(f"gt{j}_{i}", [128, T], F32) for i in range(3)] for j in range(2)]
        acc = [P.sb(f"acc{i}", [128, 512], F32) for i in range(2)]
        tmp = [P.sb(f"tmp{i}", [128, 512], F32) for i in range(2)]
        xt = [P.sb(f"xt{i}", [128, T], F32) for i in range(2)]
        pb = [P.ps(f"pb{i}", [128, 512]) for i in range(8)]
        P.dma("sp", lambda e: e.dma_start(out=vec[:], in_=vecd), writes=["vec"])
        for i in range(3):
            for kc in range(8):
                P.dma("pool", lambda e, i=i, kc=kc: e.dma_start(out=ob3[i][:, kc, :], in_=oT[i][kc * 128:(kc + 1) * 128, :]),
                      writes=[("o3", i, kc)])
        for ct in range(16):
            j = ct % 2
            for i in range(3):
                P.dma("pool", lambda e, i=i, j=j, ct=ct: e.dma_start(
                    out=wbt[j][i][:], in_=wbr[i][:, ct * 128:(ct + 1) * 128].rearrange("(kc p) c -> p kc c", p=128)),
                    writes=[("wbt", j, i)])
                P.dma("sp", lambda e, i=i, j=j, ct=ct: e.dma_start(out=gt[j][i][:], in_=gT[i * D + ct * 128:i * D + (ct + 1) * 128, :]),
                      writes=[("gt", j, i)])
                P.op("act", lambda e, i=i, j=j: e.activation(out=gt[j][i][:], in_=gt[j][i][:], func=AF.Sigmoid),
                     reads=[("gt", j, i)], writes=[("gt", j, i)])
            for i in range(3):
                for kc in range(8):
                    for tb in range(2):
                        P.op("pe", lambda e, i=i, j=j, kc=kc, tb=tb: e.matmul(
                            pb[i * 2 + tb][:], wbt[j][i][:, kc, :], ob3[i][:, kc, tb * 512:(tb + 1) * 512],
                            start=(kc == 0), stop=(kc == 7)),
                            reads=[("wbt", j, i), ("o3", i, kc)], writes=[("ps", i * 2 + tb)])
            for tb in range(2):
                sl = slice(tb * 512, (tb + 1) * 512)
                P.op("dve", lambda e, tb=tb, sl=sl, j=j: e.tensor_tensor(out=acc[tb][:], in0=pb[tb][:], in1=gt[j][0][:, sl], op=ALU.mult),
                     reads=[("ps", tb), ("gt", j, 0)], writes=[("acc", tb)])
                P.op("dve", lambda e, tb=tb, sl=sl, j=j: e.tensor_tensor(out=tmp[tb][:], in0=pb[2 + tb][:], in1=gt[j][1][:, sl], op=ALU.mult),
                     reads=[("ps", 2 + tb), ("gt", j, 1)], writes=[("tmp", tb)])
                P.op("pool", lambda e, tb=tb: e.tensor_tensor(out=acc[tb][:], in0=acc[tb][:], in1=tmp[tb][:], op=ALU.add),
                     reads=[("acc", tb), ("tmp", tb)], writes=[("acc", tb)])
                P.op("dve", lambda e, tb=tb, sl=sl, j=j: e.tensor_tensor(out=tmp[tb][:], in0=pb[4 + tb][:], in1=gt[j][2][:, sl], op=ALU.mult),
                     reads=[("ps", 4 + tb), ("gt", j, 2)], writes=[("tmp", tb)])
                P.op("pool", lambda e, tb=tb, sl=sl, ct=ct: e.tensor_tensor(out=mb[:, ct, sl], in0=acc[tb][:], in1=tmp[tb][:], op=ALU.add),
                     reads=[("acc", tb), ("tmp", tb)], writes=[("mb", ct)])

        def rhs2(kc, tb):
            return mb[:, kc, tb * 512:(tb + 1) * 512], [("mb", kc)]

        def evac2(ct, m, tb, pst, pkey):
            j = ct % 2
            sl = slice(tb * 512, (tb + 1) * 512)
            if tb == 0:
                P.dma("sp", lambda e: e.dma_start(out=xt[j][:], in_=xT[ct * 128:(ct + 1) * 128, :]), writes=[("xt", j)])
            P.op("dve", lambda e: e.scalar_tensor_tensor(out=xt[j][:, sl], in0=pst[:], scalar=vec[:, V_G1 + ct:V_G1 + ct + 1],
                                                         in1=xt[j][:, sl], op0=ALU.mult, op1=ALU.add),
                 reads=[pkey, ("xt", j), "vec"], writes=[("xt", j)])
            if tb == 1:
                P.dma("sp", lambda e: e.dma_start(out=x1T[ct * 128:(ct + 1) * 128, :], in_=xt[j][:]), reads=[("xt", j)])
        gemm(P, w_out, D, D, 512, rhs2, evac2, wbufs, pb[0:4])
        P.emit()
    return nc


def build_mlp(final):
    nc, es, P = new_prog()
    DFF = 4 * D
    xT = nc.dram_tensor("xT", [D, T], F32, kind="ExternalInput").ap()
    vecd = nc.dram_tensor("vec", [128, NV], F32, kind="ExternalInput").ap()
    w1 = nc.dram_tensor("w1", [D, DFF], F32, kind="ExternalInput").ap()
    w2 = nc.dram_tensor("w2", [DFF, D], F32, kind="ExternalInput").ap()
    x2T = nc.dram_tensor("x2T", [D, T], F32, kind="ExternalOutput").ap()
    with es:
        x32 = P.sb("x32", [128, KC, T], F32)
        hb = P.sb("hb", [128, KC, T], BF16)
        hid = P.sb("hid", [128, 16, T], BF16)
        vec = P.sb("vec", [128, NV], F32)
        avec = P.sb("avec", [128, 16], F32)
        sq = [P.sb(f"sq{i}", [128, T], F32) for i in range(2)]
        rstd = P.sb("rstd", [128, T], F32)
        wbufs = [P.sb(f"wb{i}", [128, 8192], BF16) for i in range(2)]
        rl = [P.sb(f"rl{i}", [128, 512], F32) for i in range(2)]
        pss = [P.ps(f"pss{i}", [128, 512]) for i in range(2)]
        psb = [P.ps(f"psb{i}", [128, 512]) for i in range(4)]
        ones32 = consts(P)
        P.dma("sp", lambda e: e.dma_start(out=vec[:], in_=vecd), writes=["vec"])
        load_x(P, xT, x32)
        rms_to_bf16(P, x32, vec, V_SC2, V_SH2, V_NMLP, hb, ones32, sq, rstd, pss, avec)
        rctr = [0]
        for q in range(4):
            def rhs(kc, tb):
                return hb[:, kc, tb * 512:(tb + 1) * 512], [("hb", kc)]

            def evac(ct, m, tb, pst, pkey, q=q):
                r = rctr[0] % 2
                rctr[0] += 1
                cl = ct
                P.op("act", lambda e: e.activation(out=rl[r][:], in_=pst[:], func=AF.Relu), reads=[pkey], writes=[("rl", r)])
                P.op("dve", lambda e: e.tensor_tensor(out=hid[:, cl, tb * 512:(tb + 1) * 512], in0=rl[r][:], in1=rl[r][:], op=ALU.mult),
                     reads=[("rl", r)], writes=[("hid", cl)])
            gemm(P, w1[:, q * 2048:(q + 1) * 2048], D, 2048, 512, rhs, evac, wbufs, psb)

            def rhs2(kc, tb):
                return hid[:, kc, tb * 512:(tb + 1) * 512], [("hid", kc)]

            def evac2(ct, m, tb, pst, pkey):
                sl = slice(tb * 512, (tb + 1) * 512)
                P.op("dve", lambda e: e.scalar_tensor_tensor(out=x32[:, ct, sl], in0=pst[:], scalar=vec[:, V_G2 + ct:V_G2 + ct + 1],
                                                             in1=x32[:, ct, sl], op0=ALU.mult, op1=ALU.add),
                     reads=[pkey, ("x32", ct), "vec"], writes=[("x32", ct)])
            gemm(P, w2, 2048, D, 512, rhs2, evac2, wbufs, psb, k0=q * 2048)
        if not final:
            for kc in range(KC):
                P.dma("sp", lambda e, kc=kc: e.dma_start(out=x2T[kc * 128:(kc + 1) * 128, :], in_=x32[:, kc, :]),
                      reads=[("x32", kc)])
        else:
            def final_out(kc, s):
                P.op("dve", lambda e: e.scalar_tensor_tensor(out=s[:], in0=x32[:, kc, :], scalar=avec[:, kc:kc + 1],
                                                             in1=rstd[:], op0=ALU.mult, op1=ALU.mult),
                     reads=[("x32", kc), "avec", ("rstd", 0), ("rstd", 1)], writes=[("sq", kc % 2)])
                P.dma("sp", lambda e: e.dma_start(out=x2T[kc * 128:(kc + 1) * 128, :], in_=s[:]), reads=[("sq", kc % 2)])
            rms_to_bf16(P, x32, vec, None, None, V_NFIN, None, ones32, sq, rstd, pss, avec, final_out=final_out)
        P.emit()
    return nc


def build_convc():
    nc, es, P = new_prog()
    TH = T + 32
    cgT = nc.dram_tensor("cgT", [2048, TH], F32, kind="ExternalInput").ap()
    cwd = nc.dram_tensor("cw", [128, 8 * 31], F32, kind="ExternalInput").ap()
    cvd = nc.dram_tensor("cvec", [128, 24], F32, kind="ExternalInput").ap()
    ocT = nc.dram_tensor("ocT", [1024, T], F32, kind="ExternalOutput").ap()
    with es:
        at = [P.sb(f"at{i}", [128, TH], F32) for i in range(2)]
        bt = [P.sb(f"bt{i}", [128, TH], F32) for i in range(2)]
        cv = P.sb("cv", [128, 8, T], F32)
        cw = P.sb("cw", [128, 8 * 31], F32)
        cvec = P.sb("cvec", [128, 24], F32)
        sq = [P.sb(f"sq{i}", [128, T], F32) for i in range(2)]
        mean = P.sb("mean", [128, T], F32)
        rstd = P.sb("rstd", [128, T], F32)
        eps5 = P.sb("eps5", [128, 1], F32)
        ps1 = [P.ps(f"ps1_{i}", [128, 512]) for i in range(2)]
        ps2 = [P.ps(f"ps2_{i}", [128, 512]) for i in range(2)]
        ones32 = consts(P)
        P.op("dve", lambda e: e.memset(eps5[:], 1e-5), writes=["eps5"])
        P.dma("sp", lambda e: e.dma_start(out=cw[:], in_=cwd), writes=["cw"])
        P.dma("sp", lambda e: e.dma_start(out=cvec[:], in_=cvd), writes=["cvec"])
        for ch in range(8):
            j = ch % 2
            P.dma("sp", lambda e, ch=ch, j=j: e.dma_start(out=at[j][:], in_=cgT[ch * 128:(ch + 1) * 128, :]), writes=[("at", j)])
            P.dma("sp", lambda e, ch=ch, j=j: e.dma_start(out=bt[j][:], in_=cgT[1024 + ch * 128:1024 + (ch + 1) * 128, :]), writes=[("bt", j)])
            P.op("act", lambda e, j=j: e.activation(out=bt[j][:], in_=bt[j][:], func=AF.Sigmoid), reads=[("bt", j)], writes=[("bt", j)])
            P.op("pool", lambda e, j=j: e.tensor_tensor(out=at[j][:], in0=at[j][:], in1=bt[j][:], op=ALU.mult),
                 reads=[("at", j), ("bt", j)], writes=[("at", j)])
            P.op("dve", lambda e, ch=ch, j=j: e.tensor_scalar(out=cv[:, ch, :], in0=at[j][:, 2:2 + T], scalar1=cw[:, ch * 31:ch * 31 + 1],
                                                              scalar2=cvec[:, ch:ch + 1], op0=ALU.mult, op1=ALU.add),
                 reads=[("at", j), "cw", "cvec"], writes=[("cv", ch)])
            for k in range(1, 31):
                P.op("dve", lambda e, ch=ch, j=j, k=k: e.scalar_tensor_tensor(
                    out=cv[:, ch, :], in0=at[j][:, 2 + k:2 + k + T], scalar=cw[:, ch * 31 + k:ch * 31 + k + 1],
                    in1=cv[:, ch, :], op0=ALU.mult, op1=ALU.add),
                    reads=[("at", j), "cw", ("cv", ch)], writes=[("cv", ch)])
            s = sq[j]
            P.op("act", lambda e, s=s, ch=ch: e.activation(out=s[:], in_=cv[:, ch, :], func=AF.Square),
                 reads=[("cv", ch)], writes=[("sq", j)])
            for tb in range(2):
                P.op("pe", lambda e, tb=tb, ch=ch: e.matmul(ps1[tb][:], ones32[:], cv[:, ch, tb * 512:(tb + 1) * 512],
                                                            start=(ch == 0), stop=(ch == 7)),
                     reads=[("cv", ch), "ones32"], writes=[("ps1", tb)])
                P.op("pe", lambda e, tb=tb, ch=ch, s=s: e.matmul(ps2[tb][:], ones32[:], s[:, tb * 512:(tb + 1) * 512],
                                                                 start=(ch == 0), stop=(ch == 7)),
                     reads=[("sq", j), "ones32"], writes=[("ps2", tb)])
        for tb in range(2):
            sl = slice(tb * 512, (tb + 1) * 512)
            P.op("act", lambda e, tb=tb, sl=sl: e.activation(out=mean[:, sl], in_=ps1[tb][:], func=AF.Identity, scale=1.0 / 1024),
                 reads=[("ps1", tb)], writes=[("mean", tb)])
            P.op("dve", lambda e, tb=tb, sl=sl: e.tensor_tensor(out=rstd[:, sl], in0=mean[:, sl], in1=mean[:, sl], op=ALU.mult),
                 reads=[("mean", tb)], writes=[("rstd", tb)])
            P.op("dve", lambda e, tb=tb, sl=sl: e.scalar_tensor_tensor(out=rstd[:, sl], in0=ps2[tb][:], scalar=1.0 / 1024, in1=rstd[:, sl],
                                                                       op0=ALU.mult, op1=ALU.subtract),
                 reads=[("ps2", tb), ("rstd", tb)], writes=[("rstd", tb)])
            P.op("act", lambda e, tb=tb, sl=sl: e.activation(out=rstd[:, sl], in_=rstd[:, sl], func=AF.Sqrt, bias=eps5[:], scale=1.0),
                 reads=[("rstd", tb), "eps5"], writes=[("rstd", tb)])
            P.op("dve", lambda e, tb=tb, sl=sl: e.reciprocal(out=rstd[:, sl], in_=rstd[:, sl]),
                 reads=[("rstd", tb)], writes=[("rstd", tb)])
        for ch in range(8):
            s = sq[ch % 2]
            P.op("dve", lambda e, ch=ch, s=s: e.tensor_tensor(out=s[:], in0=cv[:, ch, :], in1=mean[:], op=ALU.subtract),
                 reads=[("cv", ch), ("mean", 0), ("mean", 1)], writes=[("sq", ch % 2)])
            P.op("pool", lambda e, ch=ch, s=s: e.tensor_tensor(out=s[:], in0=s[:], in1=rstd[:], op=ALU.mult),
                 reads=[("sq", ch % 2), ("rstd", 0), ("rstd", 1)], writes=[("sq", ch % 2)])
            P.op("act", lambda e, ch=ch, s=s: e.activation(out=s[:], in_=s[:], func=AF.Silu, bias=cvec[:, 16 + ch:17 + ch],
                                                           scale=cvec[:, 8 + ch:9 + ch]),
                 reads=[("sq", ch % 2), "cvec"], writes=[("sq", ch % 2)])
            P.dma("sp", lambda e, ch=ch, s=s: e.dma_start(out=ocT[ch * 128:(ch + 1) * 128, :], in_=s[:]), reads=[("sq", ch % 2)])
        P.emit()
    return nc


def build_retln():
    nc, es, P = new_prog()
    oretT = nc.dram_tensor("oretT", [1024, T], F32, kind="ExternalInput").ap()
    bgT = nc.dram_tensor("bgT", [1024, T], F32, kind="ExternalInput").ap()
    obT = nc.dram_tensor("obT", [1024, T], F32, kind="ExternalOutput").ap()
    with es:
        xt = [[P.sb(f"xt{j}_{c}", [128, T], F32) for c in range(2)] for j in range(2)]
        gt = [P.sb(f"gt{i}", [128, T], F32) for i in range(2)]
        sq = [P.sb(f"sq{i}", [128, T], F32) for i in range(2)]
        mean = P.sb("mean", [128, T], F32)
        rstd = P.sb("rstd", [128, T], F32)
        eps5 = P.sb("eps5", [128, 1], F32)
        ps1 = [P.ps(f"ps1_{i}", [128, 512]) for i in range(2)]
        ps2 = [P.ps(f"ps2_{i}", [128, 512]) for i in range(2)]
        ones32 = consts(P)
        P.op("dve", lambda e: e.memset(eps5[:], 1e-5), writes=["eps5"])
        gi = 0
        for h in range(4):
            j = h % 2
            for c in range(2):
                P.dma("sp", lambda e, h=h, c=c, j=j: e.dma_start(out=xt[j][c][:], in_=oretT[(2 * h + c) * 128:(2 * h + c + 1) * 128, :]),
                      writes=[("xt", j, c)])
                P.op("act", lambda e, j=j, c=c: e.activation(out=sq[c][:], in_=xt[j][c][:], func=AF.Square),
                     reads=[("xt", j, c)], writes=[("sq", c)])
                for tb in range(2):
                    sl = slice(tb * 512, (tb + 1) * 512)
                    P.op("pe", lambda e, j=j, c=c, tb=tb, sl=sl: e.matmul(ps1[tb][:], ones32[:], xt[j][c][:, sl], start=(c == 0), stop=(c == 1)),
                         reads=[("xt", j, c), "ones32"], writes=[("ps1", tb)])
                    P.op("pe", lambda e, c=c, tb=tb, sl=sl: e.matmul(ps2[tb][:], ones32[:], sq[c][:, sl], start=(c == 0), stop=(c == 1)),
                         reads=[("sq", c), "ones32"], writes=[("ps2", tb)])
            for tb in range(2):
                sl = slice(tb * 512, (tb + 1) * 512)
                P.op("act", lambda e, tb=tb, sl=sl: e.activation(out=mean[:, sl], in_=ps1[tb][:], func=AF.Identity, scale=1.0 / 256),
                     reads=[("ps1", tb)], writes=[("mean", tb)])
                P.op("dve", lambda e, tb=tb, sl=sl: e.tensor_tensor(out=rstd[:, sl], in0=mean[:, sl], in1=mean[:, sl], op=ALU.mult),
                     reads=[("mean", tb)], writes=[("rstd", tb)])
                P.op("dve", lambda e, tb=tb, sl=sl: e.scalar_tensor_tensor(out=rstd[:, sl], in0=ps2[tb][:], scalar=1.0 / 256, in1=rstd[:, sl],
                                                                           op0=ALU.mult, op1=ALU.subtract),
                     reads=[("ps2", tb), ("rstd", tb)], writes=[("rstd", tb)])
                P.op("act", lambda e, tb=tb, sl=sl: e.activation(out=rstd[:, sl], in_=rstd[:, sl], func=AF.Sqrt, bias=eps5[:], scale=1.0),
                     reads=[("rstd", tb), "eps5"], writes=[("rstd", tb)])
                P.op("dve", lambda e, tb=tb, sl=sl: e.reciprocal(out=rstd[:, sl], in_=rstd[:, sl]),
                     reads=[("rstd", tb)], writes=[("rstd", tb)])
            mk = [("mean", 0), ("mean", 1)]
            rk = [("rstd", 0), ("rstd", 1)]
            for c in range(2):
                g = gt[gi % 2]
                gk = ("gt", gi % 2)
                gi += 1
                P.dma("sp", lambda e, h=h, c=c, g=g: e.dma_start(out=g[:], in_=bgT[(2 * h + c) * 128:(2 * h + c + 1) * 128, :]), writes=[gk])
                P.op("act", lambda e, g=g: e.activation(out=g[:], in_=g[:], func=AF.Silu), reads=[gk], writes=[gk])
                x = xt[j][c]
                P.op("dve", lambda e, x=x: e.tensor_tensor(out=x[:], in0=x[:], in1=mean[:], op=ALU.subtract),
                     reads=[("xt", j, c)] + mk, writes=[("xt", j, c)])
                P.op("pool", lambda e, x=x: e.tensor_tensor(out=x[:], in0=x[:], in1=rstd[:], op=ALU.mult),
                     reads=[("xt", j, c)] + rk, writes=[("xt", j, c)])
                P.op("dve", lambda e, x=x, g=g: e.tensor_tensor(out=x[:], in0=x[:], in1=g[:], op=ALU.mult),
                     reads=[("xt", j, c), gk], writes=[("xt", j, c)])
                P.dma("sp", lambda e, h=h, c=c, x=x: e.dma_start(out=obT[(2 * h + c) * 128:(2 * h + c + 1) * 128, :], in_=x[:]),
                      reads=[("xt", j, c)])
        P.emit()
    return nc


def build_ada():
    nc, es, P = new_prog()
    NCOL = 1536
    cTd = nc.dram_tensor("cT", [128, 16], F32, kind="ExternalInput").ap()
    wad = nc.dram_tensor("wa", [2, D, NCOL], F32, kind="ExternalInput").ap()
    bad = nc.dram_tensor("ba", [2, NCOL], F32, kind="ExternalInput").ap()
    modd = nc.dram_tensor("mod", [2, NCOL], F32, kind="ExternalOutput").ap()
    with es:
        ca = P.sb("ca", [128, 16], F32)
        wt = [P.sb(f"wt{i}", [128, 16, 256], F32) for i in range(2)]
        bt = P.sb("bt", [1, 2 * NCOL], F32)
        ot = P.sb("ot", [1, 2 * NCOL], F32)
        ps = [P.ps(f"ps{i}", [128, 512]) for i in range(2)]
        P.dma("sp", lambda e: e.dma_start(out=ca[:], in_=cTd), writes=["ca"])
        P.op("act", lambda e: e.activation(out=ca[:], in_=ca[:], func=AF.Silu), reads=["ca"], writes=["ca"])
        for l in range(2):
            P.dma("sp", lambda e, l=l: e.dma_start(out=bt[0:1, l * NCOL:(l + 1) * NCOL], in_=bad[l:l + 1, :]), writes=[("bt", l)])
        i = 0
        for l in range(2):
            for cb in range(NCOL // 256):
                b = i % 2
                i += 1
                P.dma("sp", lambda e, l=l, cb=cb, b=b: e.dma_start(
                    out=wt[b][:], in_=wad[l, :, cb * 256:(cb + 1) * 256].rearrange("(kc p) c -> p kc c", p=128)), writes=[("wt", b)])
                for kc in range(16):
                    P.op("pe", lambda e, b=b, kc=kc: e.matmul(ps[b][0:1, 0:256], ca[:, kc:kc + 1], wt[b][:, kc, :], start=(kc == 0), stop=(kc == 15)),
                         reads=["ca", ("wt", b)], writes=[("ps", b)])
                o0 = l * NCOL + cb * 256
                P.op("dve", lambda e, b=b, o0=o0: e.tensor_tensor(out=ot[0:1, o0:o0 + 256], in0=ps[b][0:1, 0:256], in1=bt[0:1, o0:o0 + 256], op=ALU.add),
                     reads=[("ps", b), ("bt", l)], writes=[("ot", l)])
        for l in range(2):
            P.dma("sp", lambda e, l=l: e.dma_start(out=modd[l:l + 1, :], in_=ot[0:1, l * NCOL:(l + 1) * NCOL]), reads=[("ot", l)])
        P.emit()
    return nc


TA = 8192
NPAIR = TA // 128
NEG = -30000.0
C_ID, C_TRI, C_BLK, C_SEL0, C_SEL1, C_MU, C_MLS, C_OFFD, C_ONES = range(9)
NCONST = 9
RING = 4


def gdn_consts():
    p = np.arange(128)[:, None]
    f = np.arange(128)[None, :]
    same = (p // 64) == (f // 64)
    c = np.zeros((NCONST, 128, 128), np.float32)
    c[C_ID] = (p == f)
    c[C_TRI] = same & (p <= f)
    c[C_BLK] = same
    c[C_SEL0] = (p < 64) & (f >= 0)
    c[C_SEL1] = (p >= 64) & (f >= 0)
    c[C_MU] = np.where(same & (p <= f), 0.0, NEG)
    c[C_MLS] = np.where(same & (p > f), 0.0, NEG)
    c[C_OFFD] = (p != f)
    c[C_ONES] = 1.0
    return np.ascontiguousarray(c.transpose(1, 0, 2).reshape(128, NCONST * 128))


def build_gdn(stage=9, npair=NPAIR):
    nc = bass.Bass("TRN2", target_bir_lowering=False)
    es = ExitStack()
    P = Prog(nc, es)
    qd = nc.dram_tensor("qT", [128, TA + 4], F32, kind="ExternalInput").ap()
    kd = nc.dram_tensor("kT", [128, TA + 4], F32, kind="ExternalInput").ap()
    vd = nc.dram_tensor("vT", [128, TA + 4], F32, kind="ExternalInput").ap()
    zd = nc.dram_tensor("zT", [128, TA], F32, kind="ExternalInput").ap()
    browd = nc.dram_tensor("brow", [128, TA], F32, kind="ExternalInput").ap()
    btokd = nc.dram_tensor("btok", [128, NPAIR], F32, kind="ExternalInput").ap()
    atokd = nc.dram_tensor("atok", [128, NPAIR], F32, kind="ExternalInput").ap()
    cwvd = nc.dram_tensor("cwv", [128, 12], F32, kind="ExternalInput").ap()
    hvd = nc.dram_tensor("hv", [128, 4], F32, kind="ExternalInput").ap()
    cstd = nc.dram_tensor("cst", [128, NCONST * 128], F32, kind="ExternalInput").ap()
    outd = nc.dram_tensor("oaT", [128, TA], F32, kind="ExternalOutput").ap()
    with es:
        cst = P.sb("cst", [128, NCONST * 128], F32)

        def C(i):
            return cst[:, i * 128:(i + 1) * 128]
        identb = P.sb("identb", [128, 128], BF16)
        qnb = P.sb("qnb", [128, TA], BF16)
        knb = P.sb("knb", [128, TA], BF16)
        kbb = P.sb("kbb", [128, TA], BF16)
        vnb = P.sb("vnb", [128, TA], BF16)
        oT = P.sb("oT", [128, TA], F32)
        NB = 1024
        raw = [P.sb(f"raw{i}", [128, NB + 4], F32) for i in range(2)]
        acc = [P.sb(f"acc{i}", [128, NB], F32) for i in range(2)]
        sqs = P.sb("sqs", [128, NB], F32)
        rn = P.sb("rn", [128, NB], F32)
        bsb = P.sb("bsb", [128, NB], F32)
        cwv = P.sb("cwv", [128, 12], F32)
        hv = P.sb("hv", [128, 4], F32)
        small = P.sb("small", [128, 8], F32)
        btok = P.sb("btok", [128, NPAIR], F32)
        atok = P.sb("atok", [128, NPAIR], F32)
        gt = P.sb("gt", [128, NPAIR], F32)
        gc = P.sb("gc", [128, NPAIR], F32)
        edl = P.sb("edl", [128, NPAIR], F32)
        bgc = P.sb("bgc", [128, NPAIR], F32)
        glb = [P.sb(f"glb{h}", [128, NPAIR], F32) for h in range(2)]
        dgt = [P.sb(f"dgt{i}", [128, 128], F32) for i in range(2)]
        tU = [P.sb(f"tU{i}", [128, 128], F32) for i in range(2)]
        tL = [P.sb(f"tL{i}", [128, 128], F32) for i in range(2)]
        EUs = [P.sb(f"EUs{i}", [128, 128], F32) for i in range(2)]
        egc = [P.sb(f"egc{i}", [128, 128], F32) for i in range(2)]
        Lb = [P.sb(f"Lb{i}", [128, 128], BF16) for i in range(2)]
        Ub = [P.sb(f"Ub{i}", [128, 128], BF16) for i in range(2)]
        TTb = [P.sb(f"TTb{i}", [128, 128], BF16) for i in range(2)]
        vb = [P.sb(f"vb{i}", [128, 128], BF16) for i in range(2)]
        kbg = [P.sb(f"kbg{i}", [128, 128], BF16) for i in range(2)]
        attnT = [P.sb(f"attnT{i}", [128, 128], BF16) for i in range(RING)]
        qdec = [P.sb(f"qdec{i}", [128, 128], BF16) for i in range(RING)]
        kdec = [P.sb(f"kdec{i}", [128, 128], BF16) for i in range(RING)]
        u32 = [P.sb(f"u32{i}", [128, 128], F32) for i in range(RING)]
        wTb = [P.sb(f"wTb{i}", [128, 128], BF16) for i in range(RING)]
        vnew = [P.sb(f"vnew{i}", [128, 128], BF16) for i in range(2)]
        S32 = P.sb("S32", [128, 128], F32)
        Sb = P.sb("Sb", [128, 128], BF16)
        bk = [P.ps(f"bk{i}", [128, 512]) if i != 4 else P.ps("bk4", [128, 1024], BF16) for i in range(8)]

        def q4(b, i):
            return bk[b][:, i * 128:(i + 1) * 128]
        pss = [bk[0], bk[1]]
        PDG = [q4(0, 0), q4(0, 1)]
        PKK = [q4(1, 0), q4(1, 1), q4(1, 2)]
        PINV = [q4(2, 0), q4(2, 1), q4(3, 0)]
        ptr = bk[4]
        PUW = [q4(5, 0), q4(5, 1)]
        PWS = [q4(6, 0), q4(6, 1)]
        PDS = [q4(6, 2), q4(6, 3)]
        POT = [q4(7, 0), q4(7, 1)]
        PSET = [q4(2, 2), q4(3, 2), q4(5, 2)]

        P.dma("sp", lambda e: e.dma_start(out=cst[:], in_=cstd), writes=["cst"])
        P.dma("sp", lambda e: e.dma_start(out=cwv[:], in_=cwvd), writes=["cwv"])
        P.dma("sp", lambda e: e.dma_start(out=hv[:], in_=hvd), writes=["hv"])
        P.dma("sp", lambda e: e.dma_start(out=btok[:], in_=btokd), writes=["btok"])
        P.dma("sp", lambda e: e.dma_start(out=atok[:], in_=atokd), writes=["atok"])
        P.op("dve", lambda e: e.memset(small[:, 0:1], 1e-6), writes=["small"])
        P.op("dve", lambda e: e.memset(small[:, 1:2], 1.0), reads=["small"], writes=["small"])
        P.op("dve", lambda e: e.memset(S32[:], 0.0), writes=["S32"])
        P.op("dve", lambda e: e.memset(Sb[:], 0.0), writes=["Sb"])
        P.op("dve", lambda e: e.tensor_copy(out=identb[:], in_=C(C_ID)), reads=["cst"], writes=["identb"])
        if stage >= 0:
            P.op("act", lambda e: e.activation(out=btok[:], in_=btok[:], func=AF.Sigmoid), reads=["btok"], writes=["btok"])
            P.op("act", lambda e: e.activation(out=small[:, 2:3], in_=hv[:, 0:1], func=AF.Exp), reads=["hv", "small"], writes=["small"])
            P.op("dve", lambda e: e.tensor_scalar(out=small[:, 2:3], in0=small[:, 2:3], scalar1=-1.0, scalar2=None, op0=ALU.mult),
                 reads=["small"], writes=["small"])
            P.op("act", lambda e: e.activation(out=atok[:], in_=atok[:], func=AF.Exp, bias=hv[:, 1:2], scale=1.0),
                 reads=["atok", "hv"], writes=["atok"])
            P.op("act", lambda e: e.activation(out=atok[:], in_=atok[:], func=AF.Ln, bias=small[:, 1:2], scale=1.0),
                 reads=["atok", "small"], writes=["atok"])
            P.op("dve", lambda e: e.tensor_scalar(out=gt[:], in0=atok[:], scalar1=small[:, 2:3], scalar2=None, op0=ALU.mult),
                 reads=["atok", "small"], writes=["gt"])
            P.op("pe", lambda e: e.matmul(PSET[0][:, 0:NPAIR], C(C_TRI), gt[:], start=True, stop=True), reads=["cst", "gt"], writes=[("bk", 2)])
            P.op("pe", lambda e: e.matmul(PSET[1][:, 0:NPAIR], C(C_BLK), gt[:], start=True, stop=True), reads=["cst", "gt"], writes=[("bk", 3)])
            P.op("dve", lambda e: e.tensor_copy(out=gc[:], in_=PSET[0][:, 0:NPAIR]), reads=[("bk", 2)], writes=["gc"])
            P.op("dve", lambda e: e.tensor_tensor(out=edl[:], in0=PSET[1][:, 0:NPAIR], in1=gc[:], op=ALU.subtract),
                 reads=[("bk", 3), "gc"], writes=["edl"])
            P.op("act", lambda e: e.activation(out=edl[:], in_=edl[:], func=AF.Exp), reads=["edl"], writes=["edl"])
            P.op("act", lambda e: e.activation(out=bgc[:], in_=gc[:], func=AF.Exp), reads=["gc"], writes=["bgc"])
            P.op("dve", lambda e: e.tensor_tensor(out=bgc[:], in0=bgc[:], in1=btok[:], op=ALU.mult), reads=["bgc", "btok"], writes=["bgc"])
            for h in range(2):
                P.op("pe", lambda e, h=h: e.matmul(PSET[2][:, 0:NPAIR], C(C_SEL0 + h), gt[:], start=True, stop=True),
                     reads=["cst", "gt"], writes=[("bk", 5)])
                P.op("act", lambda e, h=h: e.activation(out=glb[h][:], in_=PSET[2][:, 0:NPAIR], func=AF.Exp),
                     reads=[("bk", 5)], writes=[("glb", h)])


        rctr = [0]
        for b in range(npair * 128 // NB if stage >= 1 else 0):
            bs = slice(b * NB, (b + 1) * NB)
            P.dma("sp", lambda e, bs=bs: e.dma_start(out=bsb[:], in_=browd[:, bs]), writes=["bsb"])
            P.op("act", lambda e: e.activation(out=bsb[:], in_=bsb[:], func=AF.Sigmoid), reads=["bsb"], writes=["bsb"])
            for i, src in enumerate((qd, kd, vd)):
                j = rctr[0] % 2
                rctr[0] += 1
                P.dma("sp", lambda e, src=src, j=j, b=b: e.dma_start(out=raw[j][:], in_=src[:, b * NB:b * NB + NB + 4]),
                      writes=[("raw", j)])
                a = acc[j]
                P.op("dve", lambda e, a=a, j=j, i=i: e.tensor_scalar(out=a[:], in0=raw[j][:, 1:1 + NB], scalar1=cwv[:, i * 4:i * 4 + 1],
                                                                    scalar2=None, op0=ALU.mult),
                     reads=[("raw", j), "cwv"], writes=[("acc", j)])
                for k in range(1, 4):
                    P.op("dve", lambda e, a=a, j=j, i=i, k=k: e.scalar_tensor_tensor(
                        out=a[:], in0=raw[j][:, 1 + k:1 + k + NB], scalar=cwv[:, i * 4 + k:i * 4 + k + 1], in1=a[:],
                        op0=ALU.mult, op1=ALU.add), reads=[("raw", j), "cwv", ("acc", j)], writes=[("acc", j)])
                if i == 2:
                    P.op("act", lambda e, a=a, bs=bs: e.activation(out=vnb[:, bs], in_=a[:], func=AF.Silu),
                         reads=[("acc", j)], writes=[("vnb", b)])
                    continue
                P.op("act", lambda e, a=a: e.activation(out=a[:], in_=a[:], func=AF.Silu), reads=[("acc", j)], writes=[("acc", j)])
                P.op("act", lambda e, a=a: e.activation(out=sqs[:], in_=a[:], func=AF.Square), reads=[("acc", j)], writes=["sqs"])
                for sbk in range(NB // 512):
                    ss = slice(sbk * 512, (sbk + 1) * 512)
                    pp = pss[sbk % 2]
                    P.op("pe", lambda e, pp=pp, ss=ss: e.matmul(pp[:], C(C_ONES), sqs[:, ss], start=True, stop=True),
                         reads=["cst", "sqs"], writes=[("bk", sbk % 2)])
                    P.op("act", lambda e, pp=pp, ss=ss: e.activation(out=rn[:, ss], in_=pp[:], func=AF.Sqrt, bias=small[:, 0:1], scale=1.0),
                         reads=[("bk", sbk % 2), "small"], writes=[("rn", sbk)])
                P.op("dve", lambda e: e.reciprocal(out=rn[:], in_=rn[:]), reads=[("rn", s_) for s_ in range(NB // 512)],
                     writes=[("rn", s_) for s_ in range(NB // 512)])
                rnk = [("rn", s_) for s_ in range(NB // 512)]
                if i == 0:
                    P.op("dve", lambda e, a=a, bs=bs: e.scalar_tensor_tensor(out=qnb[:, bs], in0=a[:], scalar=128.0 ** -0.5, in1=rn[:],
                                                                             op0=ALU.mult, op1=ALU.mult),
                         reads=[("acc", j)] + rnk, writes=[("qnb", b)])
                else:
                    P.op("dve", lambda e, a=a: e.tensor_tensor(out=a[:], in0=a[:], in1=rn[:], op=ALU.mult),
                         reads=[("acc", j)] + rnk, writes=[("acc", j)])
                    P.op("act", lambda e, a=a, bs=bs: e.activation(out=knb[:, bs], in_=a[:], func=AF.Identity),
                         reads=[("acc", j)], writes=[("knb", b)])
                    P.op("pool", lambda e, a=a, bs=bs: e.tensor_tensor(out=kbb[:, bs], in0=a[:], in1=bsb[:], op=ALU.mult),
                         reads=[("acc", j), "bsb"], writes=[("kbb", b)])

        cpy = [0]

        def copy_out(out_ap, in_ap, reads, writes):
            cpy[0] += 1
            if cpy[0] % 2 == 0 and not os.environ.get("DVEONLY"):
                P.op("act", lambda e: e.activation(out=out_ap, in_=in_ap, func=AF.Identity), reads=reads, writes=writes)
            else:
                P.op("dve", lambda e: e.tensor_copy(out=out_ap, in_=in_ap), reads=reads, writes=writes)

        import os
        UPTO = int(os.environ.get('UPTO', 99))

        def prep(m):
            j = m % 2
            r = m % RING
            blk = m * 128 // NB
            ps_ = slice(m * 128, (m + 1) * 128)
            gcm = gc[:, m:m + 1]
            P.op("dve", lambda e: e.tensor_scalar(out=dgt[j][:], in0=C(C_ID), scalar1=gcm, scalar2=None, op0=ALU.mult),
                 reads=["cst", "gc"], writes=[("dgt", j)])
            P.op("pe", lambda e: e.matmul(PDG[j], C(C_ONES), dgt[j][:], start=True, stop=True),
                 reads=["cst", ("dgt", j)], writes=[("bk", 0)])
            if UPTO <= 1:
                return
            P.op("dve", lambda e: e.scalar_tensor_tensor(out=tU[j][:], in0=PDG[j], scalar=gcm, in1=C(C_MU), op0=ALU.subtract, op1=ALU.add),
                 reads=[("bk", 0), "gc", "cst"], writes=[("tU", j)])
            P.op("act", lambda e: e.activation(out=tU[j][:], in_=tU[j][:], func=AF.Exp), reads=[("tU", j)], writes=[("tU", j)])
            P.op("dve", lambda e: e.scalar_tensor_tensor(out=tL[j][:], in0=PDG[j], scalar=gcm, in1=C(C_MLS), op0=ALU.subtract, op1=ALU.subtract),
                 reads=[("bk", 0), "gc", "cst"], writes=[("tL", j)])
            P.op("act", lambda e: e.activation(out=tL[j][:], in_=tL[j][:], func=AF.Exp, scale=-1.0), reads=[("tL", j)], writes=[("tL", j)])
            P.op("act", lambda e: e.activation(out=egc[j][:], in_=PDG[j], func=AF.Exp), reads=[("bk", 0)], writes=[("egc", j)])
            P.op("pool", lambda e: e.tensor_tensor(out=EUs[j][:], in0=tU[j][:], in1=C(C_OFFD), op=ALU.mult),
                 reads=[("tU", j), "cst"], writes=[("EUs", j)])
            if UPTO <= 2:
                return
            P.op("pe", lambda e: e.matmul(PKK[0], knb[:, ps_], kbb[:, ps_], start=True, stop=True),
                 reads=[("knb", blk), ("kbb", blk)], writes=[("bk", 1)])
            P.op("pe", lambda e: e.matmul(PKK[1], kbb[:, ps_], knb[:, ps_], start=True, stop=True),
                 reads=[("knb", blk), ("kbb", blk)], writes=[("bk", 1)])
            P.op("pe", lambda e: e.matmul(PKK[2], knb[:, ps_], qnb[:, ps_], start=True, stop=True),
                 reads=[("knb", blk), ("qnb", blk)], writes=[("bk", 1)])
            if UPTO <= 3:
                return
            P.op("dve", lambda e: e.tensor_tensor(out=Ub[0][:], in0=PKK[0], in1=EUs[j][:], op=ALU.mult),
                 reads=[("bk", 1), ("EUs", j)], writes=[("Ub", 0)])
            P.op("dve", lambda e: e.tensor_tensor(out=Lb[0][:], in0=PKK[1], in1=tL[j][:], op=ALU.mult),
                 reads=[("bk", 1), ("tL", j)], writes=[("Lb", 0)])
            P.op("dve", lambda e: e.tensor_tensor(out=attnT[r][:], in0=PKK[2], in1=tU[j][:], op=ALU.mult),
                 reads=[("bk", 1), ("tU", j)], writes=[("attnT", r)])
            P.op("dve", lambda e: e.scalar_tensor_tensor(out=TTb[0][:], in0=Ub[0][:], scalar=-1.0, in1=C(C_ID), op0=ALU.mult, op1=ALU.add),
                 reads=[("Ub", 0), "cst"], writes=[("TTb", 0)])
            if UPTO <= 4:
                return
            cur = 0
            for k in range(5):
                nx = 1 - cur
                P.op("pe", lambda e, cur=cur: e.matmul(PINV[0], Ub[cur][:], Lb[cur][:], start=True, stop=True),
                     reads=[("Ub", cur), ("Lb", cur)], writes=[("bk", 2)])
                if k < 4:
                    P.op("pe", lambda e, cur=cur: e.matmul(PINV[1], Lb[cur][:], Ub[cur][:], start=True, stop=True),
                         reads=[("Ub", cur), ("Lb", cur)], writes=[("bk", 2)])
                copy_out(Lb[nx][:], PINV[0], [("bk", 2)], [("Lb", nx)])
                if k < 4:
                    copy_out(Ub[nx][:], PINV[1], [("bk", 2)], [("Ub", nx)])
                if os.environ.get("NOTT"):
                    cur = nx
                    continue
                P.op("pe", lambda e, cur=cur: e.matmul(PINV[2], identb[:], TTb[cur][:], start=True, stop=False),
                     reads=["identb", ("TTb", cur)], writes=[("bk", 3)])
                P.op("pe", lambda e, cur=cur, nx=nx: e.matmul(PINV[2], Lb[nx][:], TTb[cur][:], start=False, stop=True),
                     reads=[("Lb", nx), ("TTb", cur)], writes=[("bk", 3)])
                copy_out(TTb[nx][:], PINV[2], [("bk", 3)], [("TTb", nx)])
                cur = nx
            TT = TTb[cur]
            ttk = ("TTb", cur)
            if UPTO <= 5:
                return
            P.op("pe", lambda e: e.transpose(ptr[:, 0:128], vnb[:, ps_], identb[:]), reads=[("vnb", blk), "identb"], writes=[("bk", 4)])
            P.op("pe", lambda e: e.transpose(ptr[:, 128:256], knb[:, ps_], identb[:]), reads=[("knb", blk), "identb"], writes=[("bk", 4)])
            P.op("dve", lambda e: e.tensor_scalar(out=vb[j][:], in0=ptr[:, 0:128], scalar1=btok[:, m:m + 1], scalar2=None, op0=ALU.mult),
                 reads=[("bk", 4), "btok"], writes=[("vb", j)])
            P.op("dve", lambda e: e.tensor_scalar(out=kbg[j][:], in0=ptr[:, 128:256], scalar1=bgc[:, m:m + 1], scalar2=None, op0=ALU.mult),
                 reads=[("bk", 4), "bgc"], writes=[("kbg", j)])
            P.op("dve", lambda e: e.tensor_scalar(out=kdec[r][:], in0=ptr[:, 128:256], scalar1=edl[:, m:m + 1], scalar2=None, op0=ALU.mult),
                 reads=[("bk", 4), "edl"], writes=[("kdec", r)])
            if UPTO <= 6:
                return
            P.op("pe", lambda e: e.matmul(PUW[0], TT[:], vb[j][:], start=True, stop=True), reads=[ttk, ("vb", j)], writes=[("bk", 5)])
            P.op("pe", lambda e: e.matmul(PUW[1], kbg[j][:], TT[:], start=True, stop=True), reads=[ttk, ("kbg", j)], writes=[("bk", 5)])
            copy_out(u32[r][:], PUW[0], [("bk", 5)], [("u32", r)])
            copy_out(wTb[r][:], PUW[1], [("bk", 5)], [("wTb", r)])
            P.op("pool", lambda e: e.tensor_tensor(out=qdec[r][:], in0=qnb[:, ps_], in1=egc[j][:], op=ALU.mult),
                 reads=[("qnb", blk), ("egc", j)], writes=[("qdec", r)])

        def scan(m):
            r = m % RING
            for h in range(2):
                n = 2 * m + h
                hs = slice(h * 64, (h + 1) * 64)
                j = n % 2
                P.op("pe", lambda e, j=j, hs=hs: e.matmul(PWS[j][hs, :], wTb[r][:, hs], Sb[:], start=True, stop=True),
                     reads=[("wTb", r), "Sb"], writes=[("bk", 6)])
                P.op("dve", lambda e, j=j, hs=hs: e.tensor_tensor(out=vnew[j][hs, :], in0=u32[r][hs, :], in1=PWS[j][hs, :], op=ALU.subtract),
                     reads=[("u32", r), ("bk", 6)], writes=[("vnew", j)])
                P.op("pe", lambda e, j=j, hs=hs: e.matmul(POT[j][:, 0:64], Sb[:], qdec[r][:, hs], start=True, stop=False),
                     reads=["Sb", ("qdec", r)], writes=[("bk", 7)])
                P.op("pe", lambda e, j=j, hs=hs: e.matmul(POT[j][:, 0:64], vnew[j][hs, :], attnT[r][hs, hs], start=False, stop=True),
                     reads=[("vnew", j), ("attnT", r)], writes=[("bk", 7)])
                P.op("pe", lambda e, j=j, hs=hs: e.matmul(PDS[j], kdec[r][hs, :], vnew[j][hs, :], start=True, stop=True),
                     reads=[("kdec", r), ("vnew", j)], writes=[("bk", 6)])
                P.op("act", lambda e, j=j, n=n: e.activation(out=oT[:, n * 64:(n + 1) * 64], in_=POT[j][:, 0:64], func=AF.Identity),
                     reads=[("bk", 7)], writes=[("oT", n * 64 // NB)])
                P.op("dve", lambda e, j=j, h=h: e.scalar_tensor_tensor(out=S32[:], in0=S32[:], scalar=glb[h][:, m:m + 1], in1=PDS[j],
                                                                       op0=ALU.mult, op1=ALU.add),
                     reads=["S32", ("glb", h), ("bk", 6)], writes=["S32"])
                P.op("act", lambda e: e.activation(out=Sb[:], in_=S32[:], func=AF.Identity), reads=["S32"], writes=["Sb"])

        LAG = 2
        for m in range(npair + LAG):
            if m < npair and stage >= 2:
                prep(m)
            if m >= LAG and stage >= 3:
                scan(m - LAG)

        import os
        if os.environ.get("SKIP_POST"):
            P.dma("sp", lambda e: e.dma_start(out=outd[:, 0:1024], in_=cst[:, 0:1024]), reads=["cst"])
        for b in range(npair * 128 // NB if not os.environ.get("SKIP_POST") else 0):
            bs = slice(b * NB, (b + 1) * NB)
            j = b % 2
            P.dma("sp", lambda e, bs=bs, j=j: e.dma_start(out=raw[j][:, 0:NB], in_=zd[:, bs]), writes=[("raw", j)])
            P.op("act", lambda e, j=j: e.activation(out=raw[j][:, 0:NB], in_=raw[j][:, 0:NB], func=AF.Silu), reads=[("raw", j)], writes=[("raw", j)])
            P.op("act", lambda e, bs=bs: e.activation(out=sqs[:], in_=oT[:, bs], func=AF.Square), reads=[("oT", b)], writes=["sqs"])
            for sbk in range(NB // 512):
                ss = slice(sbk * 512, (sbk + 1) * 512)
                pp = pss[sbk % 2]
                P.op("pe", lambda e, pp=pp, ss=ss: e.matmul(pp[:], C(C_ONES), sqs[:, ss], start=True, stop=True),
                     reads=["cst", "sqs"], writes=[("bk", sbk % 2)])
                P.op("act", lambda e, pp=pp, ss=ss: e.activation(out=rn[:, ss], in_=pp[:], func=AF.Sqrt, bias=small[:, 0:1], scale=1.0 / 128),
                     reads=[("bk", sbk % 2), "small"], writes=[("rn", sbk)])
            rnk = [("rn", s_) for s_ in range(NB // 512)]
            P.op("dve", lambda e: e.reciprocal(out=rn[:], in_=rn[:]), reads=rnk, writes=rnk)
            a = acc[j]
            P.op("dve", lambda e, a=a, bs=bs: e.scalar_tensor_tensor(out=a[:], in0=oT[:, bs], scalar=hv[:, 2:3], in1=rn[:], op0=ALU.mult, op1=ALU.mult),
                 reads=[("oT", b), "hv"] + rnk, writes=[("acc", j)])
            P.op("pool", lambda e, a=a, j=j: e.tensor_tensor(out=a[:], in0=a[:], in1=raw[j][:, 0:NB], op=ALU.mult),
                 reads=[("acc", j), ("raw", j)], writes=[("acc", j)])
            P.dma("sp", lambda e, a=a, bs=bs: e.dma_start(out=outd[:, bs], in_=a[:]), reads=[("acc", j)])
        P.emit()
    return nc


import math

TA = 8192
NT = TA // 128
HT = 32
MAGIC = 12582912.0
C1 = 6.28125
C2 = 2.0 * math.pi - 6.28125
PI_SAFE = 3.1415925


def ret_consts():
    p = np.arange(128)[:, None]
    f = np.arange(128)[None, :]
    c = np.zeros((128, 256), np.float32)
    c[:, 0:128] = (p <= f)
    c[:, 128:256] = (p == f)
    return c


def ret_rc(head):
    gamma = 1.0 - 2.0 ** (-5.0 - head)
    rc = np.zeros((128, 72), np.float32)
    half = 64
    rc[:, 0:64] = (10000.0 ** (-np.arange(half, dtype=np.float32) / half)).astype(np.float32)[None, :]
    p = np.arange(128, dtype=np.float64)
    rc[:, 64] = gamma ** (p + 1)
    rc[:, 65] = 128.0 ** -0.5 * gamma ** (-(p + 1))
    rc[:, 66] = gamma ** 128
    rc[:, 67] = math.pi / 2
    return rc


def build_ret(nt=NT):
    nc = bass.Bass("TRN2", target_bir_lowering=False)
    es = ExitStack()
    P = Prog(nc, es)
    qd = nc.dram_tensor("q_tok", [128, NT, 128], F32, kind="ExternalInput").ap()
    kd = nc.dram_tensor("k_tok", [128, NT, 128], F32, kind="ExternalInput").ap()
    vd = nc.dram_tensor("v_tok", [128, NT, 128], F32, kind="ExternalInput").ap()
    posd = nc.dram_tensor("pos_tok", [128, NT], I32, kind="ExternalInput").ap()
    rcd = nc.dram_tensor("rc", [128, 72], F32, kind="ExternalInput").ap()
    cstd = nc.dram_tensor("cst", [128, 256], F32, kind="ExternalInput").ap()
    outd = nc.dram_tensor("oretT", [128, TA], F32, kind="ExternalOutput").ap()
    nh = (nt + HT - 1) // HT
    with es:
        cst = P.sb("cst", [128, 256], F32)
        rc = P.sb("rc", [128, 72], F32)
        posi = P.sb("posi", [128, NT], I32)
        posf = P.sb("posf", [128, NT], F32)
        identb = P.sb("identb", [128, 128], BF16)
        qh = P.sb("qh", [128, HT, 128], F32)
        kh = P.sb("kh", [128, HT, 128], F32)
        ang = P.sb("ang", [128, HT, 64], F32)
        tt = P.sb("tt", [128, HT, 64], F32)
        cs = P.sb("cs", [128, HT, 64], F32)
        sn = P.sb("sn", [128, HT, 64], F32)
        A = P.sb("A", [128, HT, 64], F32)
        B = P.sb("B", [128, HT, 64], F32)
        qb = P.sb("qb", [128, NT, 128], BF16)
        kb = P.sb("kb", [128, NT, 128], BF16)
        vb = P.sb("vb", [128, NT, 128], BF16)
        oT = P.sb("oT", [128, TA], F32)
        qinT = [P.sb(f"qinT{i}", [128, 128], BF16) for i in range(2)]
        koutT = [P.sb(f"koutT{i}", [128, 128], BF16) for i in range(2)]
        scm = [P.sb(f"scm{i}", [128, 128], BF16) for i in range(2)]
        W32 = P.sb("W32", [128, 128], F32)
        Rb = P.sb("Rb", [128, 128], BF16)
        bkq = P.ps("bkq", [128, 1024], BF16)
        bkk = P.ps("bkk", [128, 1024], BF16)
        bks = P.ps("bks", [128, 512])
        bko = P.ps("bko", [128, 512])
        bkr = P.ps("bkr", [128, 512])

        P.dma("sp", lambda e: e.dma_start(out=cst[:], in_=cstd), writes=["cst"])
        P.dma("sp", lambda e: e.dma_start(out=rc[:], in_=rcd), writes=["rc"])
        P.dma("sp", lambda e: e.dma_start(out=posi[:], in_=posd), writes=["posi"])
        for hq in range(nh * 2):
            P.dma("pool", lambda e, hq=hq: e.dma_start(out=vb[:, hq * 16:(hq + 1) * 16, :], in_=vd[:, hq * 16:(hq + 1) * 16, :]),
                  writes=[("vb", hq // 2)])
        P.op("dve", lambda e: e.tensor_copy(out=posf[:], in_=posi[:]), reads=["posi"], writes=["posf"])
        P.op("dve", lambda e: e.tensor_copy(out=identb[:], in_=cst[:, 128:256]), reads=["cst"], writes=["identb"])
        P.op("dve", lambda e: e.memset(W32[:], 0.0), writes=["W32"])
        P.op("dve", lambda e: e.memset(Rb[:], 0.0), writes=["Rb"])

        def fl(t):
            return t[:].rearrange("p a b -> p (a b)")

        def reduce_to(dst_key):
            P.op("dve", lambda e: e.tensor_scalar(out=fl(tt), in0=fl(ang), scalar1=1.0 / (2 * math.pi), scalar2=MAGIC, op0=ALU.mult, op1=ALU.add),
                 reads=["ang"], writes=["tt"])
            P.op("dve", lambda e: e.tensor_scalar(out=fl(tt), in0=fl(tt), scalar1=-MAGIC, scalar2=None, op0=ALU.add),
                 reads=["tt"], writes=["tt"])
            P.op("dve", lambda e: e.scalar_tensor_tensor(out=fl(A), in0=fl(tt), scalar=-C1, in1=fl(ang), op0=ALU.mult, op1=ALU.add),
                 reads=["tt", "ang"], writes=["A"])
            P.op("dve", lambda e: e.scalar_tensor_tensor(out=fl(A), in0=fl(tt), scalar=-C2, in1=fl(A), op0=ALU.mult, op1=ALU.add),
                 reads=["tt", "A"], writes=["A"])
            P.op("dve", lambda e: e.tensor_scalar(out=fl(A), in0=fl(A), scalar1=-PI_SAFE, scalar2=PI_SAFE, op0=ALU.max, op1=ALU.min),
                 reads=["A"], writes=["A"])

        for hh in range(nh):
            m0 = hh * HT
            P.dma("sp", lambda e, m0=m0: e.dma_start(out=qh[:], in_=qd[:, m0:m0 + HT, :]), writes=["qh"])
            P.dma("sp", lambda e, m0=m0: e.dma_start(out=kh[:], in_=kd[:, m0:m0 + HT, :]), writes=["kh"])
            for i in range(HT):
                P.op("pool", lambda e, i=i, m0=m0: e.tensor_scalar(out=ang[:, i, :], in0=rc[:, 0:64], scalar1=posf[:, m0 + i:m0 + i + 1],
                                                                  scalar2=None, op0=ALU.mult),
                     reads=["rc", "posf"], writes=["ang"])
            reduce_to("A")
            P.op("act", lambda e: e.activation(out=fl(sn), in_=fl(A), func=AF.Sin), reads=["A"], writes=["sn"])
            P.op("dve", lambda e: e.scalar_tensor_tensor(out=fl(A), in0=fl(A), scalar=-1.0, in1=fl(A), op0=ALU.mult, op1=ALU.max), reads=["A"], writes=["A"])
            P.op("act", lambda e: e.activation(out=fl(cs), in_=fl(A), func=AF.Sin, scale=-1.0, bias=rc[:, 67:68]), reads=["A", "rc"], writes=["cs"])
            for (src, dst, col, eng) in ((qh, qb, 64, "dve"), (kh, kb, 65, "dve")):
                x1 = src[:, :, 0:64]
                x2 = src[:, :, 64:128]
                skey = "qh" if src is qh else "kh"
                dkey = ("qb", hh) if dst is qb else ("kb", hh)
                P.op("dve", lambda e, x1=x1: e.tensor_tensor(out=A[:], in0=x1, in1=cs[:], op=ALU.mult), reads=[skey, "cs"], writes=["A"])
                P.op("pool", lambda e, x2=x2: e.tensor_tensor(out=B[:], in0=x2, in1=sn[:], op=ALU.mult), reads=[skey, "sn"], writes=["B"])
                P.op("dve", lambda e: e.tensor_tensor(out=A[:], in0=A[:], in1=B[:], op=ALU.subtract), reads=["A", "B"], writes=["A"])
                P.op("dve", lambda e, dst=dst, m0=m0, col=col: e.tensor_scalar(out=dst[:, m0:m0 + HT, 0:64], in0=A[:], scalar1=rc[:, col:col + 1],
                                                                               scalar2=None, op0=ALU.mult),
                     reads=["A", "rc"], writes=[dkey])
                P.op("dve", lambda e, x1=x1: e.tensor_tensor(out=A[:], in0=x1, in1=sn[:], op=ALU.mult), reads=[skey, "sn"], writes=["A"])
                P.op("pool", lambda e, x2=x2: e.tensor_tensor(out=B[:], in0=x2, in1=cs[:], op=ALU.mult), reads=[skey, "cs"], writes=["B"])
                P.op("dve", lambda e: e.tensor_tensor(out=A[:], in0=A[:], in1=B[:], op=ALU.add), reads=["A", "B"], writes=["A"])
                P.op("dve", lambda e, dst=dst, m0=m0, col=col: e.tensor_scalar(out=dst[:, m0:m0 + HT, 64:128], in0=A[:], scalar1=rc[:, col:col + 1],
                                                                               scalar2=None, op0=ALU.mult),
                     reads=["A", "rc"], writes=[dkey])

        for m in range(nt):
            hh = m // HT
            j = m % 2
            P.op("pe", lambda e, m=m: e.transpose(bkq[:, 0:128], qb[:, m, :], identb[:]), reads=[("qb", hh), "identb"], writes=[("bk", 0)])
            P.op("pe", lambda e, m=m: e.transpose(bkk[:, 0:128], kb[:, m, :], identb[:]), reads=[("kb", hh), "identb"], writes=[("bk", 1)])
            P.op("dve", lambda e, j=j: e.tensor_copy(out=qinT[j][:], in_=bkq[:, 0:128]), reads=[("bk", 0)], writes=[("qinT", j)])
            P.op("act", lambda e, j=j: e.activation(out=koutT[j][:], in_=bkk[:, 0:128], func=AF.Identity), reads=[("bk", 1)], writes=[("koutT", j)])
            P.op("pe", lambda e, j=j: e.matmul(bks[:, 0:128], koutT[j][:], qinT[j][:], start=True, stop=True),
                 reads=[("koutT", j), ("qinT", j)], writes=[("bk", 2)])
            P.op("dve", lambda e, j=j: e.tensor_tensor(out=scm[j][:], in0=bks[:, 0:128], in1=cst[:, 0:128], op=ALU.mult),
                 reads=[("bk", 2), "cst"], writes=[("scm", j)])
            P.op("pe", lambda e, j=j, m=m: e.matmul(bko[:, 0:128], vb[:, m, :], scm[j][:], start=True, stop=False),
                 reads=[("vb", hh), ("scm", j)], writes=[("bk", 3)])
            P.op("pe", lambda e, j=j: e.matmul(bko[:, 0:128], Rb[:], qinT[j][:], start=False, stop=True),
                 reads=["Rb", ("qinT", j)], writes=[("bk", 3)])
            P.op("pe", lambda e, m=m: e.matmul(bkr[:, 0:128], kb[:, m, :], vb[:, m, :], start=True, stop=True),
                 reads=[("kb", hh), ("vb", hh)], writes=[("bk", 4)])
            P.op("act", lambda e, m=m: e.activation(out=oT[:, m * 128:(m + 1) * 128], in_=bko[:, 0:128], func=AF.Identity),
                 reads=[("bk", 3)], writes=[("oT", m // 8)])
            P.op("dve", lambda e: e.scalar_tensor_tensor(out=W32[:], in0=W32[:], scalar=rc[:, 66:67], in1=bkr[:, 0:128], op0=ALU.mult, op1=ALU.add),
                 reads=["W32", "rc", ("bk", 4)], writes=["W32"])
            P.op("act", lambda e: e.activation(out=Rb[:], in_=W32[:], func=AF.Identity, scale=rc[:, 66:67]), reads=["W32", "rc"], writes=["Rb"])
            if m % 8 == 7:
                g = m // 8
                P.dma("sp", lambda e, g=g: e.dma_start(out=outd[:, g * 1024:(g + 1) * 1024], in_=oT[:, g * 1024:(g + 1) * 1024]),
                      reads=[("oT", g)])
        P.emit()
    return nc


_sizes = (1024, 1024, 1024, 1024, 8, 8, 512, 512, 1024, 1024, 2048, 6144)
_off = np.concatenate([[0], np.cumsum(_sizes)])
_order = [0, 1, 2, 3, 6, 7, 8, 9, 10, 11, 4, 5]
PERM = np.concatenate([np.arange(_off[i], _off[i + 1]) for i in _order])
_psizes = [_sizes[i] for i in _order]
_poff = np.concatenate([[0], np.cumsum(_psizes)])
P_AQ, P_AK, P_AV, P_AZ, P_BQ, P_BK, P_BV, P_BG, P_CGLU, P_GATE, P_BETA, P_ALPHA = [int(v) for v in _poff[:12]]


def pk(v):
    return np.ascontiguousarray(v.reshape(-1, 128).T)


def make_vec(mod_l, nmix, nmlp, nfin):
    parts = [pk(m) for m in np.split(mod_l, 6)] + [pk(nmix), pk(nmlp), pk(nfin)]
    return np.ascontiguousarray(np.concatenate(parts, axis=1).astype(np.float32))


def make_cgT(cg, c, T=1024):
    out = np.zeros((2048, T + 32), np.float32)
    lo = c * T - 32
    if lo >= 0:
        out[:] = cg[lo:(c + 1) * T].T
    else:
        out[:, 32:] = cg[0:T].T
    return out


def make_convc_params(dw_w, dw_b, ln_w, ln_b):
    cw = np.ascontiguousarray(dw_w.reshape(31, 8, 128).transpose(2, 1, 0).reshape(128, 8 * 31)).astype(np.float32)
    cvec = np.ascontiguousarray(np.concatenate([pk(dw_b), pk(ln_w), pk(ln_b)], axis=1)).astype(np.float32)
    return cw, cvec


def make_gdn_inputs(aq, ak, av, az, abeta, aalpha, conv_w, a_log, dt_bias, norm_w, hd, cst):
    Tn = aq.shape[0]
    hs = slice(hd * 128, (hd + 1) * 128)

    def padT(a):
        o = np.zeros((128, Tn + 4), np.float32)
        o[:, 4:] = a[:, hs].T
        return o
    cwv = np.zeros((128, 12), np.float32)
    for i in range(3):
        cwv[:, i * 4:(i + 1) * 4] = conv_w[:, i * 1024 + hd * 128:i * 1024 + (hd + 1) * 128].T
    hv = np.zeros((128, 4), np.float32)
    hv[:, 0] = a_log[hd]
    hv[:, 1] = dt_bias[hd]
    hv[:, 2] = norm_w
    return {"qT": padT(aq), "kT": padT(ak), "vT": padT(av), "zT": np.ascontiguousarray(az[:, hs].T),
            "brow": np.ascontiguousarray(np.broadcast_to(abeta[:, hd][None, :], (128, Tn))),
            "btok": np.ascontiguousarray(abeta[:, hd].reshape(-1, 128).T),
            "atok": np.ascontiguousarray(aalpha[:, hd].reshape(-1, 128).T),
            "cwv": cwv, "hv": hv, "cst": cst}


def tokmaj(a, cols):
    x = a[:, cols]
    return np.ascontiguousarray(x.reshape(-1, 128, x.shape[1]).transpose(1, 0, 2))


_PROGS = {}


def _prog(name, fn):
    if name not in _PROGS:
        _PROGS[name] = fn()
    return _PROGS[name]


def _run(nc, in_maps):
    res = run_bass_kernel_spmd(nc, in_maps, core_ids=list(range(len(in_maps))))
    return res.results


def _c(a):
    return np.ascontiguousarray(a, dtype=np.float32)


def kernel(x, c, positions, w_ada, b_ada, norm_mix_w, norm_mlp_w, w_in, conv_qkv_w,
           gdn_a_log, gdn_dt_bias, gdn_norm_w, conv_dw_w, conv_dw_b, conv_ln_w, conv_ln_b,
           w_branch_a, w_branch_b, w_branch_c, w_out, w_mlp_in, w_mlp_out, final_norm_w):
    NCORE = 8
    TT = 1024
    x = np.asarray(x, np.float32)[0]
    positions = np.asarray(positions)
    nc = _prog("ada", build_ada)
    ims = [{"cT": pk(np.asarray(c, np.float32)[0]),
            "wa": _c(np.asarray(w_ada)[:, :, j * 1536:(j + 1) * 1536]),
            "ba": _c(np.asarray(b_ada)[:, j * 1536:(j + 1) * 1536])} for j in range(NCORE)]
    res = _run(nc, ims)
    mod = np.concatenate([r["mod"] for r in res], axis=1)
    xT = [_c(x[j * TT:(j + 1) * TT].T) for j in range(NCORE)]
    gcst = gdn_consts()
    rcst = ret_consts()
    pos_tok = np.ascontiguousarray(positions[0].reshape(-1, 128).T.astype(np.int32))
    depth = np.asarray(w_in).shape[0]
    for l in range(depth):
        vec = make_vec(mod[l], np.asarray(norm_mix_w)[l], np.asarray(norm_mlp_w)[l], np.asarray(final_norm_w))
        w_in_p = _c(np.asarray(w_in)[l][:, PERM])
        res = _run(_prog("pre", build_pre), [{"xT": xT[j], "vec": vec, "w_in": w_in_p} for j in range(NCORE)])
        projT = np.concatenate([r["projT"] for r in res], axis=1)
        del res, w_in_p
        cqw = np.asarray(conv_qkv_w)[l]
        ims = []
        for hd in range(NCORE):
            def padT(r0):
                o = np.zeros((128, 8192 + 4), np.float32)
                o[:, 4:] = projT[r0 + hd * 128:r0 + (hd + 1) * 128]
                return o
            cwv = np.zeros((128, 12), np.float32)
            for i in range(3):
                cwv[:, i * 4:(i + 1) * 4] = cqw[:, i * 1024 + hd * 128:i * 1024 + (hd + 1) * 128].T
            hv = np.zeros((128, 4), np.float32)
            hv[:, 0] = np.asarray(gdn_a_log)[l][hd]
            hv[:, 1] = np.asarray(gdn_dt_bias)[l][hd]
            hv[:, 2] = np.asarray(gdn_norm_w)[l]
            brow = projT[P_BETA + hd]
            arow = projT[P_ALPHA + hd]
            ims.append({"qT": padT(P_AQ), "kT": padT(P_AK), "vT": padT(P_AV),
                        "zT": _c(projT[P_AZ + hd * 128:P_AZ + (hd + 1) * 128]),
                        "brow": _c(np.broadcast_to(brow[None, :], (128, 8192))),
                        "btok": _c(brow.reshape(-1, 128).T), "atok": _c(arow.reshape(-1, 128).T),
                        "cwv": cwv, "hv": hv, "cst": gcst})
        res = _run(_prog("gdn", build_gdn), ims)
        oaT = np.concatenate([r["oaT"] for r in res], axis=0)
        ims = []
        for j in range(NCORE):
            head, half = j // 2, j % 2

            def tokm(r0):
                a = projT[r0:r0 + 128]
                return _c(a.reshape(128, 64, 128).transpose(2, 1, 0))
            ims.append({"q_tok": tokm(P_BQ + head * 128), "k_tok": tokm(P_BK + head * 128),
                        "v_tok": tokm(P_BV + head * 256 + half * 128), "pos_tok": pos_tok,
                        "rc": ret_rc(head), "cst": rcst})
        res = _run(_prog("ret", build_ret), ims)
        oretT = np.concatenate([r["oretT"] for r in res], axis=0)
        cw, cvec = make_convc_params(np.asarray(conv_dw_w)[l], np.asarray(conv_dw_b)[l],
                                     np.asarray(conv_ln_w)[l], np.asarray(conv_ln_b)[l])
        ims = []
        for j in range(NCORE):
            cg = np.zeros((2048, TT + 32), np.float32)
            lo = j * TT - 32
            if lo >= 0:
                cg[:] = projT[P_CGLU:P_CGLU + 2048, lo:(j + 1) * TT]
            else:
                cg[:, 32:] = projT[P_CGLU:P_CGLU + 2048, 0:TT]
            ims.append({"cgT": cg, "cw": cw, "cvec": cvec})
        res = _run(_prog("convc", build_convc), ims)
        ocT = [r["ocT"] for r in res]
        ims = [{"oretT": _c(oretT[:, j * TT:(j + 1) * TT]), "bgT": _c(projT[P_BG:P_BG + 1024, j * TT:(j + 1) * TT])}
               for j in range(NCORE)]
        res = _run(_prog("retln", build_retln), ims)
        obT = [r["obT"] for r in res]
        wba, wbb, wbc = _c(np.asarray(w_branch_a)[l]), _c(np.asarray(w_branch_b)[l]), _c(np.asarray(w_branch_c)[l])
        wo = _c(np.asarray(w_out)[l])
        ims = [{"xT": xT[j], "vec": vec, "oaT": _c(oaT[:, j * TT:(j + 1) * TT]), "obT": obT[j], "ocT": ocT[j],
                "gT": _c(projT[P_GATE:P_GATE + 6144, j * TT:(j + 1) * TT]),
                "wba": wba, "wbb": wbb, "wbc": wbc, "w_out": wo} for j in range(NCORE)]
        res = _run(_prog("merge", build_merge), ims)
        x1T = [r["x1T"] for r in res]
        del projT
        final = (l == depth - 1)
        w1, w2 = _c(np.asarray(w_mlp_in)[l]), _c(np.asarray(w_mlp_out)[l])
        ims = [{"xT": x1T[j], "vec": vec, "w1": w1, "w2": w2} for j in range(NCORE)]
        res = _run(_prog("mlpF" if final else "mlp", (lambda: build_mlp(True)) if final else (lambda: build_mlp(False))), ims)
        xT = [r["x2T"] for r in res]
    out = np.concatenate([t.T for t in xT], axis=0)[None]
    return np.ascontiguousarray(out, dtype=np.float32)
```

```python
import os

import numpy as np
from contextlib import ExitStack
import concourse.bass as bass
import concourse.mybir as mybir
from concourse.bass_utils import run_bass_kernel_spmd

F32 = mybir.dt.float32
BF16 = mybir.dt.bfloat16
I32 = mybir.dt.int32
AF = mybir.ActivationFunctionType
ALU = mybir.AluOpType
AX = mybir.AxisListType


class Prog:
    ENGS = ("pe", "dve", "act", "pool", "sp")
    NDMA = 8

    def __init__(self, nc, es, same_engine_sync=True):
        self.nc = nc
        self.es = es
        self.same = same_engine_sync
        self.ops = {e: [] for e in self.ENGS}
        self.sems = {}
        self.cnt = {}
        for e in ("pe", "dve", "act", "pool"):
            self.sems[e] = es.enter_context(nc.semaphore("c_" + e))
            self.cnt[e] = 0
        for q in ("sp", "pool", "act"):
            for i in range(self.NDMA):
                k = ("dma", q, i)
                self.sems[k] = es.enter_context(nc.semaphore(f"d_{q}{i}"))
                self.cnt[k] = 0
        self.dma_i = {"sp": 0, "pool": 0, "act": 0}
        self.waited = {}
        self.lastw = {}
        self.readers = {}
        self.n_ops = 0

    def sb(self, name, shape, dt):
        return self.es.enter_context(self.nc.sbuf_tensor("s_" + name, list(shape), dt))

    def ps(self, name, shape, dt=F32):
        return self.es.enter_context(self.nc.psum_tensor("p_" + name, list(shape), dt))

    def _deps(self, reads, writes):
        deps = {}

        def add(tok):
            if tok is None:
                return
            k, v = tok
            if deps.get(k, 0) < v:
                deps[k] = v
        for b in reads:
            add(self.lastw.get(b))
        for b in writes:
            add(self.lastw.get(b))
            for k, v in self.readers.get(b, {}).items():
                add((k, v))
        return deps

    def _commit(self, tok, reads, writes):
        k, v = tok
        for b in reads:
            r = self.readers.setdefault(b, {})
            if r.get(k, 0) < v:
                r[k] = v
        for b in writes:
            self.lastw[b] = tok
            self.readers[b] = {}

    def _waits(self, eng, deps):
        waits = []
        for k, v in deps.items():
            if k == "pe" and eng == "pe":
                continue
            if (not self.same) and k == eng:
                continue
            if self.waited.get((eng, k), 0) >= v:
                continue
            self.waited[(eng, k)] = v
            waits.append((k, v))
        return waits

    @staticmethod
    def _excl(reads, writes):
        ex = [b for b in reads if isinstance(b, tuple) and b[0] == "bk"]
        if ex:
            writes = list(writes) + [b for b in ex if b not in writes]
        return reads, writes

    def op(self, eng, fn, reads=(), writes=()):
        reads, writes = self._excl(reads, writes)
        deps = self._deps(reads, writes)
        waits = self._waits(eng, deps)
        self.cnt[eng] += 1
        tok = (eng, self.cnt[eng])
        self.ops[eng].append((waits, fn, (eng, 1)))
        self._commit(tok, reads, writes)
        self.n_ops += 1
        return tok

    def dma(self, q, fn, reads=(), writes=()):
        i = self.dma_i[q]
        self.dma_i[q] += 1
        k = ("dma", q, i % self.NDMA)
        deps = self._deps(reads, writes)
        if self.cnt[k] > 0:
            if deps.get(k, 0) < self.cnt[k]:
                deps[k] = self.cnt[k]
        waits = self._waits(q, deps)
        self.cnt[k] += 16
        tok = (k, self.cnt[k])
        self.ops[q].append((waits, fn, (k, 16)))
        self._commit(tok, reads, writes)
        self.n_ops += 1
        return tok

    def emit(self):
        final = []
        for k, v in self.cnt.items():
            if v > 0 and self.waited.get(("sp", k), 0) < v:
                final.append((k, v))
        nc = self.nc
        sems = self.sems
        ops = self.ops

        def run(e, name, fin=False):
            for waits, fn, (sk, inc) in ops[name]:
                for k, v in waits:
                    e.wait_ge(sems[k], v)
                ins = fn(e)
                ins.then_inc(sems[sk], inc)
            if fin:
                for k, v in final:
                    e.wait_ge(sems[k], v)

        with nc.Block() as block:
            @block.tensor
            def _(e):
                run(e, "pe")

            @block.vector
            def _(e):
                run(e, "dve")

            @block.scalar
            def _(e):
                run(e, "act")

            @block.gpsimd
            def _(e):
                run(e, "pool")

            @block.sync
            def _(e):
                run(e, "sp", fin=True)


D = 2048
T = 1024
KC = D // 128
EPS = 1e-6
INW = 15376
V_SH1, V_SC1, V_G1, V_SH2, V_SC2, V_G2, V_NMIX, V_NMLP, V_NFIN = [i * 16 for i in range(9)]
NV = 9 * 16


def new_prog():
    nc = bass.Bass("TRN2", target_bir_lowering=False)
    es = ExitStack()
    P = Prog(nc, es)
    P.wctr = 0
    P.psctr = 0
    return nc, es, P


def gemm(P, w_dram, K, NC, CB, rhs, evac, wbufs, psb, ntb=T // 512, k0=0):
    kcn = K // 128
    nblocks = (NC + CB - 1) // CB
    for cb in range(nblocks):
        c0 = cb * CB
        cw = min(CB, NC - c0)
        b = P.wctr % 2
        P.wctr += 1
        wt = wbufs[b][:, 0:kcn * CB].rearrange("p (kc c) -> p kc c", c=CB)
        src = w_dram[k0:k0 + K, c0:c0 + cw].rearrange("(kc p) c -> p kc c", p=128)
        P.dma("pool", lambda e, wt=wt, src=src, cw=cw: e.dma_start(out=wt[:, :, 0:cw], in_=src),
              writes=[("wt", b)])
        for ci in range((cw + 127) // 128):
            m = min(128, cw - ci * 128)
            ct = (c0 + ci * 128) // 128
            slot = P.psctr % 2
            P.psctr += 1
            for kc in range(kcn):
                for tb in range(ntb):
                    pst = psb[slot * ntb + tb]
                    rap, rkeys = rhs(kc, tb)
                    P.op("pe", lambda e, pst=pst, wt=wt, kc=kc, ci=ci, m=m, rap=rap, st=(kc == 0), sp=(kc == kcn - 1):
                         e.matmul(pst[0:m, :], wt[:, kc, ci * 128:ci * 128 + m], rap, start=st, stop=sp),
                         reads=[("wt", b)] + rkeys, writes=[("ps", slot * ntb + tb)])
            for tb in range(ntb):
                evac(ct, m, tb, psb[slot * ntb + tb], ("ps", slot * ntb + tb))


def rms_to_bf16(P, x32, vec, c_scale, c_shift, c_nw, hb, ones32, sq, rstd, pss, avec, final_out=None):
    for kc in range(KC):
        s = sq[kc % 2]
        P.op("act", lambda e, s=s, kc=kc: e.activation(out=s[:], in_=x32[:, kc, :], func=AF.Square),
             reads=[("x32", kc)], writes=[("sq", kc % 2)])
        for tb in range(T // 512):
            P.op("pe", lambda e, s=s, tb=tb, kc=kc: e.matmul(pss[tb][:], ones32[:], s[:, tb * 512:(tb + 1) * 512],
                                                            start=(kc == 0), stop=(kc == KC - 1)),
                 reads=[("sq", kc % 2), "ones32"], writes=[("pss", tb)])
    for tb in range(T // 512):
        P.op("act", lambda e, tb=tb: e.activation(out=rstd[:, tb * 512:(tb + 1) * 512], in_=pss[tb][:], func=AF.Sqrt,
                                                  bias=epsb[0][:], scale=1.0 / D),
             reads=[("pss", tb), "epsb"], writes=[("rstd", tb)])
        P.op("dve", lambda e, tb=tb: e.reciprocal(out=rstd[:, tb * 512:(tb + 1) * 512], in_=rstd[:, tb * 512:(tb + 1) * 512]),
             reads=[("rstd", tb)], writes=[("rstd", tb)])
    if c_scale is not None:
        P.op("dve", lambda e: e.scalar_tensor_tensor(out=avec[:], in0=vec[:, c_scale:c_scale + 16], scalar=1.0,
                                                     in1=vec[:, c_nw:c_nw + 16], op0=ALU.add, op1=ALU.mult),
             reads=["vec"], writes=["avec"])
    else:
        P.op("dve", lambda e: e.tensor_copy(out=avec[:], in_=vec[:, c_nw:c_nw + 16]), reads=["vec"], writes=["avec"])
    for kc in range(KC):
        s = sq[kc % 2]
        if final_out is None:
            P.op("dve", lambda e, s=s, kc=kc: e.scalar_tensor_tensor(out=s[:], in0=x32[:, kc, :], scalar=avec[:, kc:kc + 1],
                                                                     in1=rstd[:], op0=ALU.mult, op1=ALU.mult),
                 reads=[("x32", kc), "avec", ("rstd", 0), ("rstd", 1)], writes=[("sq", kc % 2)])
            P.op("act", lambda e, s=s, kc=kc: e.activation(out=hb[:, kc, :], in_=s[:], func=AF.Identity,
                                                           bias=vec[:, c_shift + kc:c_shift + kc + 1], scale=1.0),
                 reads=[("sq", kc % 2), "vec"], writes=[("hb", kc)])
        else:
            final_out(kc, s)


epsb = [None]


def consts(P, need_ident=False):
    ones32 = P.sb("ones32", [128, 128], F32)
    P.op("dve", lambda e: e.memset(ones32[:], 1.0), writes=["ones32"])
    eb = P.sb("epsb", [128, 1], F32)
    P.op("dve", lambda e: e.memset(eb[:], EPS), writes=["epsb"])
    epsb[0] = eb
    return ones32


def load_x(P, xT, x32):
    for kc in range(KC):
        P.dma("sp", lambda e, kc=kc: e.dma_start(out=x32[:, kc, :], in_=xT[kc * 128:(kc + 1) * 128, :]),
              writes=[("x32", kc)])


def build_pre():
    nc, es, P = new_prog()
    xT = nc.dram_tensor("xT", [D, T], F32, kind="ExternalInput").ap()
    vecd = nc.dram_tensor("vec", [128, NV], F32, kind="ExternalInput").ap()
    w_in = nc.dram_tensor("w_in", [D, INW], F32, kind="ExternalInput").ap()
    projT = nc.dram_tensor("projT", [INW, T], F32, kind="ExternalOutput").ap()
    with es:
        x32 = P.sb("x32", [128, KC, T], F32)
        hb = P.sb("hb", [128, KC, T], BF16)
        vec = P.sb("vec", [128, NV], F32)
        avec = P.sb("avec", [128, 16], F32)
        sq = [P.sb(f"sq{i}", [128, T], F32) for i in range(2)]
        rstd = P.sb("rstd", [128, T], F32)
        wbufs = [P.sb(f"wb{i}", [128, 8192], BF16) for i in range(2)]
        ob = [P.sb(f"ob{i}", [128, T], F32) for i in range(2)]
        pss = [P.ps(f"pss{i}", [128, 512]) for i in range(2)]
        psb = [P.ps(f"psb{i}", [128, 512]) for i in range(4)]
        ones32 = consts(P)
        P.dma("sp", lambda e: e.dma_start(out=vec[:], in_=vecd), writes=["vec"])
        load_x(P, xT, x32)
        rms_to_bf16(P, x32, vec, V_SC1, V_SH1, V_NMIX, hb, ones32, sq, rstd, pss, avec)

        def rhs(kc, tb):
            return hb[:, kc, tb * 512:(tb + 1) * 512], [("hb", kc)]

        def evac(ct, m, tb, pst, pkey):
            o = ob[ct % 2]
            eng = "dve" if tb == 0 else "act"
            if eng == "dve":
                P.op("dve", lambda e: e.tensor_copy(out=o[0:m, tb * 512:(tb + 1) * 512], in_=pst[0:m, :]),
                     reads=[pkey], writes=[("ob", ct % 2, tb)])
            else:
                P.op("act", lambda e: e.activation(out=o[0:m, tb * 512:(tb + 1) * 512], in_=pst[0:m, :], func=AF.Identity),
                     reads=[pkey], writes=[("ob", ct % 2, tb)])
            if tb == T // 512 - 1:
                P.dma("sp", lambda e: e.dma_start(out=projT[ct * 128:ct * 128 + m, :], in_=o[0:m, :]),
                      reads=[("ob", ct % 2, 0), ("ob", ct % 2, 1)])
        gemm(P, w_in, D, INW, 512, rhs, evac, wbufs, psb)
        P.emit()
    return nc


def build_merge():
    nc, es, P = new_prog()
    xT = nc.dram_tensor("xT", [D, T], F32, kind="ExternalInput").ap()
    vecd = nc.dram_tensor("vec", [128, NV], F32, kind="ExternalInput").ap()
    oT = [nc.dram_tensor(n, [1024, T], F32, kind="ExternalInput").ap() for n in ("oaT", "obT", "ocT")]
    gT = nc.dram_tensor("gT", [3 * D, T], F32, kind="ExternalInput").ap()
    wbr = [nc.dram_tensor(n, [1024, D], F32, kind="ExternalInput").ap() for n in ("wba", "wbb", "wbc")]
    w_out = nc.dram_tensor("w_out", [D, D], F32, kind="ExternalInput").ap()
    x1T = nc.dram_tensor("x1T", [D, T], F32, kind="ExternalOutput").ap()
    with es:
        ob3 = [P.sb(f"o3_{i}", [128, 8, T], BF16) for i in range(3)]
        mb = P.sb("mb", [128, KC, T], BF16)
        vec = P.sb("vec", [128, NV], F32)
        wbufs = [P.sb(f"wb{i}", [128, 8192], BF16) for i in range(2)]
        wbt = [[P.sb(f"wbt{j}_{i}", [128, 8, 128], BF16) for i in range(3)] for j in range(2)]
        gt = [[P.sb(f"gt{j}_{i}", [128, T], F32) for i in range(3)] for j in range(2)]
        acc = [P.sb(f"acc{i}", [128, 512], F32) for i in range(2)]
        tmp = [P.sb(f"tmp{i}", [128, 512], F32) for i in range(2)]
        xt = [P.sb(f"xt{i}", [128, T], F32) for i in range(2)]
        pb = [P.ps(f"pb{i}", [128, 512]) for i in range(8)]
        P.dma("sp", lambda e: e.dma_start(out=vec[:], in_=vecd), writes=["vec"])
        for i in range(3):
            for kc in range(8):
                P.dma("pool", lambda e, i=i, kc=kc: e.dma_start(out=ob3[i][:, kc, :], in_=oT[i][kc * 128:(kc + 1) * 128, :]),
                      writes=[("o3", i, kc)])
        for ct in range(16):
            j = ct % 2
            for i in range(3):
                P.dma("pool", lambda e, i=i, j=j, ct=ct: e.dma_start(
                    out=wbt[j][i][:], in_=wbr[i][:, ct * 128:(ct + 1) * 128].rearrange("(kc p) c -> p kc c", p=128)),
                    writes=[("wbt", j, i)])
                P.dma("sp", lambda e, i=i, j=j, ct=ct: e.dma_start(out=gt[j][i][:], in_=gT[i * D + ct * 128:i * D + (ct + 1) * 128, :]),
                      writes=[("gt", j, i)])
                P.op("act", lambda e, i=i, j=j: e.activation(out=gt[j][i][:], in_=gt[j][i][:], func=AF.Sigmoid),
                     reads=[("gt", j, i)], writes=[("gt", j, i)])
            for i in range(3):
                for kc in range(8):
                    for tb in range(2):
                        P.op("pe", lambda e, i=i, j=j, kc=kc, tb=tb: e.matmul(
                            pb[i * 2 + tb][:], wbt[j][i][:, kc, :], ob3[i][:, kc, tb * 512:(tb + 1) * 512],
                            start=(kc == 0), stop=(kc == 7)),
                            reads=[("wbt", j, i), ("o3", i, kc)], writes=[("ps", i * 2 + tb)])
            for tb in range(2):
                sl = slice(tb * 512, (tb + 1) * 512)
                P.op("dve", lambda e, tb=tb, sl=sl, j=j: e.tensor_tensor(out=acc[tb][:], in0=pb[tb][:], in1=gt[j][0][:, sl], op=ALU.mult),
                     reads=[("ps", tb), ("gt", j, 0)], writes=[("acc", tb)])
                P.op("dve", lambda e, tb=tb, sl=sl, j=j: e.tensor_tensor(out=tmp[tb][:], in0=pb[2 + tb][:], in1=gt[j][1][:, sl], op=ALU.mult),
                     reads=[("ps", 2 + tb), ("gt", j, 1)], writes=[("tmp", tb)])
                P.op("dve", lambda e, tb=tb: e.tensor_tensor(out=acc[tb][:], in0=acc[tb][:], in1=tmp[tb][:], op=ALU.add),
                     reads=[("acc", tb), ("tmp", tb)], writes=[("acc", tb)])
                P.op("dve", lambda e, tb=tb, sl=sl, j=j: e.tensor_tensor(out=tmp[tb][:], in0=pb[4 + tb][:], in1=gt[j][2][:, sl], op=ALU.mult),
                     reads=[("ps", 4 + tb), ("gt", j, 2)], writes=[("tmp", tb)])
                P.op("dve", lambda e, tb=tb, sl=sl, ct=ct: e.tensor_tensor(out=mb[:, ct, sl], in0=acc[tb][:], in1=tmp[tb][:], op=ALU.add),
                     reads=[("acc", tb), ("tmp", tb)], writes=[("mb", ct)])

        def rhs2(kc, tb):
            return mb[:, kc, tb * 512:(tb + 1) * 512], [("mb", kc)]

        def evac2(ct, m, tb, pst, pkey):
            j = ct % 2
            sl = slice(tb * 512, (tb + 1) * 512)
            if tb == 0:
                P.dma("sp", lambda e: e.dma_start(out=xt[j][:], in_=xT[ct * 128:(ct + 1) * 128, :]), writes=[("xt", j)])
            P.op("dve", lambda e: e.scalar_tensor_tensor(out=xt[j][:, sl], in0=pst[:], scalar=vec[:, V_G1 + ct:V_G1 + ct + 1],
                                                         in1=xt[j][:, sl], op0=ALU.mult, op1=ALU.add),
                 reads=[pkey, ("xt", j), "vec"], writes=[("xt", j)])
            if tb == 1:
                P.dma("sp", lambda e: e.dma_start(out=x1T[ct * 128:(ct + 1) * 128, :], in_=xt[j][:]), reads=[("xt", j)])
        gemm(P, w_out, D, D, 512, rhs2, evac2, wbufs, pb[0:4])
        P.emit()
    return nc


def build_mlp(final):
    nc, es, P = new_prog()
    DFF = 4 * D
    xT = nc.dram_tensor("xT", [D, T], F32, kind="ExternalInput").ap()
    vecd = nc.dram_tensor("vec", [128, NV], F32, kind="ExternalInput").ap()
    w1 = nc.dram_tensor("w1", [D, DFF], F32, kind="ExternalInput").ap()
    w2 = nc.dram_tensor("w2", [DFF, D], F32, kind="ExternalInput").ap()
    x2T = nc.dram_tensor("x2T", [D, T], F32, kind="ExternalOutput").ap()
    with es:
        x32 = P.sb("x32", [128, KC, T], F32)
        hb = P.sb("hb", [128, KC, T], BF16)
        hid = P.sb("hid", [128, 16, T], BF16)
        vec = P.sb("vec", [128, NV], F32)
        avec = P.sb("avec", [128, 16], F32)
        sq = [P.sb(f"sq{i}", [128, T], F32) for i in range(2)]
        rstd = P.sb("rstd", [128, T], F32)
        wbufs = [P.sb(f"wb{i}", [128, 8192], BF16) for i in range(2)]
        rl = [P.sb(f"rl{i}", [128, 512], F32) for i in range(2)]
        pss = [P.ps(f"pss{i}", [128, 512]) for i in range(2)]
        psb = [P.ps(f"psb{i}", [128, 512]) for i in range(4)]
        ones32 = consts(P)
        P.dma("sp", lambda e: e.dma_start(out=vec[:], in_=vecd), writes=["vec"])
        load_x(P, xT, x32)
        rms_to_bf16(P, x32, vec, V_SC2, V_SH2, V_NMLP, hb, ones32, sq, rstd, pss, avec)
        rctr = [0]
        for q in range(4):
            def rhs(kc, tb):
                return hb[:, kc, tb * 512:(tb + 1) * 512], [("hb", kc)]

            def evac(ct, m, tb, pst, pkey, q=q):
                r = rctr[0] % 2
                rctr[0] += 1
                cl = ct
                P.op("act", lambda e: e.activation(out=rl[r][:], in_=pst[:], func=AF.Relu), reads=[pkey], writes=[("rl", r)])
                P.op("dve", lambda e: e.tensor_tensor(out=hid[:, cl, tb * 512:(tb + 1) * 512], in0=rl[r][:], in1=rl[r][:], op=ALU.mult),
                     reads=[("rl", r)], writes=[("hid", cl)])
            gemm(P, w1[:, q * 2048:(q + 1) * 2048], D, 2048, 512, rhs, evac, wbufs, psb)

            def rhs2(kc, tb):
                return hid[:, kc, tb * 512:(tb + 1) * 512], [("hid", kc)]

            def evac2(ct, m, tb, pst, pkey):
                sl = slice(tb * 512, (tb + 1) * 512)
                P.op("dve", lambda e: e.scalar_tensor_tensor(out=x32[:, ct, sl], in0=pst[:], scalar=vec[:, V_G2 + ct:V_G2 + ct + 1],
                                                             in1=x32[:, ct, sl], op0=ALU.mult, op1=ALU.add),
                     reads=[pkey, ("x32", ct), "vec"], writes=[("x32", ct)])
            gemm(P, w2, 2048, D, 512, rhs2, evac2, wbufs, psb, k0=q * 2048)
        if not final:
            for kc in range(KC):
                P.dma("sp", lambda e, kc=kc: e.dma_start(out=x2T[kc * 128:(kc + 1) * 128, :], in_=x32[:, kc, :]),
                      reads=[("x32", kc)])
        else:
            def final_out(kc, s):
                P.op("dve", lambda e: e.scalar_tensor_tensor(out=s[:], in0=x32[:, kc, :], scalar=avec[:, kc:kc + 1],
                                                             in1=rstd[:], op0=ALU.mult, op1=ALU.mult),
                     reads=[("x32", kc), "avec", ("rstd", 0), ("rstd", 1)], writes=[("sq", kc % 2)])
                P.dma("sp", lambda e: e.dma_start(out=x2T[kc * 128:(kc + 1) * 128, :], in_=s[:]), reads=[("sq", kc % 2)])
            rms_to_bf16(P, x32, vec, None, None, V_NFIN, None, ones32, sq, rstd, pss, avec, final_out=final_out)
        P.emit()
    return nc


def build_convc():
    nc, es, P = new_prog()
    TH = T + 32
    cgT = nc.dram_tensor("cgT", [2048, TH], F32, kind="ExternalInput").ap()
    cwd = nc.dram_tensor("cw", [128, 8 * 31], F32, kind="ExternalInput").ap()
    cvd = nc.dram_tensor("cvec", [128, 24], F32, kind="ExternalInput").ap()
    ocT = nc.dram_tensor("ocT", [1024, T], F32, kind="ExternalOutput").ap()
    with es:
        at = [P.sb(f"at{i}", [128, TH], F32) for i in range(2)]
        bt = [P.sb(f"bt{i}", [128, TH], F32) for i in range(2)]
        cv = P.sb("cv", [128, 8, T], F32)
        cw = P.sb("cw", [128, 8 * 31], F32)
        cvec = P.sb("cvec", [128, 24], F32)
        sq = [P.sb(f"sq{i}", [128, T], F32) for i in range(2)]
        mean = P.sb("mean", [128, T], F32)
        rstd = P.sb("rstd", [128, T], F32)
        eps5 = P.sb("eps5", [128, 1], F32)
        ps1 = [P.ps(f"ps1_{i}", [128, 512]) for i in range(2)]
        ps2 = [P.ps(f"ps2_{i}", [128, 512]) for i in range(2)]
        ones32 = consts(P)
        P.op("dve", lambda e: e.memset(eps5[:], 1e-5), writes=["eps5"])
        P.dma("sp", lambda e: e.dma_start(out=cw[:], in_=cwd), writes=["cw"])
        P.dma("sp", lambda e: e.dma_start(out=cvec[:], in_=cvd), writes=["cvec"])
        for ch in range(8):
            j = ch % 2
            P.dma("sp", lambda e, ch=ch, j=j: e.dma_start(out=at[j][:], in_=cgT[ch * 128:(ch + 1) * 128, :]), writes=[("at", j)])
            P.dma("sp", lambda e, ch=ch, j=j: e.dma_start(out=bt[j][:], in_=cgT[1024 + ch * 128:1024 + (ch + 1) * 128, :]), writes=[("bt", j)])
            P.op("act", lambda e, j=j: e.activation(out=bt[j][:], in_=bt[j][:], func=AF.Sigmoid), reads=[("bt", j)], writes=[("bt", j)])
            P.op("pool", lambda e, j=j: e.tensor_tensor(out=at[j][:], in0=at[j][:], in1=bt[j][:], op=ALU.mult),
                 reads=[("at", j), ("bt", j)], writes=[("at", j)])
            P.op("dve", lambda e, ch=ch, j=j: e.tensor_scalar(out=cv[:, ch, :], in0=at[j][:, 2:2 + T], scalar1=cw[:, ch * 31:ch * 31 + 1],
                                                              scalar2=cvec[:, ch:ch + 1], op0=ALU.mult, op1=ALU.add),
                 reads=[("at", j), "cw", "cvec"], writes=[("cv", ch)])
            for k in range(1, 31):
                P.op("dve", lambda e, ch=ch, j=j, k=k: e.scalar_tensor_tensor(
                    out=cv[:, ch, :], in0=at[j][:, 2 + k:2 + k + T], scalar=cw[:, ch * 31 + k:ch * 31 + k + 1],
                    in1=cv[:, ch, :], op0=ALU.mult, op1=ALU.add),
                    reads=[("at", j), "cw", ("cv", ch)], writes=[("cv", ch)])
            s = sq[j]
            P.op("act", lambda e, s=s, ch=ch: e.activation(out=s[:], in_=cv[:, ch, :], func=AF.Square),
                 reads=[("cv", ch)], writes=[("sq", j)])
            for tb in range(2):
                P.op("pe", lambda e, tb=tb, ch=ch: e.matmul(ps1[tb][:], ones32[:], cv[:, ch, tb * 512:(tb + 1) * 512],
                                                            start=(ch == 0), stop=(ch == 7)),
                     reads=[("cv", ch), "ones32"], writes=[("ps1", tb)])
                P.op("pe", lambda e, tb=tb, ch=ch, s=s: e.matmul(ps2[tb][:], ones32[:], s[:, tb * 512:(tb + 1) * 512],
                                                                 start=(ch == 0), stop=(ch == 7)),
                     reads=[("sq", j), "ones32"], writes=[("ps2", tb)])
        for tb in range(2):
            sl = slice(tb * 512, (tb + 1) * 512)
            P.op("act", lambda e, tb=tb, sl=sl: e.activation(out=mean[:, sl], in_=ps1[tb][:], func=AF.Identity, scale=1.0 / 1024),
                 reads=[("ps1", tb)], writes=[("mean", tb)])
            P.op("dve", lambda e, tb=tb, sl=sl: e.tensor_tensor(out=rstd[:, sl], in0=mean[:, sl], in1=mean[:, sl], op=ALU.mult),
                 reads=[("mean", tb)], writes=[("rstd", tb)])
            P.op("dve", lambda e, tb=tb, sl=sl: e.scalar_tensor_tensor(out=rstd[:, sl], in0=ps2[tb][:], scalar=1.0 / 1024, in1=rstd[:, sl],
                                                                       op0=ALU.mult, op1=ALU.subtract),
                 reads=[("ps2", tb), ("rstd", tb)], writes=[("rstd", tb)])
            P.op("act", lambda e, tb=tb, sl=sl: e.activation(out=rstd[:, sl], in_=rstd[:, sl], func=AF.Sqrt, bias=eps5[:], scale=1.0),
                 reads=[("rstd", tb), "eps5"], writes=[("rstd", tb)])
            P.op("dve", lambda e, tb=tb, sl=sl: e.reciprocal(out=rstd[:, sl], in_=rstd[:, sl]),
                 reads=[("rstd", tb)], writes=[("rstd", tb)])
        for ch in range(8):
            s = sq[ch % 2]
            P.op("dve", lambda e, ch=ch, s=s: e.tensor_tensor(out=s[:], in0=cv[:, ch, :], in1=mean[:], op=ALU.subtract),
                 reads=[("cv", ch), ("mean", 0), ("mean", 1)], writes=[("sq", ch % 2)])
            P.op("pool", lambda e, ch=ch, s=s: e.tensor_tensor(out=s[:], in0=s[:], in1=rstd[:], op=ALU.mult),
                 reads=[("sq", ch % 2), ("rstd", 0), ("rstd", 1)], writes=[("sq", ch % 2)])
            P.op("act", lambda e, ch=ch, s=s: e.activation(out=s[:], in_=s[:], func=AF.Silu, bias=cvec[:, 16 + ch:17 + ch],
                                                           scale=cvec[:, 8 + ch:9 + ch]),
                 reads=[("sq", ch % 2), "cvec"], writes=[("sq", ch % 2)])
            P.dma("sp", lambda e, ch=ch, s=s: e.dma_start(out=ocT[ch * 128:(ch + 1) * 128, :], in_=s[:]), reads=[("sq", ch % 2)])
        P.emit()
    return nc


def build_retln():
    nc, es, P = new_prog()
    oretT = nc.dram_tensor("oretT", [1024, T], F32, kind="ExternalInput").ap()
    bgT = nc.dram_tensor("bgT", [1024, T], F32, kind="ExternalInput").ap()
    obT = nc.dram_tensor("obT", [1024, T], F32, kind="ExternalOutput").ap()
    with es:
        xt = [[P.sb(f"xt{j}_{c}", [128, T], F32) for c in range(2)] for j in range(2)]
        gt = [P.sb(f"gt{i}", [128, T], F32) for i in range(2)]
        sq = [P.sb(f"sq{i}", [128, T], F32) for i in range(2)]
        mean = P.sb("mean", [128, T], F32)
        rstd = P.sb("rstd", [128, T], F32)
        eps5 = P.sb("eps5", [128, 1], F32)
        ps1 = [P.ps(f"ps1_{i}", [128, 512]) for i in range(2)]
        ps2 = [P.ps(f"ps2_{i}", [128, 512]) for i in range(2)]
        ones32 = consts(P)
        P.op("dve", lambda e: e.memset(eps5[:], 1e-5), writes=["eps5"])
        gi = 0
        for h in range(4):
            j = h % 2
            for c in range(2):
                P.dma("sp", lambda e, h=h, c=c, j=j: e.dma_start(out=xt[j][c][:], in_=oretT[(2 * h + c) * 128:(2 * h + c + 1) * 128, :]),
                      writes=[("xt", j, c)])
                P.op("act", lambda e, j=j, c=c: e.activation(out=sq[c][:], in_=xt[j][c][:], func=AF.Square),
                     reads=[("xt", j, c)], writes=[("sq", c)])
                for tb in range(2):
                    sl = slice(tb * 512, (tb + 1) * 512)
                    P.op("pe", lambda e, j=j, c=c, tb=tb, sl=sl: e.matmul(ps1[tb][:], ones32[:], xt[j][c][:, sl], start=(c == 0), stop=(c == 1)),
                         reads=[("xt", j, c), "ones32"], writes=[("ps1", tb)])
                    P.op("pe", lambda e, c=c, tb=tb, sl=sl: e.matmul(ps2[tb][:], ones32[:], sq[c][:, sl], start=(c == 0), stop=(c == 1)),
                         reads=[("sq", c), "ones32"], writes=[("ps2", tb)])
            for tb in range(2):
                sl = slice(tb * 512, (tb + 1) * 512)
                P.op("act", lambda e, tb=tb, sl=sl: e.activation(out=mean[:, sl], in_=ps1[tb][:], func=AF.Identity, scale=1.0 / 256),
                     reads=[("ps1", tb)], writes=[("mean", tb)])
                P.op("dve", lambda e, tb=tb, sl=sl: e.tensor_tensor(out=rstd[:, sl], in0=mean[:, sl], in1=mean[:, sl], op=ALU.mult),
                     reads=[("mean", tb)], writes=[("rstd", tb)])
                P.op("dve", lambda e, tb=tb, sl=sl: e.scalar_tensor_tensor(out=rstd[:, sl], in0=ps2[tb][:], scalar=1.0 / 256, in1=rstd[:, sl],
                                                                           op0=ALU.mult, op1=ALU.subtract),
                     reads=[("ps2", tb), ("rstd", tb)], writes=[("rstd", tb)])
                P.op("act", lambda e, tb=tb, sl=sl: e.activation(out=rstd[:, sl], in_=rstd[:, sl], func=AF.Sqrt, bias=eps5[:], scale=1.0),
                     reads=[("rstd", tb), "eps5"], writes=[("rstd", tb)])
                P.op("dve", lambda e, tb=tb, sl=sl: e.reciprocal(out=rstd[:, sl], in_=rstd[:, sl]),
                     reads=[("rstd", tb)], writes=[("rstd", tb)])
            mk = [("mean", 0), ("mean", 1)]
            rk = [("rstd", 0), ("rstd", 1)]
            for c in range(2):
                g = gt[gi % 2]
                gk = ("gt", gi % 2)
                gi += 1
                P.dma("sp", lambda e, h=h, c=c, g=g: e.dma_start(out=g[:], in_=bgT[(2 * h + c) * 128:(2 * h + c + 1) * 128, :]), writes=[gk])
                P.op("act", lambda e, g=g: e.activation(out=g[:], in_=g[:], func=AF.Silu), reads=[gk], writes=[gk])
                x = xt[j][c]
                P.op("dve", lambda e, x=x: e.tensor_tensor(out=x[:], in0=x[:], in1=mean[:], op=ALU.subtract),
                     reads=[("xt", j, c)] + mk, writes=[("xt", j, c)])
                P.op("pool", lambda e, x=x: e.tensor_tensor(out=x[:], in0=x[:], in1=rstd[:], op=ALU.mult),
                     reads=[("xt", j, c)] + rk, writes=[("xt", j, c)])
                P.op("dve", lambda e, x=x, g=g: e.tensor_tensor(out=x[:], in0=x[:], in1=g[:], op=ALU.mult),
                     reads=[("xt", j, c), gk], writes=[("xt", j, c)])
                P.dma("sp", lambda e, h=h, c=c, x=x: e.dma_start(out=obT[(2 * h + c) * 128:(2 * h + c + 1) * 128, :], in_=x[:]),
                      reads=[("xt", j, c)])
        P.emit()
    return nc


def build_ada():
    nc, es, P = new_prog()
    NCOL = 1536
    cTd = nc.dram_tensor("cT", [128, 16], F32, kind="ExternalInput").ap()
    wad = nc.dram_tensor("wa", [2, D, NCOL], F32, kind="ExternalInput").ap()
    bad = nc.dram_tensor("ba", [2, NCOL], F32, kind="ExternalInput").ap()
    modd = nc.dram_tensor("mod", [2, NCOL], F32, kind="ExternalOutput").ap()
    with es:
        ca = P.sb("ca", [128, 16], F32)
        wt = [P.sb(f"wt{i}", [128, 16, 256], F32) for i in range(2)]
        bt = P.sb("bt", [1, 2 * NCOL], F32)
        ot = P.sb("ot", [1, 2 * NCOL], F32)
        ps = [P.ps(f"ps{i}", [128, 512]) for i in range(2)]
        P.dma("sp", lambda e: e.dma_start(out=ca[:], in_=cTd), writes=["ca"])
        P.op("act", lambda e: e.activation(out=ca[:], in_=ca[:], func=AF.Silu), reads=["ca"], writes=["ca"])
        for l in range(2):
            P.dma("sp", lambda e, l=l: e.dma_start(out=bt[0:1, l * NCOL:(l + 1) * NCOL], in_=bad[l:l + 1, :]), writes=[("bt", l)])
        i = 0
        for l in range(2):
            for cb in range(NCOL // 256):
                b = i % 2
                i += 1
                P.dma("sp", lambda e, l=l, cb=cb, b=b: e.dma_start(
                    out=wt[b][:], in_=wad[l, :, cb * 256:(cb + 1) * 256].rearrange("(kc p) c -> p kc c", p=128)), writes=[("wt", b)])
                for kc in range(16):
                    P.op("pe", lambda e, b=b, kc=kc: e.matmul(ps[b][0:1, 0:256], ca[:, kc:kc + 1], wt[b][:, kc, :], start=(kc == 0), stop=(kc == 15)),
                         reads=["ca", ("wt", b)], writes=[("ps", b)])
                o0 = l * NCOL + cb * 256
                P.op("dve", lambda e, b=b, o0=o0: e.tensor_tensor(out=ot[0:1, o0:o0 + 256], in0=ps[b][0:1, 0:256], in1=bt[0:1, o0:o0 + 256], op=ALU.add),
                     reads=[("ps", b), ("bt", l)], writes=[("ot", l)])
        for l in range(2):
            P.dma("sp", lambda e, l=l: e.dma_start(out=modd[l:l + 1, :], in_=ot[0:1, l * NCOL:(l + 1) * NCOL]), reads=[("ot", l)])
        P.emit()
    return nc


TA = 8192
NPAIR = TA // 128
NEG = -30000.0
C_ID, C_TRI, C_BLK, C_SEL0, C_SEL1, C_MU, C_MLS, C_OFFD, C_ONES = range(9)
NCONST = 9
RING = 4


def gdn_consts():
    p = np.arange(128)[:, None]
    f = np.arange(128)[None, :]
    same = (p // 64) == (f // 64)
    c = np.zeros((NCONST, 128, 128), np.float32)
    c[C_ID] = (p == f)
    c[C_TRI] = same & (p <= f)
    c[C_BLK] = same
    c[C_SEL0] = (p < 64) & (f >= 0)
    c[C_SEL1] = (p >= 64) & (f >= 0)
    c[C_MU] = np.where(same & (p <= f), 0.0, NEG)
    c[C_MLS] = np.where(same & (p > f), 0.0, NEG)
    c[C_OFFD] = (p != f)
    c[C_ONES] = 1.0
    return np.ascontiguousarray(c.transpose(1, 0, 2).reshape(128, NCONST * 128))


def build_gdn(stage=9, npair=NPAIR):
    nc = bass.Bass("TRN2", target_bir_lowering=False)
    es = ExitStack()
    P = Prog(nc, es)
    qd = nc.dram_tensor("qT", [128, TA + 4], F32, kind="ExternalInput").ap()
    kd = nc.dram_tensor("kT", [128, TA + 4], F32, kind="ExternalInput").ap()
    vd = nc.dram_tensor("vT", [128, TA + 4], F32, kind="ExternalInput").ap()
    zd = nc.dram_tensor("zT", [128, TA], F32, kind="ExternalInput").ap()
    browd = nc.dram_tensor("brow", [128, TA], F32, kind="ExternalInput").ap()
    btokd = nc.dram_tensor("btok", [128, NPAIR], F32, kind="ExternalInput").ap()
    atokd = nc.dram_tensor("atok", [128, NPAIR], F32, kind="ExternalInput").ap()
    cwvd = nc.dram_tensor("cwv", [128, 12], F32, kind="ExternalInput").ap()
    hvd = nc.dram_tensor("hv", [128, 4], F32, kind="ExternalInput").ap()
    cstd = nc.dram_tensor("cst", [128, NCONST * 128], F32, kind="ExternalInput").ap()
    outd = nc.dram_tensor("oaT", [128, TA], F32, kind="ExternalOutput").ap()
    with es:
        cst = P.sb("cst", [128, NCONST * 128], F32)

        def C(i):
            return cst[:, i * 128:(i + 1) * 128]
        identb = P.sb("identb", [128, 128], BF16)
        qnb = P.sb("qnb", [128, TA], BF16)
        knb = P.sb("knb", [128, TA], BF16)
        kbb = P.sb("kbb", [128, TA], BF16)
        vnb = P.sb("vnb", [128, TA], BF16)
        oT = P.sb("oT", [128, TA], F32)
        NB = 1024
        raw = [P.sb(f"raw{i}", [128, NB + 4], F32) for i in range(2)]
        acc = [P.sb(f"acc{i}", [128, NB], F32) for i in range(2)]
        sqs = P.sb("sqs", [128, NB], F32)
        rn = P.sb("rn", [128, NB], F32)
        bsb = P.sb("bsb", [128, NB], F32)
        cwv = P.sb("cwv", [128, 12], F32)
        hv = P.sb("hv", [128, 4], F32)
        small = P.sb("small", [128, 8], F32)
        btok = P.sb("btok", [128, NPAIR], F32)
        atok = P.sb("atok", [128, NPAIR], F32)
        gt = P.sb("gt", [128, NPAIR], F32)
        gc = P.sb("gc", [128, NPAIR], F32)
        edl = P.sb("edl", [128, NPAIR], F32)
        bgc = P.sb("bgc", [128, NPAIR], F32)
        glb = [P.sb(f"glb{h}", [128, NPAIR], F32) for h in range(2)]
        dgt = [P.sb(f"dgt{i}", [128, 128], F32) for i in range(3)]
        tU = [P.sb(f"tU{i}", [128, 128], F32) for i in range(3)]
        tL = [P.sb(f"tL{i}", [128, 128], F32) for i in range(3)]
        EUs = [P.sb(f"EUs{i}", [128, 128], F32) for i in range(3)]
        egc = [P.sb(f"egc{i}", [128, 128], F32) for i in range(3)]
        Lb = [P.sb(f"Lb{i}", [128, 128], BF16) for i in range(6)]
        Ub = [P.sb(f"Ub{i}", [128, 128], BF16) for i in range(6)]
        TTb = [P.sb(f"TTb{i}", [128, 128], BF16) for i in range(6)]
        vb = [P.sb(f"vb{i}", [128, 128], BF16) for i in range(3)]
        kbg = [P.sb(f"kbg{i}", [128, 128], BF16) for i in range(3)]
        attnT = [P.sb(f"attnT{i}", [128, 128], BF16) for i in range(RING)]
        qdec = [P.sb(f"qdec{i}", [128, 128], BF16) for i in range(RING)]
        kdec = [P.sb(f"kdec{i}", [128, 128], BF16) for i in range(RING)]
        ub = [P.sb(f"ub{i}", [128, 128], BF16) for i in range(RING)]
        At = [[P.sb(f"At{i}_{h}", [128, 128], BF16) for h in range(2)] for i in range(RING)]
        wtok = [P.sb(f"wtok{i}", [128, 128], BF16) for i in range(3)]
        S32 = P.sb("S32", [128, 128], F32)
        Sb = P.sb("Sb", [128, 128], BF16)
        bk = [P.ps(f"bk{i}", [128, 512]) for i in range(8)]

        def q4(b, i):
            return bk[b][:, i * 128:(i + 1) * 128]
        pss = [bk[0], bk[1]]
        PDGs = [q4(2 * j, 0) for j in range(3)]
        PKKs = [[q4(2 * j, 1), q4(2 * j, 2), q4(2 * j, 3)] for j in range(3)]
        PTRs = [bk[2 * j + 1][:, 0:256] for j in range(3)]
        PINVs = [[q4(2 * j + 1, 2), q4(2 * j + 1, 3), q4(2 * j, 0)] for j in range(3)]
        PUWs = [[q4(2 * j + 1, 0), q4(2 * j + 1, 1)] for j in range(3)]
        PATs = [[q4(2 * j, 1), q4(2 * j + 1, 2)] for j in range(3)]
        PQPs = [q4(2 * j, 2) for j in range(3)]
        PWS = [q4(6, 0), q4(6, 1)]
        PDS = [q4(6, 2), q4(6, 3)]
        POT = [q4(7, 0), q4(7, 1)]
        PSET = [q4(1, 3), q4(3, 3), q4(5, 3)]
        PSETK = [("bk", 1), ("bk", 3), ("bk", 5)]

        P.dma("sp", lambda e: e.dma_start(out=cst[:], in_=cstd), writes=["cst"])
        P.dma("sp", lambda e: e.dma_start(out=cwv[:], in_=cwvd), writes=["cwv"])
        P.dma("sp", lambda e: e.dma_start(out=hv[:], in_=hvd), writes=["hv"])
        P.dma("sp", lambda e: e.dma_start(out=btok[:], in_=btokd), writes=["btok"])
        P.dma("sp", lambda e: e.dma_start(out=atok[:], in_=atokd), writes=["atok"])
        P.op("dve", lambda e: e.memset(small[:, 0:1], 1e-6), writes=["small"])
        P.op("dve", lambda e: e.memset(small[:, 1:2], 1.0), reads=["small"], writes=["small"])
        P.op("dve", lambda e: e.memset(S32[:], 0.0), writes=["S32"])
        P.op("dve", lambda e: e.memset(Sb[:], 0.0), writes=["Sb"])
        P.op("dve", lambda e: e.tensor_copy(out=identb[:], in_=C(C_ID)), reads=["cst"], writes=["identb"])
        if stage >= 0:
            P.op("act", lambda e: e.activation(out=btok[:], in_=btok[:], func=AF.Sigmoid), reads=["btok"], writes=["btok"])
            P.op("act", lambda e: e.activation(out=small[:, 2:3], in_=hv[:, 0:1], func=AF.Exp), reads=["hv", "small"], writes=["small"])
            P.op("dve", lambda e: e.tensor_scalar(out=small[:, 2:3], in0=small[:, 2:3], scalar1=-1.0, scalar2=None, op0=ALU.mult),
                 reads=["small"], writes=["small"])
            P.op("act", lambda e: e.activation(out=atok[:], in_=atok[:], func=AF.Exp, bias=hv[:, 1:2], scale=1.0),
                 reads=["atok", "hv"], writes=["atok"])
            P.op("act", lambda e: e.activation(out=atok[:], in_=atok[:], func=AF.Ln, bias=small[:, 1:2], scale=1.0),
                 reads=["atok", "small"], writes=["atok"])
            P.op("dve", lambda e: e.tensor_scalar(out=gt[:], in0=atok[:], scalar1=small[:, 2:3], scalar2=None, op0=ALU.mult),
                 reads=["atok", "small"], writes=["gt"])
            P.op("pe", lambda e: e.matmul(PSET[0][:, 0:NPAIR], C(C_TRI), gt[:], start=True, stop=True), reads=["cst", "gt"], writes=[PSETK[0]])
            P.op("pe", lambda e: e.matmul(PSET[1][:, 0:NPAIR], C(C_BLK), gt[:], start=True, stop=True), reads=["cst", "gt"], writes=[PSETK[1]])
            P.op("dve", lambda e: e.tensor_copy(out=gc[:], in_=PSET[0][:, 0:NPAIR]), reads=[PSETK[0]], writes=["gc"])
            P.op("dve", lambda e: e.tensor_tensor(out=edl[:], in0=PSET[1][:, 0:NPAIR], in1=gc[:], op=ALU.subtract),
                 reads=[PSETK[1], "gc"], writes=["edl"])
            P.op("act", lambda e: e.activation(out=edl[:], in_=edl[:], func=AF.Exp), reads=["edl"], writes=["edl"])
            P.op("act", lambda e: e.activation(out=bgc[:], in_=gc[:], func=AF.Exp), reads=["gc"], writes=["bgc"])
            P.op("dve", lambda e: e.tensor_tensor(out=bgc[:], in0=bgc[:], in1=btok[:], op=ALU.mult), reads=["bgc", "btok"], writes=["bgc"])
            for h in range(2):
                P.op("pe", lambda e, h=h: e.matmul(PSET[2][:, 0:NPAIR], C(C_SEL0 + h), gt[:], start=True, stop=True),
                     reads=["cst", "gt"], writes=[PSETK[2]])
                P.op("act", lambda e, h=h: e.activation(out=glb[h][:], in_=PSET[2][:, 0:NPAIR], func=AF.Exp),
                     reads=[PSETK[2]], writes=[("glb", h)])


        rctr = [0]
        for b in range(npair * 128 // NB if stage >= 1 else 0):
            bs = slice(b * NB, (b + 1) * NB)
            P.dma("sp", lambda e, bs=bs: e.dma_start(out=bsb[:], in_=browd[:, bs]), writes=["bsb"])
            P.op("act", lambda e: e.activation(out=bsb[:], in_=bsb[:], func=AF.Sigmoid), reads=["bsb"], writes=["bsb"])
            for i, src in enumerate((qd, kd, vd)):
                j = rctr[0] % 2
                rctr[0] += 1
                P.dma("sp", lambda e, src=src, j=j, b=b: e.dma_start(out=raw[j][:], in_=src[:, b * NB:b * NB + NB + 4]),
                      writes=[("raw", j)])
                a = acc[j]
                P.op("dve", lambda e, a=a, j=j, i=i: e.tensor_scalar(out=a[:], in0=raw[j][:, 1:1 + NB], scalar1=cwv[:, i * 4:i * 4 + 1],
                                                                    scalar2=None, op0=ALU.mult),
                     reads=[("raw", j), "cwv"], writes=[("acc", j)])
                for k in range(1, 4):
                    P.op("dve", lambda e, a=a, j=j, i=i, k=k: e.scalar_tensor_tensor(
                        out=a[:], in0=raw[j][:, 1 + k:1 + k + NB], scalar=cwv[:, i * 4 + k:i * 4 + k + 1], in1=a[:],
                        op0=ALU.mult, op1=ALU.add), reads=[("raw", j), "cwv", ("acc", j)], writes=[("acc", j)])
                if i == 2:
                    P.op("act", lambda e, a=a, bs=bs: e.activation(out=vnb[:, bs], in_=a[:], func=AF.Silu),
                         reads=[("acc", j)], writes=[("vnb", b)])
                    continue
                P.op("act", lambda e, a=a: e.activation(out=a[:], in_=a[:], func=AF.Silu), reads=[("acc", j)], writes=[("acc", j)])
                P.op("act", lambda e, a=a: e.activation(out=sqs[:], in_=a[:], func=AF.Square), reads=[("acc", j)], writes=["sqs"])
                for sbk in range(NB // 512):
                    ss = slice(sbk * 512, (sbk + 1) * 512)
                    pp = pss[sbk % 2]
                    P.op("pe", lambda e, pp=pp, ss=ss: e.matmul(pp[:], C(C_ONES), sqs[:, ss], start=True, stop=True),
                         reads=["cst", "sqs"], writes=[("bk", sbk % 2)])
                    P.op("act", lambda e, pp=pp, ss=ss: e.activation(out=rn[:, ss], in_=pp[:], func=AF.Sqrt, bias=small[:, 0:1], scale=1.0),
                         reads=[("bk", sbk % 2), "small"], writes=[("rn", sbk)])
                P.op("dve", lambda e: e.reciprocal(out=rn[:], in_=rn[:]), reads=[("rn", s_) for s_ in range(NB // 512)],
                     writes=[("rn", s_) for s_ in range(NB // 512)])
                rnk = [("rn", s_) for s_ in range(NB // 512)]
                if i == 0:
                    P.op("dve", lambda e, a=a, bs=bs: e.scalar_tensor_tensor(out=qnb[:, bs], in0=a[:], scalar=128.0 ** -0.5, in1=rn[:],
                                                                             op0=ALU.mult, op1=ALU.mult),
                         reads=[("acc", j)] + rnk, writes=[("qnb", b)])
                else:
                    P.op("dve", lambda e, a=a: e.tensor_tensor(out=a[:], in0=a[:], in1=rn[:], op=ALU.mult),
                         reads=[("acc", j)] + rnk, writes=[("acc", j)])
                    P.op("act", lambda e, a=a, bs=bs: e.activation(out=knb[:, bs], in_=a[:], func=AF.Identity),
                         reads=[("acc", j)], writes=[("knb", b)])
                    P.op("pool", lambda e, a=a, bs=bs: e.tensor_tensor(out=kbb[:, bs], in0=a[:], in1=bsb[:], op=ALU.mult),
                         reads=[("acc", j), "bsb"], writes=[("kbb", b)])

        cpy = [0]

        def copy_out(out_ap, in_ap, reads, writes):
            cpy[0] += 1
            if cpy[0] % 4 != 0:
                P.op("act", lambda e: e.activation(out=out_ap, in_=in_ap, func=AF.Identity), reads=reads, writes=writes)
            else:
                P.op("dve", lambda e: e.tensor_copy(out=out_ap, in_=in_ap), reads=reads, writes=writes)

        def prep(m):
            j = m % 3
            r = m % RING
            blk = m * 128 // NB
            ps_ = slice(m * 128, (m + 1) * 128)
            gcm = gc[:, m:m + 1]
            PDG_, PKK, PINV, ptr, PUW = PDGs[j], PKKs[j], PINVs[j], PTRs[j], PUWs[j]
            BA, BB = ("bk", 2 * j), ("bk", 2 * j + 1)
            LB = [Lb[j * 2], Lb[j * 2 + 1]]
            UB = [Ub[j * 2], Ub[j * 2 + 1]]
            TB = [TTb[j * 2], TTb[j * 2 + 1]]

            def lk(i):
                return ("Lb", j * 2 + i)

            def uk(i):
                return ("Ub", j * 2 + i)

            def tk(i):
                return ("TTb", j * 2 + i)
            P.op("pool", lambda e: e.tensor_scalar(out=dgt[j][:], in0=C(C_ID), scalar1=gcm, scalar2=None, op0=ALU.mult),
                 reads=["cst", "gc"], writes=[("dgt", j)])
            P.op("pe", lambda e: e.matmul(PDG_, C(C_ONES), dgt[j][:], start=True, stop=True),
                 reads=["cst", ("dgt", j)], writes=[BA])
            yield
            P.op("dve", lambda e: e.scalar_tensor_tensor(out=tU[j][:], in0=PDG_, scalar=gcm, in1=C(C_MU), op0=ALU.subtract, op1=ALU.add),
                 reads=[BA, "gc", "cst"], writes=[("tU", j)])
            P.op("dve", lambda e: e.scalar_tensor_tensor(out=tL[j][:], in0=PDG_, scalar=gcm, in1=C(C_MLS), op0=ALU.subtract, op1=ALU.subtract),
                 reads=[BA, "gc", "cst"], writes=[("tL", j)])
            P.op("act", lambda e: e.activation(out=egc[j][:], in_=PDG_, func=AF.Exp), reads=[BA], writes=[("egc", j)])
            P.op("act", lambda e: e.activation(out=tU[j][:], in_=tU[j][:], func=AF.Exp), reads=[("tU", j)], writes=[("tU", j)])
            P.op("act", lambda e: e.activation(out=tL[j][:], in_=tL[j][:], func=AF.Exp, scale=-1.0), reads=[("tL", j)], writes=[("tL", j)])
            P.op("pe", lambda e: e.matmul(PKK[0], knb[:, ps_], kbb[:, ps_], start=True, stop=True),
                 reads=[("knb", blk), ("kbb", blk)], writes=[BA])
            P.op("pe", lambda e: e.matmul(PKK[1], kbb[:, ps_], knb[:, ps_], start=True, stop=True),
                 reads=[("knb", blk), ("kbb", blk)], writes=[BA])
            P.op("pe", lambda e: e.matmul(PKK[2], knb[:, ps_], qnb[:, ps_], start=True, stop=True),
                 reads=[("knb", blk), ("qnb", blk)], writes=[BA])
            P.op("pe", lambda e: e.matmul(ptr[:, 0:128], vnb[:, ps_], identb[:], start=True, stop=True), reads=[("vnb", blk), "identb"], writes=[BB])
            P.op("pe", lambda e: e.matmul(ptr[:, 128:256], knb[:, ps_], identb[:], start=True, stop=True), reads=[("knb", blk), "identb"], writes=[BB])
            yield
            P.op("pool", lambda e: e.tensor_tensor(out=EUs[j][:], in0=tU[j][:], in1=C(C_OFFD), op=ALU.mult),
                 reads=[("tU", j), "cst"], writes=[("EUs", j)])
            P.op("dve", lambda e: e.tensor_scalar(out=vb[j][:], in0=ptr[:, 0:128], scalar1=btok[:, m:m + 1], scalar2=None, op0=ALU.mult),
                 reads=[BB, "btok"], writes=[("vb", j)])
            P.op("dve", lambda e: e.tensor_scalar(out=kbg[j][:], in0=ptr[:, 128:256], scalar1=bgc[:, m:m + 1], scalar2=None, op0=ALU.mult),
                 reads=[BB, "bgc"], writes=[("kbg", j)])
            P.op("dve", lambda e: e.tensor_scalar(out=kdec[r][:], in0=ptr[:, 128:256], scalar1=edl[:, m:m + 1], scalar2=None, op0=ALU.mult),
                 reads=[BB, "edl"], writes=[("kdec", r)])
            P.op("pool", lambda e: e.tensor_tensor(out=qdec[r][:], in0=qnb[:, ps_], in1=egc[j][:], op=ALU.mult),
                 reads=[("qnb", blk), ("egc", j)], writes=[("qdec", r)])
            yield
            P.op("dve", lambda e: e.tensor_tensor(out=LB[0][:], in0=PKK[1], in1=tL[j][:], op=ALU.mult),
                 reads=[BA, ("tL", j)], writes=[lk(0)])
            P.op("dve", lambda e: e.tensor_tensor(out=attnT[r][:], in0=PKK[2], in1=tU[j][:], op=ALU.mult),
                 reads=[BA, ("tU", j)], writes=[("attnT", r)])
            P.op("dve", lambda e: e.tensor_tensor(out=UB[0][:], in0=PKK[0], in1=EUs[j][:], op=ALU.mult),
                 reads=[BA, ("EUs", j)], writes=[uk(0)])
            P.op("pool", lambda e: e.tensor_tensor(out=TB[0][:], in0=C(C_ID), in1=UB[0][:], op=ALU.subtract),
                 reads=[uk(0), "cst"], writes=[tk(0)])
            yield
            cur = 0
            for k in range(5):
                nx = 1 - cur
                P.op("pe", lambda e, cur=cur: e.matmul(PINV[0], UB[cur][:], LB[cur][:], start=True, stop=True),
                     reads=[uk(cur), lk(cur)], writes=[BB])
                if k < 4:
                    P.op("pe", lambda e, cur=cur: e.matmul(PINV[1], LB[cur][:], UB[cur][:], start=True, stop=True),
                         reads=[uk(cur), lk(cur)], writes=[BB])
                yield
                copy_out(LB[nx][:], PINV[0], [BB], [lk(nx)])
                if k < 4:
                    copy_out(UB[nx][:], PINV[1], [BB], [uk(nx)])
                P.op("pe", lambda e, cur=cur: e.matmul(PINV[2], identb[:], TB[cur][:], start=True, stop=False),
                     reads=["identb", tk(cur)], writes=[BA])
                P.op("pe", lambda e, cur=cur, nx=nx: e.matmul(PINV[2], LB[nx][:], TB[cur][:], start=False, stop=True),
                     reads=[lk(nx), tk(cur)], writes=[BA])
                yield
                copy_out(TB[nx][:], PINV[2], [BA], [tk(nx)])
                cur = nx
            TT = TB[cur]
            ttk = tk(cur)
            P.op("pe", lambda e: e.matmul(PUW[0], TT[:], vb[j][:], start=True, stop=True), reads=[ttk, ("vb", j)], writes=[BB])
            P.op("pe", lambda e: e.matmul(PUW[1], TT[:], kbg[j][:], start=True, stop=True), reads=[ttk, ("kbg", j)], writes=[BB])
            yield
            copy_out(ub[r][:], PUW[0], [BB], [("ub", r)])
            copy_out(wtok[j][:], PUW[1], [BB], [("wtok", j)])
            PAT = PATs[j]
            PATK = [BA, BB]
            for h in range(2):
                hs = slice(h * 64, (h + 1) * 64)
                P.op("pe", lambda e, h=h, hs=hs: e.matmul(PAT[h], wtok[j][hs, :], kdec[r][hs, :], start=True, stop=True),
                     reads=[("wtok", j), ("kdec", r)], writes=[PATK[h]])
            P.op("pe", lambda e: e.matmul(PQPs[j], wtok[j][:], attnT[r][:], start=True, stop=True),
                 reads=[("wtok", j), ("attnT", r)], writes=[BA])
            yield
            for h in range(2):
                P.op("act", lambda e, h=h: e.activation(out=At[r][h][:], in_=PAT[h], func=AF.Identity, scale=-1.0),
                     reads=[PATK[h]], writes=[("At", r, h)])
            P.op("dve", lambda e: e.tensor_tensor(out=qdec[r][:], in0=qdec[r][:], in1=PQPs[j], op=ALU.subtract),
                 reads=[("qdec", r), BA], writes=[("qdec", r)])

        def scan(m):
            r = m % RING
            for h in range(2):
                n = 2 * m + h
                hs = slice(h * 64, (h + 1) * 64)
                j = n % 2
                P.op("pe", lambda e, j=j, h=h: e.matmul(PDS[j], At[r][h][:], Sb[:], start=True, stop=False),
                     reads=[("At", r, h), "Sb"], writes=[("bk", 6)])
                P.op("pe", lambda e, j=j, hs=hs: e.matmul(PDS[j], kdec[r][hs, :], ub[r][hs, :], start=False, stop=True),
                     reads=[("kdec", r), ("ub", r)], writes=[("bk", 6)])
                P.op("pe", lambda e, j=j, hs=hs: e.matmul(POT[j][:, 0:64], Sb[:], qdec[r][:, hs], start=True, stop=False),
                     reads=["Sb", ("qdec", r)], writes=[("bk", 7)])
                P.op("pe", lambda e, j=j, hs=hs: e.matmul(POT[j][:, 0:64], ub[r][hs, :], attnT[r][hs, hs], start=False, stop=True),
                     reads=[("ub", r), ("attnT", r)], writes=[("bk", 7)])
                yield
                P.op("dve", lambda e, j=j, h=h: e.scalar_tensor_tensor(out=Sb[:], in0=S32[:], scalar=glb[h][:, m:m + 1], in1=PDS[j],
                                                                       op0=ALU.mult, op1=ALU.add),
                     reads=["S32", ("glb", h), ("bk", 6)], writes=["Sb"])
                P.op("dve", lambda e, j=j, h=h: e.scalar_tensor_tensor(out=S32[:], in0=S32[:], scalar=glb[h][:, m:m + 1], in1=PDS[j],
                                                                       op0=ALU.mult, op1=ALU.add),
                     reads=["S32", ("glb", h), ("bk", 6)], writes=["S32"])
                P.op("act", lambda e, j=j, n=n: e.activation(out=oT[:, n * 64:(n + 1) * 64], in_=POT[j][:, 0:64], func=AF.Identity),
                     reads=[("bk", 7)], writes=[("oT", n * 64 // NB)])
                yield

        KP = int(os.environ.get("KP", 3))
        prep_next, prep_done, scan_next, scan_done = 0, set(), 0, 0
        active = []
        scan_active = False
        while scan_done < npair:
            while (prep_next < npair and sum(1 for a_ in active if a_[0] == "p") < KP and prep_next < scan_done + RING):
                active.append(["p", prep_next, prep(prep_next)])
                prep_next += 1
            if not scan_active and scan_next < npair and scan_next in prep_done:
                active.append(["s", scan_next, scan(scan_next)])
                scan_active = True
                scan_next += 1
            for a_ in list(active):
                try:
                    next(a_[2])
                except StopIteration:
                    active.remove(a_)
                    if a_[0] == "p":
                        prep_done.add(a_[1])
                    else:
                        scan_done += 1
                        scan_active = False

        if os.environ.get("SKIP_POST"):
            P.dma("sp", lambda e: e.dma_start(out=outd[:, 0:1024], in_=cst[:, 0:1024]), reads=["cst"])
        for b in range(npair * 128 // NB if not os.environ.get("SKIP_POST") else 0):
            bs = slice(b * NB, (b + 1) * NB)
            j = b % 2
            P.dma("sp", lambda e, bs=bs, j=j: e.dma_start(out=raw[j][:, 0:NB], in_=zd[:, bs]), writes=[("raw", j)])
            P.op("act", lambda e, j=j: e.activation(out=raw[j][:, 0:NB], in_=raw[j][:, 0:NB], func=AF.Silu), reads=[("raw", j)], writes=[("raw", j)])
            P.op("act", lambda e, bs=bs: e.activation(out=sqs[:], in_=oT[:, bs], func=AF.Square), reads=[("oT", b)], writes=["sqs"])
            for sbk in range(NB // 512):
                ss = slice(sbk * 512, (sbk + 1) * 512)
                pp = pss[sbk % 2]
                P.op("pe", lambda e, pp=pp, ss=ss: e.matmul(pp[:], C(C_ONES), sqs[:, ss], start=True, stop=True),
                     reads=["cst", "sqs"], writes=[("bk", sbk % 2)])
                P.op("act", lambda e, pp=pp, ss=ss: e.activation(out=rn[:, ss], in_=pp[:], func=AF.Sqrt, bias=small[:, 0:1], scale=1.0 / 128),
                     reads=[("bk", sbk % 2), "small"], writes=[("rn", sbk)])
            rnk = [("rn", s_) for s_ in range(NB // 512)]
            P.op("dve", lambda e: e.reciprocal(out=rn[:], in_=rn[:]), reads=rnk, writes=rnk)
            a = acc[j]
            P.op("dve", lambda e, a=a, bs=bs: e.scalar_tensor_tensor(out=a[:], in0=oT[:, bs], scalar=hv[:, 2:3], in1=rn[:], op0=ALU.mult, op1=ALU.mult),
                 reads=[("oT", b), "hv"] + rnk, writes=[("acc", j)])
            P.op("pool", lambda e, a=a, j=j: e.tensor_tensor(out=a[:], in0=a[:], in1=raw[j][:, 0:NB], op=ALU.mult),
                 reads=[("acc", j), ("raw", j)], writes=[("acc", j)])
            P.dma("sp", lambda e, a=a, bs=bs: e.dma_start(out=outd[:, bs], in_=a[:]), reads=[("acc", j)])
        P.emit()
    return nc


import math

TA = 8192
NT = TA // 128
HT = 32
MAGIC = 12582912.0
C1 = 6.28125
C2 = 2.0 * math.pi - 6.28125
PI_SAFE = 3.1415925


def ret_consts():
    p = np.arange(128)[:, None]
    f = np.arange(128)[None, :]
    c = np.zeros((128, 256), np.float32)
    c[:, 0:128] = (p <= f)
    c[:, 128:256] = (p == f)
    return c


def ret_rc(head):
    gamma = 1.0 - 2.0 ** (-5.0 - head)
    rc = np.zeros((128, 72), np.float32)
    half = 64
    rc[:, 0:64] = (10000.0 ** (-np.arange(half, dtype=np.float32) / half)).astype(np.float32)[None, :]
    p = np.arange(128, dtype=np.float64)
    rc[:, 64] = gamma ** (p + 1)
    rc[:, 65] = 128.0 ** -0.5 * gamma ** (-(p + 1))
    rc[:, 66] = gamma ** 128
    rc[:, 67] = math.pi / 2
    return rc


def build_ret(nt=NT):
    nc = bass.Bass("TRN2", target_bir_lowering=False)
    es = ExitStack()
    P = Prog(nc, es)
    qd = nc.dram_tensor("q_tok", [128, NT, 128], F32, kind="ExternalInput").ap()
    kd = nc.dram_tensor("k_tok", [128, NT, 128], F32, kind="ExternalInput").ap()
    vd = nc.dram_tensor("v_tok", [128, NT, 128], F32, kind="ExternalInput").ap()
    posd = nc.dram_tensor("pos_tok", [128, NT], I32, kind="ExternalInput").ap()
    rcd = nc.dram_tensor("rc", [128, 72], F32, kind="ExternalInput").ap()
    cstd = nc.dram_tensor("cst", [128, 256], F32, kind="ExternalInput").ap()
    outd = nc.dram_tensor("oretT", [128, TA], F32, kind="ExternalOutput").ap()
    nh = (nt + HT - 1) // HT
    with es:
        cst = P.sb("cst", [128, 256], F32)
        rc = P.sb("rc", [128, 72], F32)
        posi = P.sb("posi", [128, NT], I32)
        posf = P.sb("posf", [128, NT], F32)
        identb = P.sb("identb", [128, 128], BF16)
        qh = P.sb("qh", [128, HT, 128], F32)
        kh = P.sb("kh", [128, HT, 128], F32)
        ang = P.sb("ang", [128, HT, 64], F32)
        tt = P.sb("tt", [128, HT, 64], F32)
        cs = P.sb("cs", [128, HT, 64], F32)
        sn = P.sb("sn", [128, HT, 64], F32)
        A = P.sb("A", [128, HT, 64], F32)
        B = P.sb("B", [128, HT, 64], F32)
        qb = P.sb("qb", [128, NT, 128], BF16)
        kb = P.sb("kb", [128, NT, 128], BF16)
        vb = P.sb("vb", [128, NT, 128], BF16)
        oT = P.sb("oT", [128, TA], F32)
        qinT = [P.sb(f"qinT{i}", [128, 128], BF16) for i in range(2)]
        koutT = [P.sb(f"koutT{i}", [128, 128], BF16) for i in range(2)]
        qinT2 = [P.sb(f"qinT2{i}", [128, 128], BF16) for i in range(2)]
        scm = [P.sb(f"scm{i}", [128, 128], BF16) for i in range(2)]
        W32 = P.sb("W32", [128, 128], F32)
        Rb = P.sb("Rb", [128, 128], BF16)
        bkq = P.ps("bkq", [128, 1024], BF16)
        bkk = P.ps("bkk", [128, 1024], BF16)
        bks = P.ps("bks", [128, 512])
        bko = P.ps("bko", [128, 512])
        bkr = P.ps("bkr", [128, 512])

        P.dma("sp", lambda e: e.dma_start(out=cst[:], in_=cstd), writes=["cst"])
        P.dma("sp", lambda e: e.dma_start(out=rc[:], in_=rcd), writes=["rc"])
        P.dma("sp", lambda e: e.dma_start(out=posi[:], in_=posd), writes=["posi"])
        for hq in range(nh * 2):
            P.dma("pool", lambda e, hq=hq: e.dma_start(out=vb[:, hq * 16:(hq + 1) * 16, :], in_=vd[:, hq * 16:(hq + 1) * 16, :]),
                  writes=[("vb", hq // 2)])
        P.op("dve", lambda e: e.tensor_copy(out=posf[:], in_=posi[:]), reads=["posi"], writes=["posf"])
        P.op("dve", lambda e: e.tensor_copy(out=identb[:], in_=cst[:, 128:256]), reads=["cst"], writes=["identb"])
        P.op("dve", lambda e: e.memset(W32[:], 0.0), writes=["W32"])
        P.op("dve", lambda e: e.memset(Rb[:], 0.0), writes=["Rb"])

        def fl(t):
            return t[:].rearrange("p a b -> p (a b)")

        def reduce_to(dst_key):
            P.op("dve", lambda e: e.tensor_scalar(out=fl(tt), in0=fl(ang), scalar1=1.0 / (2 * math.pi), scalar2=MAGIC, op0=ALU.mult, op1=ALU.add),
                 reads=["ang"], writes=["tt"])
            P.op("dve", lambda e: e.tensor_scalar(out=fl(tt), in0=fl(tt), scalar1=-MAGIC, scalar2=None, op0=ALU.add),
                 reads=["tt"], writes=["tt"])
            P.op("dve", lambda e: e.scalar_tensor_tensor(out=fl(A), in0=fl(tt), scalar=-C1, in1=fl(ang), op0=ALU.mult, op1=ALU.add),
                 reads=["tt", "ang"], writes=["A"])
            P.op("dve", lambda e: e.scalar_tensor_tensor(out=fl(A), in0=fl(tt), scalar=-C2, in1=fl(A), op0=ALU.mult, op1=ALU.add),
                 reads=["tt", "A"], writes=["A"])
            P.op("dve", lambda e: e.tensor_scalar(out=fl(A), in0=fl(A), scalar1=-PI_SAFE, scalar2=PI_SAFE, op0=ALU.max, op1=ALU.min),
                 reads=["A"], writes=["A"])

        for hh in range(nh):
            m0 = hh * HT
            P.dma("sp", lambda e, m0=m0: e.dma_start(out=qh[:], in_=qd[:, m0:m0 + HT, :]), writes=["qh"])
            P.dma("sp", lambda e, m0=m0: e.dma_start(out=kh[:], in_=kd[:, m0:m0 + HT, :]), writes=["kh"])
            for i in range(HT):
                P.op("pool", lambda e, i=i, m0=m0: e.tensor_scalar(out=ang[:, i, :], in0=rc[:, 0:64], scalar1=posf[:, m0 + i:m0 + i + 1],
                                                                  scalar2=None, op0=ALU.mult),
                     reads=["rc", "posf"], writes=["ang"])
            reduce_to("A")
            P.op("act", lambda e: e.activation(out=fl(sn), in_=fl(A), func=AF.Sin), reads=["A"], writes=["sn"])
            P.op("dve", lambda e: e.scalar_tensor_tensor(out=fl(A), in0=fl(A), scalar=-1.0, in1=fl(A), op0=ALU.mult, op1=ALU.max), reads=["A"], writes=["A"])
            P.op("act", lambda e: e.activation(out=fl(cs), in_=fl(A), func=AF.Sin, scale=-1.0, bias=rc[:, 67:68]), reads=["A", "rc"], writes=["cs"])
            for (src, dst, col, eng) in ((qh, qb, 64, "dve"), (kh, kb, 65, "dve")):
                x1 = src[:, :, 0:64]
                x2 = src[:, :, 64:128]
                skey = "qh" if src is qh else "kh"
                dkey = ("qb", hh) if dst is qb else ("kb", hh)
                P.op("dve", lambda e, x1=x1: e.tensor_tensor(out=A[:], in0=x1, in1=cs[:], op=ALU.mult), reads=[skey, "cs"], writes=["A"])
                P.op("pool", lambda e, x2=x2: e.tensor_tensor(out=B[:], in0=x2, in1=sn[:], op=ALU.mult), reads=[skey, "sn"], writes=["B"])
                P.op("dve", lambda e: e.tensor_tensor(out=A[:], in0=A[:], in1=B[:], op=ALU.subtract), reads=["A", "B"], writes=["A"])
                P.op("dve", lambda e, dst=dst, m0=m0, col=col: e.tensor_scalar(out=dst[:, m0:m0 + HT, 0:64], in0=A[:], scalar1=rc[:, col:col + 1],
                                                                               scalar2=None, op0=ALU.mult),
                     reads=["A", "rc"], writes=[dkey])
                P.op("dve", lambda e, x1=x1: e.tensor_tensor(out=A[:], in0=x1, in1=sn[:], op=ALU.mult), reads=[skey, "sn"], writes=["A"])
                P.op("pool", lambda e, x2=x2: e.tensor_tensor(out=B[:], in0=x2, in1=cs[:], op=ALU.mult), reads=[skey, "cs"], writes=["B"])
                P.op("dve", lambda e: e.tensor_tensor(out=A[:], in0=A[:], in1=B[:], op=ALU.add), reads=["A", "B"], writes=["A"])
                P.op("dve", lambda e, dst=dst, m0=m0, col=col: e.tensor_scalar(out=dst[:, m0:m0 + HT, 64:128], in0=A[:], scalar1=rc[:, col:col + 1],
                                                                               scalar2=None, op0=ALU.mult),
                     reads=["A", "rc"], writes=[dkey])

        for m in range(nt):
            hh = m // HT
            j = m % 2
            P.op("pe", lambda e, m=m: e.transpose(bkq[:, 0:128], qb[:, m, :], identb[:]), reads=[("qb", hh), "identb"], writes=[("bk", 0)])
            P.op("pe", lambda e, m=m: e.transpose(bkk[:, 0:128], kb[:, m, :], identb[:]), reads=[("kb", hh), "identb"], writes=[("bk", 1)])
            P.op("dve", lambda e, j=j: e.tensor_copy(out=qinT[j][:], in_=bkq[:, 0:128]), reads=[("bk", 0)], writes=[("qinT", j)])
            P.op("act", lambda e, j=j: e.activation(out=koutT[j][:], in_=bkk[:, 0:128], func=AF.Identity), reads=[("bk", 1)], writes=[("koutT", j)])
            P.op("pe", lambda e, j=j: e.matmul(bks[:, 0:128], koutT[j][:], qinT[j][:], start=True, stop=True),
                 reads=[("koutT", j), ("qinT", j)], writes=[("bk", 2)])
            P.op("dve", lambda e, j=j: e.tensor_tensor(out=scm[j][:], in0=bks[:, 0:128], in1=cst[:, 0:128], op=ALU.mult),
                 reads=[("bk", 2), "cst"], writes=[("scm", j)])
            P.op("pe", lambda e, j=j, m=m: e.matmul(bko[:, 0:128], vb[:, m, :], scm[j][:], start=True, stop=False),
                 reads=[("vb", hh), ("scm", j)], writes=[("bk", 3)])
            P.op("pe", lambda e, j=j: e.matmul(bko[:, 0:128], Rb[:], qinT[j][:], start=False, stop=True),
                 reads=["Rb", ("qinT", j)], writes=[("bk", 3)])
            P.op("pe", lambda e, m=m: e.matmul(bkr[:, 0:128], kb[:, m, :], vb[:, m, :], start=True, stop=True),
                 reads=[("kb", hh), ("vb", hh)], writes=[("bk", 4)])
            P.op("act", lambda e, m=m: e.activation(out=oT[:, m * 128:(m + 1) * 128], in_=bko[:, 0:128], func=AF.Identity),
                 reads=[("bk", 3)], writes=[("oT", m // 8)])
            P.op("dve", lambda e: e.scalar_tensor_tensor(out=W32[:], in0=W32[:], scalar=rc[:, 66:67], in1=bkr[:, 0:128], op0=ALU.mult, op1=ALU.add),
                 reads=["W32", "rc", ("bk", 4)], writes=["W32"])
            P.op("act", lambda e: e.activation(out=Rb[:], in_=W32[:], func=AF.Identity, scale=rc[:, 66:67]), reads=["W32", "rc"], writes=["Rb"])
            if m % 8 == 7:
                g = m // 8
                P.dma("sp", lambda e, g=g: e.dma_start(out=outd[:, g * 1024:(g + 1) * 1024], in_=oT[:, g * 1024:(g + 1) * 1024]),
                      reads=[("oT", g)])
        P.emit()
    return nc


_sizes = (1024, 1024, 1024, 1024, 8, 8, 512, 512, 1024, 1024, 2048, 6144)
_off = np.concatenate([[0], np.cumsum(_sizes)])
_order = [0, 1, 2, 3, 6, 7, 8, 9, 10, 11, 4, 5]
PERM = np.concatenate([np.arange(_off[i], _off[i + 1]) for i in _order])
_psizes = [_sizes[i] for i in _order]
_poff = np.concatenate([[0], np.cumsum(_psizes)])
P_AQ, P_AK, P_AV, P_AZ, P_BQ, P_BK, P_BV, P_BG, P_CGLU, P_GATE, P_BETA, P_ALPHA = [int(v) for v in _poff[:12]]


def pk(v):
    return np.ascontiguousarray(v.reshape(-1, 128).T)


def make_vec(mod_l, nmix, nmlp, nfin):
    parts = [pk(m) for m in np.split(mod_l, 6)] + [pk(nmix), pk(nmlp), pk(nfin)]
    return np.ascontiguousarray(np.concatenate(parts, axis=1).astype(np.float32))


def make_cgT(cg, c, T=1024):
    out = np.zeros((2048, T + 32), np.float32)
    lo = c * T - 32
    if lo >= 0:
        out[:] = cg[lo:(c + 1) * T].T
    else:
        out[:, 32:] = cg[0:T].T
    return out


def make_convc_params(dw_w, dw_b, ln_w, ln_b):
    cw = np.ascontiguousarray(dw_w.reshape(31, 8, 128).transpose(2, 1, 0).reshape(128, 8 * 31)).astype(np.float32)
    cvec = np.ascontiguousarray(np.concatenate([pk(dw_b), pk(ln_w), pk(ln_b)], axis=1)).astype(np.float32)
    return cw, cvec


def make_gdn_inputs(aq, ak, av, az, abeta, aalpha, conv_w, a_log, dt_bias, norm_w, hd, cst):
    Tn = aq.shape[0]
    hs = slice(hd * 128, (hd + 1) * 128)

    def padT(a):
        o = np.zeros((128, Tn + 4), np.float32)
        o[:, 4:] = a[:, hs].T
        return o
    cwv = np.zeros((128, 12), np.float32)
    for i in range(3):
        cwv[:, i * 4:(i + 1) * 4] = conv_w[:, i * 1024 + hd * 128:i * 1024 + (hd + 1) * 128].T
    hv = np.zeros((128, 4), np.float32)
    hv[:, 0] = a_log[hd]
    hv[:, 1] = dt_bias[hd]
    hv[:, 2] = norm_w
    return {"qT": padT(aq), "kT": padT(ak), "vT": padT(av), "zT": np.ascontiguousarray(az[:, hs].T),
            "brow": np.ascontiguousarray(np.broadcast_to(abeta[:, hd][None, :], (128, Tn))),
            "btok": np.ascontiguousarray(abeta[:, hd].reshape(-1, 128).T),
            "atok": np.ascontiguousarray(aalpha[:, hd].reshape(-1, 128).T),
            "cwv": cwv, "hv": hv, "cst": cst}


def tokmaj(a, cols):
    x = a[:, cols]
    return np.ascontiguousarray(x.reshape(-1, 128, x.shape[1]).transpose(1, 0, 2))


_PROGS = {}


def _prog(name, fn):
    if name not in _PROGS:
        _PROGS[name] = fn()
    return _PROGS[name]


def _run(nc, in_maps):
    res = run_bass_kernel_spmd(nc, in_maps, core_ids=list(range(len(in_maps))))
    return res.results


def _c(a):
    return np.ascontiguousarray(a, dtype=np.float32)


def kernel(x, c, positions, w_ada, b_ada, norm_mix_w, norm_mlp_w, w_in, conv_qkv_w,
           gdn_a_log, gdn_dt_bias, gdn_norm_w, conv_dw_w, conv_dw_b, conv_ln_w, conv_ln_b,
           w_branch_a, w_branch_b, w_branch_c, w_out, w_mlp_in, w_mlp_out, final_norm_w):
    NCORE = 8
    TT = 1024
    x = np.asarray(x, np.float32)[0]
    positions = np.asarray(positions)
    nc = _prog("ada", build_ada)
    ims = [{"cT": pk(np.asarray(c, np.float32)[0]),
            "wa": _c(np.asarray(w_ada)[:, :, j * 1536:(j + 1) * 1536]),
            "ba": _c(np.asarray(b_ada)[:, j * 1536:(j + 1) * 1536])} for j in range(NCORE)]
    res = _run(nc, ims)
    mod = np.concatenate([r["mod"] for r in res], axis=1)
    xT = [_c(x[j * TT:(j + 1) * TT].T) for j in range(NCORE)]
    gcst = gdn_consts()
    rcst = ret_consts()
    pos_tok = np.ascontiguousarray(positions[0].reshape(-1, 128).T.astype(np.int32))
    depth = np.asarray(w_in).shape[0]
    for l in range(depth):
        vec = make_vec(mod[l], np.asarray(norm_mix_w)[l], np.asarray(norm_mlp_w)[l], np.asarray(final_norm_w))
        w_in_p = _c(np.asarray(w_in)[l][:, PERM])
        res = _run(_prog("pre", build_pre), [{"xT": xT[j], "vec": vec, "w_in": w_in_p} for j in range(NCORE)])
        projT = np.concatenate([r["projT"] for r in res], axis=1)
        del res, w_in_p
        cqw = np.asarray(conv_qkv_w)[l]
        ims = []
        for hd in range(NCORE):
            def padT(r0):
                o = np.zeros((128, 8192 + 4), np.float32)
                o[:, 4:] = projT[r0 + hd * 128:r0 + (hd + 1) * 128]
                return o
            cwv = np.zeros((128, 12), np.float32)
            for i in range(3):
                cwv[:, i * 4:(i + 1) * 4] = cqw[:, i * 1024 + hd * 128:i * 1024 + (hd + 1) * 128].T
            hv = np.zeros((128, 4), np.float32)
            hv[:, 0] = np.asarray(gdn_a_log)[l][hd]
            hv[:, 1] = np.asarray(gdn_dt_bias)[l][hd]
            hv[:, 2] = np.asarray(gdn_norm_w)[l]
            brow = projT[P_BETA + hd]
            arow = projT[P_ALPHA + hd]
            ims.append({"qT": padT(P_AQ), "kT": padT(P_AK), "vT": padT(P_AV),
                        "zT": _c(projT[P_AZ + hd * 128:P_AZ + (hd + 1) * 128]),
                        "brow": _c(np.broadcast_to(brow[None, :], (128, 8192))),
                        "btok": _c(brow.reshape(-1, 128).T), "atok": _c(arow.reshape(-1, 128).T),
                        "cwv": cwv, "hv": hv, "cst": gcst})
        res = _run(_prog("gdn", build_gdn), ims)
        oaT = np.concatenate([r["oaT"] for r in res], axis=0)
        ims = []
        for j in range(NCORE):
            head, half = j // 2, j % 2

            def tokm(r0):
                a = projT[r0:r0 + 128]
                return _c(a.reshape(128, 64, 128).transpose(2, 1, 0))
            ims.append({"q_tok": tokm(P_BQ + head * 128), "k_tok": tokm(P_BK + head * 128),
                        "v_tok": tokm(P_BV + head * 256 + half * 128), "pos_tok": pos_tok,
                        "rc": ret_rc(head), "cst": rcst})
        res = _run(_prog("ret", build_ret), ims)
        oretT = np.concatenate([r["oretT"] for r in res], axis=0)
        cw, cvec = make_convc_params(np.asarray(conv_dw_w)[l], np.asarray(conv_dw_b)[l],
                                     np.asarray(conv_ln_w)[l], np.asarray(conv_ln_b)[l])
        ims = []
        for j in range(NCORE):
            cg = np.zeros((2048, TT + 32), np.float32)
            lo = j * TT - 32
            if lo >= 0:
                cg[:] = projT[P_CGLU:P_CGLU + 2048, lo:(j + 1) * TT]
            else:
                cg[:, 32:] = projT[P_CGLU:P_CGLU + 2048, 0:TT]
            ims.append({"cgT": cg, "cw": cw, "cvec": cvec})
        res = _run(_prog("convc", build_convc), ims)
        ocT = [r["ocT"] for r in res]
        ims = [{"oretT": _c(oretT[:, j * TT:(j + 1) * TT]), "bgT": _c(projT[P_BG:P_BG + 1024, j * TT:(j + 1) * TT])}
               for j in range(NCORE)]
        res = _run(_prog("retln", build_retln), ims)
        obT = [r["obT"] for r in res]
        wba, wbb, wbc = _c(np.asarray(w_branch_a)[l]), _c(np.asarray(w_branch_b)[l]), _c(np.asarray(w_branch_c)[l])
        wo = _c(np.asarray(w_out)[l])
        ims = [{"xT": xT[j], "vec": vec, "oaT": _c(oaT[:, j * TT:(j + 1) * TT]), "obT": obT[j], "ocT": ocT[j],
                "gT": _c(projT[P_GATE:P_GATE + 6144, j * TT:(j + 1) * TT]),
                "wba": wba, "wbb": wbb, "wbc": wbc, "w_out": wo} for j in range(NCORE)]
        res = _run(_prog("merge", build_merge), ims)
        x1T = [r["x1T"] for r in res]
        del projT
        final = (l == depth - 1)
        w1, w2 = _c(np.asarray(w_mlp_in)[l]), _c(np.asarray(w_mlp_out)[l])
        ims = [{"xT": x1T[j], "vec": vec, "w1": w1, "w2": w2} for j in range(NCORE)]
        res = _run(_prog("mlpF" if final else "mlp", (lambda: build_mlp(True)) if final else (lambda: build_mlp(False))), ims)
        xT = [r["x2T"] for r in res]
    out = np.concatenate([t.T for t in xT], axis=0)[None]
    return np.ascontiguousarray(out, dtype=np.float32)
```

```python
import os

import numpy as np
from contextlib import ExitStack
import concourse.bass as bass
import concourse.mybir as mybir
from concourse.bass_utils import run_bass_kernel_spmd

F32 = mybir.dt.float32
BF16 = mybir.dt.bfloat16
I32 = mybir.dt.int32
AF = mybir.ActivationFunctionType
ALU = mybir.AluOpType
AX = mybir.AxisListType


class Prog:
    ENGS = ("pe", "dve", "act", "pool", "sp")
    NDMA = 8

    def __init__(self, nc, es, same_engine_sync=True):
        self.nc = nc
        self.es = es
        self.same = same_engine_sync
        self.ops = {e: [] for e in self.ENGS}
        self.sems = {}
        self.cnt = {}
        for e in ("pe", "dve", "act", "pool"):
            self.sems[e] = es.enter_context(nc.semaphore("c_" + e))
            self.cnt[e] = 0
        for q in ("sp", "pool", "act"):
            for i in range(self.NDMA):
                k = ("dma", q, i)
                self.sems[k] = es.enter_context(nc.semaphore(f"d_{q}{i}"))
                self.cnt[k] = 0
        self.dma_i = {"sp": 0, "pool": 0, "act": 0}
        self.waited = {}
        self.lastw = {}
        self.readers = {}
        self.n_ops = 0

    def sb(self, name, shape, dt):
        return self.es.enter_context(self.nc.sbuf_tensor("s_" + name, list(shape), dt))

    def ps(self, name, shape, dt=F32):
        return self.es.enter_context(self.nc.psum_tensor("p_" + name, list(shape), dt))

    def _deps(self, reads, writes):
        deps = {}

        def add(tok):
            if tok is None:
                return
            k, v = tok
            if deps.get(k, 0) < v:
                deps[k] = v
        for b in reads:
            add(self.lastw.get(b))
        for b in writes:
            add(self.lastw.get(b))
            for k, v in self.readers.get(b, {}).items():
                add((k, v))
        return deps

    def _commit(self, tok, reads, writes):
        k, v = tok
        for b in reads:
            r = self.readers.setdefault(b, {})
            if r.get(k, 0) < v:
                r[k] = v
        for b in writes:
            self.lastw[b] = tok
            self.readers[b] = {}

    def _waits(self, eng, deps):
        waits = []
        for k, v in deps.items():
            if k == "pe" and eng == "pe":
                continue
            if (not self.same) and k == eng:
                continue
            if self.waited.get((eng, k), 0) >= v:
                continue
            self.waited[(eng, k)] = v
            waits.append((k, v))
        return waits

    @staticmethod
    def _excl(reads, writes):
        ex = [b for b in reads if isinstance(b, tuple) and b[0] == "bk"]
        if ex:
            writes = list(writes) + [b for b in ex if b not in writes]
        return reads, writes

    def op(self, eng, fn, reads=(), writes=()):
        reads, writes = self._excl(reads, writes)
        deps = self._deps(reads, writes)
        waits = self._waits(eng, deps)
        self.cnt[eng] += 1
        tok = (eng, self.cnt[eng])
        self.ops[eng].append((waits, fn, (eng, 1)))
        self._commit(tok, reads, writes)
        self.n_ops += 1
        return tok

    def dma(self, q, fn, reads=(), writes=()):
        i = self.dma_i[q]
        self.dma_i[q] += 1
        k = ("dma", q, i % self.NDMA)
        deps = self._deps(reads, writes)
        if self.cnt[k] > 0:
            if deps.get(k, 0) < self.cnt[k]:
                deps[k] = self.cnt[k]
        waits = self._waits(q, deps)
        self.cnt[k] += 16
        tok = (k, self.cnt[k])
        self.ops[q].append((waits, fn, (k, 16)))
        self._commit(tok, reads, writes)
        self.n_ops += 1
        return tok

    def emit(self):
        final = []
        for k, v in self.cnt.items():
            if v > 0 and self.waited.get(("sp", k), 0) < v:
                final.append((k, v))
        nc = self.nc
        sems = self.sems
        ops = self.ops

        def run(e, name, fin=False):
            for waits, fn, (sk, inc) in ops[name]:
                for k, v in waits:
                    e.wait_ge(sems[k], v)
                ins = fn(e)
                ins.then_inc(sems[sk], inc)
            if fin:
                for k, v in final:
                    e.wait_ge(sems[k], v)

        with nc.Block() as block:
            @block.tensor
            def _(e):
                run(e, "pe")

            @block.vector
            def _(e):
                run(e, "dve")

            @block.scalar
            def _(e):
                run(e, "act")

            @block.gpsimd
            def _(e):
                run(e, "pool")

            @block.sync
            def _(e):
                run(e, "sp", fin=True)


D = 2048
T = 1024
KC = D // 128
EPS = 1e-6
INW = 15376
V_SH1, V_SC1, V_G1, V_SH2, V_SC2, V_G2, V_NMIX, V_NMLP, V_NFIN = [i * 16 for i in range(9)]
NV = 9 * 16


def new_prog():
    nc = bass.Bass("TRN2", target_bir_lowering=False)
    es = ExitStack()
    P = Prog(nc, es)
    P.wctr = 0
    P.psctr = 0
    return nc, es, P


def gemm(P, w_dram, K, NC, CB, rhs, evac, wbufs, psb, ntb=T // 512, k0=0, extra=None, after=None):
    kcn = K // 128
    nblocks = (NC + CB - 1) // CB
    for cb in range(nblocks):
        c0 = cb * CB
        cw = min(CB, NC - c0)
        b = P.wctr % 2
        P.wctr += 1
        wt = wbufs[b][:, 0:kcn * CB].rearrange("p (kc c) -> p kc c", c=CB)
        src = w_dram[k0:k0 + K, c0:c0 + cw].rearrange("(kc p) c -> p kc c", p=128)
        P.dma("pool", lambda e, wt=wt, src=src, cw=cw: e.dma_start(out=wt[:, :, 0:cw], in_=src),
              writes=[("wt", b)])
        for ci in range((cw + 127) // 128):
            m = min(128, cw - ci * 128)
            ct = (c0 + ci * 128) // 128
            slot = P.psctr % 2
            P.psctr += 1
            for kc in range(kcn):
                for tb in range(ntb):
                    pst = psb[slot * ntb + tb]
                    rap, rkeys = rhs(kc, tb)
                    P.op("pe", lambda e, pst=pst, wt=wt, kc=kc, ci=ci, m=m, rap=rap, st=(kc == 0), sp=(kc == kcn - 1):
                         e.matmul(pst[0:m, :], wt[:, kc, ci * 128:ci * 128 + m], rap, start=st, stop=sp),
                         reads=[("wt", b)] + rkeys, writes=[("ps", slot * ntb + tb)])
            if extra is not None:
                extra(ct, m, wt, ci, slot, ("wt", b))
            for tb in range(ntb):
                evac(ct, m, tb, psb[slot * ntb + tb], ("ps", slot * ntb + tb))
            if after is not None:
                after(ct)


def rms_to_bf16(P, x32, vec, c_scale, c_shift, c_nw, hb, ones32, sq, rstd, pss, avec, final_out=None):
    for kc in range(KC):
        s = sq[kc % 2]
        P.op("act", lambda e, s=s, kc=kc: e.activation(out=s[:], in_=x32[:, kc, :], func=AF.Square),
             reads=[("x32", kc)], writes=[("sq", kc % 2)])
        for tb in range(T // 512):
            P.op("pe", lambda e, s=s, tb=tb, kc=kc: e.matmul(pss[tb][:], ones32[:], s[:, tb * 512:(tb + 1) * 512],
                                                            start=(kc == 0), stop=(kc == KC - 1)),
                 reads=[("sq", kc % 2), "ones32"], writes=[("pss", tb)])
    for tb in range(T // 512):
        P.op("act", lambda e, tb=tb: e.activation(out=rstd[:, tb * 512:(tb + 1) * 512], in_=pss[tb][:], func=AF.Sqrt,
                                                  bias=epsb[0][:], scale=1.0 / D),
             reads=[("pss", tb), "epsb"], writes=[("rstd", tb)])
        P.op("dve", lambda e, tb=tb: e.reciprocal(out=rstd[:, tb * 512:(tb + 1) * 512], in_=rstd[:, tb * 512:(tb + 1) * 512]),
             reads=[("rstd", tb)], writes=[("rstd", tb)])
    if c_scale is not None:
        P.op("dve", lambda e: e.scalar_tensor_tensor(out=avec[:], in0=vec[:, c_scale:c_scale + 16], scalar=1.0,
                                                     in1=vec[:, c_nw:c_nw + 16], op0=ALU.add, op1=ALU.mult),
             reads=["vec"], writes=["avec"])
    else:
        P.op("dve", lambda e: e.tensor_copy(out=avec[:], in_=vec[:, c_nw:c_nw + 16]), reads=["vec"], writes=["avec"])
    for kc in range(KC):
        s = sq[kc % 2]
        if final_out is None:
            P.op("dve", lambda e, s=s, kc=kc: e.scalar_tensor_tensor(out=s[:], in0=x32[:, kc, :], scalar=avec[:, kc:kc + 1],
                                                                     in1=rstd[:], op0=ALU.mult, op1=ALU.mult),
                 reads=[("x32", kc), "avec", ("rstd", 0), ("rstd", 1)], writes=[("sq", kc % 2)])
            P.op("act", lambda e, s=s, kc=kc: e.activation(out=hb[:, kc, :], in_=s[:], func=AF.Identity,
                                                           bias=vec[:, c_shift + kc:c_shift + kc + 1], scale=1.0),
                 reads=[("sq", kc % 2), "vec"], writes=[("hb", kc)])
        else:
            final_out(kc, s)


epsb = [None]


def consts(P, need_ident=False):
    ones32 = P.sb("ones32", [128, 128], F32)
    P.op("dve", lambda e: e.memset(ones32[:], 1.0), writes=["ones32"])
    eb = P.sb("epsb", [128, 1], F32)
    P.op("dve", lambda e: e.memset(eb[:], EPS), writes=["epsb"])
    epsb[0] = eb
    return ones32


def load_x(P, xT, x32):
    for kc in range(KC):
        P.dma("sp", lambda e, kc=kc: e.dma_start(out=x32[:, kc, :], in_=xT[kc * 128:(kc + 1) * 128, :]),
              writes=[("x32", kc)])


def build_pre():
    nc, es, P = new_prog()
    xT = nc.dram_tensor("xT", [D, T], F32, kind="ExternalInput").ap()
    vecd = nc.dram_tensor("vec", [128, NV], F32, kind="ExternalInput").ap()
    w_in = nc.dram_tensor("w_in", [D, INW], F32, kind="ExternalInput").ap()
    projT = nc.dram_tensor("projT", [INW, T], F32, kind="ExternalOutput").ap()
    with es:
        x32 = P.sb("x32", [128, KC, T], F32)
        hb = P.sb("hb", [128, KC, T], BF16)
        vec = P.sb("vec", [128, NV], F32)
        avec = P.sb("avec", [128, 16], F32)
        sq = [P.sb(f"sq{i}", [128, T], F32) for i in range(2)]
        rstd = P.sb("rstd", [128, T], F32)
        wbufs = [P.sb(f"wb{i}", [128, 8192], BF16) for i in range(2)]
        ob = [P.sb(f"ob{i}", [128, T], F32) for i in range(2)]
        pss = [P.ps(f"pss{i}", [128, 512]) for i in range(2)]
        psb = [P.ps(f"psb{i}", [128, 512]) for i in range(4)]
        ones32 = consts(P)
        P.dma("sp", lambda e: e.dma_start(out=vec[:], in_=vecd), writes=["vec"])
        load_x(P, xT, x32)
        rms_to_bf16(P, x32, vec, V_SC1, V_SH1, V_NMIX, hb, ones32, sq, rstd, pss, avec)

        def rhs(kc, tb):
            return hb[:, kc, tb * 512:(tb + 1) * 512], [("hb", kc)]

        def evac(ct, m, tb, pst, pkey):
            o = ob[ct % 2]
            eng = "dve" if tb == 0 else "act"
            if eng == "dve":
                P.op("dve", lambda e: e.tensor_copy(out=o[0:m, tb * 512:(tb + 1) * 512], in_=pst[0:m, :]),
                     reads=[pkey], writes=[("ob", ct % 2, tb)])
            else:
                P.op("act", lambda e: e.activation(out=o[0:m, tb * 512:(tb + 1) * 512], in_=pst[0:m, :], func=AF.Identity),
                     reads=[pkey], writes=[("ob", ct % 2, tb)])
            if tb == T // 512 - 1:
                P.dma("sp", lambda e: e.dma_start(out=projT[ct * 128:ct * 128 + m, :], in_=o[0:m, :]),
                      reads=[("ob", ct % 2, 0), ("ob", ct % 2, 1)])
        gemm(P, w_in, D, INW, 512, rhs, evac, wbufs, psb)
        P.emit()
    return nc


def build_merge():
    nc, es, P = new_prog()
    xT = nc.dram_tensor("xT", [D, T], F32, kind="ExternalInput").ap()
    vecd = nc.dram_tensor("vec", [128, NV], F32, kind="ExternalInput").ap()
    oT = [nc.dram_tensor(n, [1024, T], F32, kind="ExternalInput").ap() for n in ("oaT", "obT", "ocT")]
    gT = nc.dram_tensor("gT", [3 * D, T], F32, kind="ExternalInput").ap()
    wbr = [nc.dram_tensor(n, [1024, D], F32, kind="ExternalInput").ap() for n in ("wba", "wbb", "wbc")]
    w_out = nc.dram_tensor("w_out", [D, D], F32, kind="ExternalInput").ap()
    x1T = nc.dram_tensor("x1T", [D, T], F32, kind="ExternalOutput").ap()
    with es:
        ob3 = [P.sb(f"o3_{i}", [128, 8, T], BF16) for i in range(3)]
        mb = P.sb("mb", [128, KC, T], BF16)
        vec = P.sb("vec", [128, NV], F32)
        wbufs = [P.sb(f"wb{i}", [128, 8192], BF16) for i in range(2)]
        wbt = [[P.sb(f"wbt{j}_{i}", [128, 8, 128], BF16) for i in range(3)] for j in range(2)]
        gt = [[P.sb(f"gt{j}_{i}", [128, T], F32) for i in range(3)] for j in range(2)]
        acc = [P.sb(f"acc{i}", [128, 512], F32) for i in range(2)]
        tmp = [P.sb(f"tmp{i}", [128, 512], F32) for i in range(2)]
        xt = [P.sb(f"xt{i}", [128, T], F32) for i in range(2)]
        pb = [P.ps(f"pb{i}", [128, 512]) for i in range(8)]
        P.dma("sp", lambda e: e.dma_start(out=vec[:], in_=vecd), writes=["vec"])
        for i in range(3):
            for kc in range(8):
                P.dma("pool", lambda e, i=i, kc=kc: e.dma_start(out=ob3[i][:, kc, :], in_=oT[i][kc * 128:(kc + 1) * 128, :]),
                      writes=[("o3", i, kc)])
        for ct in range(16):
            j = ct % 2
            for i in range(3):
                P.dma("pool", lambda e, i=i, j=j, ct=ct: e.dma_start(
                    out=wbt[j][i][:], in_=wbr[i][:, ct * 128:(ct + 1) * 128].rearrange("(kc p) c -> p kc c", p=128)),
                    writes=[("wbt", j, i)])
                P.dma("sp", lambda e, i=i, j=j, ct=ct: e.dma_start(out=gt[j][i][:], in_=gT[i * D + ct * 128:i * D + (ct + 1) * 128, :]),
                      writes=[("gt", j, i)])
                P.op("act", lambda e, i=i, j=j: e.activation(out=gt[j][i][:], in_=gt[j][i][:], func=AF.Sigmoid),
                     reads=[("gt", j, i)], writes=[("gt", j, i)])
            for i in range(3):
                for kc in range(8):
                    for tb in range(2):
                        P.op("pe", lambda e, i=i, j=j, kc=kc, tb=tb: e.matmul(
                            pb[i * 2 + tb][:], wbt[j][i][:, kc, :], ob3[i][:, kc, tb * 512:(tb + 1) * 512],
                            start=(kc == 0), stop=(kc == 7)),
                            reads=[("wbt", j, i), ("o3", i, kc)], writes=[("ps", i * 2 + tb)])
            for tb in range(2):
                sl = slice(tb * 512, (tb + 1) * 512)
                P.op("dve", lambda e, tb=tb, sl=sl, j=j: e.tensor_tensor(out=acc[tb][:], in0=pb[tb][:], in1=gt[j][0][:, sl], op=ALU.mult),
                     reads=[("ps", tb), ("gt", j, 0)], writes=[("acc", tb)])
                P.op("dve", lambda e, tb=tb, sl=sl, j=j: e.tensor_tensor(out=tmp[tb][:], in0=pb[2 + tb][:], in1=gt[j][1][:, sl], op=ALU.mult),
                     reads=[("ps", 2 + tb), ("gt", j, 1)], writes=[("tmp", tb)])
                P.op("dve", lambda e, tb=tb: e.tensor_tensor(out=acc[tb][:], in0=acc[tb][:], in1=tmp[tb][:], op=ALU.add),
                     reads=[("acc", tb), ("tmp", tb)], writes=[("acc", tb)])
                P.op("dve", lambda e, tb=tb, sl=sl, j=j: e.tensor_tensor(out=tmp[tb][:], in0=pb[4 + tb][:], in1=gt[j][2][:, sl], op=ALU.mult),
                     reads=[("ps", 4 + tb), ("gt", j, 2)], writes=[("tmp", tb)])
                P.op("dve", lambda e, tb=tb, sl=sl, ct=ct: e.tensor_tensor(out=mb[:, ct, sl], in0=acc[tb][:], in1=tmp[tb][:], op=ALU.add),
                     reads=[("acc", tb), ("tmp", tb)], writes=[("mb", ct)])

        def rhs2(kc, tb):
            return mb[:, kc, tb * 512:(tb + 1) * 512], [("mb", kc)]

        def evac2(ct, m, tb, pst, pkey):
            j = ct % 2
            sl = slice(tb * 512, (tb + 1) * 512)
            if tb == 0:
                P.dma("sp", lambda e: e.dma_start(out=xt[j][:], in_=xT[ct * 128:(ct + 1) * 128, :]), writes=[("xt", j)])
            P.op("dve", lambda e: e.scalar_tensor_tensor(out=xt[j][:, sl], in0=pst[:], scalar=vec[:, V_G1 + ct:V_G1 + ct + 1],
                                                         in1=xt[j][:, sl], op0=ALU.mult, op1=ALU.add),
                 reads=[pkey, ("xt", j), "vec"], writes=[("xt", j)])
            if tb == 1:
                P.dma("sp", lambda e: e.dma_start(out=x1T[ct * 128:(ct + 1) * 128, :], in_=xt[j][:]), reads=[("xt", j)])
        gemm(P, w_out, D, D, 512, rhs2, evac2, wbufs, pb[0:4])
        P.emit()
    return nc


def build_mlp(final):
    nc, es, P = new_prog()
    DFF = 4 * D
    xT = nc.dram_tensor("xT", [D, T], F32, kind="ExternalInput").ap()
    vecd = nc.dram_tensor("vec", [128, NV], F32, kind="ExternalInput").ap()
    w1 = nc.dram_tensor("w1", [D, DFF], F32, kind="ExternalInput").ap()
    w2 = nc.dram_tensor("w2", [DFF, D], F32, kind="ExternalInput").ap()
    x2T = nc.dram_tensor("x2T", [D, T], F32, kind="ExternalOutput").ap()
    with es:
        x32 = P.sb("x32", [128, KC, T], F32)
        hb = P.sb("hb", [128, KC, T], BF16)
        hid = P.sb("hid", [128, 16, T], BF16)
        vec = P.sb("vec", [128, NV], F32)
        avec = P.sb("avec", [128, 16], F32)
        sq = [P.sb(f"sq{i}", [128, T], F32) for i in range(2)]
        rstd = P.sb("rstd", [128, T], F32)
        wbufs = [P.sb(f"wb{i}", [128, 8192], BF16) for i in range(2)]
        rl = [P.sb(f"rl{i}", [128, 512], F32) for i in range(2)]
        pss = [P.ps(f"pss{i}", [128, 512]) for i in range(2)]
        psb = [P.ps(f"psb{i}", [128, 512]) for i in range(4)]
        ones32 = consts(P)
        P.dma("sp", lambda e: e.dma_start(out=vec[:], in_=vecd), writes=["vec"])
        load_x(P, xT, x32)
        rms_to_bf16(P, x32, vec, V_SC2, V_SH2, V_NMLP, hb, ones32, sq, rstd, pss, avec)
        rctr = [0]
        for q in range(4):
            def rhs(kc, tb):
                return hb[:, kc, tb * 512:(tb + 1) * 512], [("hb", kc)]

            def evac(ct, m, tb, pst, pkey, q=q):
                r = rctr[0] % 2
                rctr[0] += 1
                cl = ct
                P.op("act", lambda e: e.activation(out=rl[r][:], in_=pst[:], func=AF.Relu), reads=[pkey], writes=[("rl", r)])
                P.op("dve", lambda e: e.tensor_tensor(out=hid[:, cl, tb * 512:(tb + 1) * 512], in0=rl[r][:], in1=rl[r][:], op=ALU.mult),
                     reads=[("rl", r)], writes=[("hid", cl)])
            gemm(P, w1[:, q * 2048:(q + 1) * 2048], D, 2048, 512, rhs, evac, wbufs, psb)

            def rhs2(kc, tb):
                return hid[:, kc, tb * 512:(tb + 1) * 512], [("hid", kc)]

            def evac2(ct, m, tb, pst, pkey):
                sl = slice(tb * 512, (tb + 1) * 512)
                P.op("dve", lambda e: e.scalar_tensor_tensor(out=x32[:, ct, sl], in0=pst[:], scalar=vec[:, V_G2 + ct:V_G2 + ct + 1],
                                                             in1=x32[:, ct, sl], op0=ALU.mult, op1=ALU.add),
                     reads=[pkey, ("x32", ct), "vec"], writes=[("x32", ct)])
            gemm(P, w2, 2048, D, 512, rhs2, evac2, wbufs, psb, k0=q * 2048)
        if not final:
            for kc in range(KC):
                P.dma("sp", lambda e, kc=kc: e.dma_start(out=x2T[kc * 128:(kc + 1) * 128, :], in_=x32[:, kc, :]),
                      reads=[("x32", kc)])
        else:
            def final_out(kc, s):
                P.op("dve", lambda e: e.scalar_tensor_tensor(out=s[:], in0=x32[:, kc, :], scalar=avec[:, kc:kc + 1],
                                                             in1=rstd[:], op0=ALU.mult, op1=ALU.mult),
                     reads=[("x32", kc), "avec", ("rstd", 0), ("rstd", 1)], writes=[("sq", kc % 2)])
                P.dma("sp", lambda e: e.dma_start(out=x2T[kc * 128:(kc + 1) * 128, :], in_=s[:]), reads=[("sq", kc % 2)])
            rms_to_bf16(P, x32, vec, None, None, V_NFIN, None, ones32, sq, rstd, pss, avec, final_out=final_out)
        P.emit()
    return nc


def build_convc():
    nc, es, P = new_prog()
    TH = T + 32
    cgT = nc.dram_tensor("cgT", [2048, TH], F32, kind="ExternalInput").ap()
    cwd = nc.dram_tensor("cw", [128, 8 * 31], F32, kind="ExternalInput").ap()
    cvd = nc.dram_tensor("cvec", [128, 24], F32, kind="ExternalInput").ap()
    ocT = nc.dram_tensor("ocT", [1024, T], F32, kind="ExternalOutput").ap()
    with es:
        at = [P.sb(f"at{i}", [128, TH], F32) for i in range(2)]
        bt = [P.sb(f"bt{i}", [128, TH], F32) for i in range(2)]
        cv = P.sb("cv", [128, 8, T], F32)
        cw = P.sb("cw", [128, 8 * 31], F32)
        cvec = P.sb("cvec", [128, 24], F32)
        sq = [P.sb(f"sq{i}", [128, T], F32) for i in range(2)]
        mean = P.sb("mean", [128, T], F32)
        rstd = P.sb("rstd", [128, T], F32)
        eps5 = P.sb("eps5", [128, 1], F32)
        ps1 = [P.ps(f"ps1_{i}", [128, 512]) for i in range(2)]
        ps2 = [P.ps(f"ps2_{i}", [128, 512]) for i in range(2)]
        ones32 = consts(P)
        P.op("dve", lambda e: e.memset(eps5[:], 1e-5), writes=["eps5"])
        P.dma("sp", lambda e: e.dma_start(out=cw[:], in_=cwd), writes=["cw"])
        P.dma("sp", lambda e: e.dma_start(out=cvec[:], in_=cvd), writes=["cvec"])
        for ch in range(8):
            j = ch % 2
            P.dma("sp", lambda e, ch=ch, j=j: e.dma_start(out=at[j][:], in_=cgT[ch * 128:(ch + 1) * 128, :]), writes=[("at", j)])
            P.dma("sp", lambda e, ch=ch, j=j: e.dma_start(out=bt[j][:], in_=cgT[1024 + ch * 128:1024 + (ch + 1) * 128, :]), writes=[("bt", j)])
            P.op("act", lambda e, j=j: e.activation(out=bt[j][:], in_=bt[j][:], func=AF.Sigmoid), reads=[("bt", j)], writes=[("bt", j)])
            P.op("pool", lambda e, j=j: e.tensor_tensor(out=at[j][:], in0=at[j][:], in1=bt[j][:], op=ALU.mult),
                 reads=[("at", j), ("bt", j)], writes=[("at", j)])
            P.op("dve", lambda e, ch=ch, j=j: e.tensor_scalar(out=cv[:, ch, :], in0=at[j][:, 2:2 + T], scalar1=cw[:, ch * 31:ch * 31 + 1],
                                                              scalar2=cvec[:, ch:ch + 1], op0=ALU.mult, op1=ALU.add),
                 reads=[("at", j), "cw", "cvec"], writes=[("cv", ch)])
            for k in range(1, 31):
                P.op("dve", lambda e, ch=ch, j=j, k=k: e.scalar_tensor_tensor(
                    out=cv[:, ch, :], in0=at[j][:, 2 + k:2 + k + T], scalar=cw[:, ch * 31 + k:ch * 31 + k + 1],
                    in1=cv[:, ch, :], op0=ALU.mult, op1=ALU.add),
                    reads=[("at", j), "cw", ("cv", ch)], writes=[("cv", ch)])
            s = sq[j]
            P.op("act", lambda e, s=s, ch=ch: e.activation(out=s[:], in_=cv[:, ch, :], func=AF.Square),
                 reads=[("cv", ch)], writes=[("sq", j)])
            for tb in range(2):
                P.op("pe", lambda e, tb=tb, ch=ch: e.matmul(ps1[tb][:], ones32[:], cv[:, ch, tb * 512:(tb + 1) * 512],
                                                            start=(ch == 0), stop=(ch == 7)),
                     reads=[("cv", ch), "ones32"], writes=[("ps1", tb)])
                P.op("pe", lambda e, tb=tb, ch=ch, s=s: e.matmul(ps2[tb][:], ones32[:], s[:, tb * 512:(tb + 1) * 512],
                                                                 start=(ch == 0), stop=(ch == 7)),
                     reads=[("sq", j), "ones32"], writes=[("ps2", tb)])
        for tb in range(2):
            sl = slice(tb * 512, (tb + 1) * 512)
            P.op("act", lambda e, tb=tb, sl=sl: e.activation(out=mean[:, sl], in_=ps1[tb][:], func=AF.Identity, scale=1.0 / 1024),
                 reads=[("ps1", tb)], writes=[("mean", tb)])
            P.op("dve", lambda e, tb=tb, sl=sl: e.tensor_tensor(out=rstd[:, sl], in0=mean[:, sl], in1=mean[:, sl], op=ALU.mult),
                 reads=[("mean", tb)], writes=[("rstd", tb)])
            P.op("dve", lambda e, tb=tb, sl=sl: e.scalar_tensor_tensor(out=rstd[:, sl], in0=ps2[tb][:], scalar=1.0 / 1024, in1=rstd[:, sl],
                                                                       op0=ALU.mult, op1=ALU.subtract),
                 reads=[("ps2", tb), ("rstd", tb)], writes=[("rstd", tb)])
            P.op("act", lambda e, tb=tb, sl=sl: e.activation(out=rstd[:, sl], in_=rstd[:, sl], func=AF.Sqrt, bias=eps5[:], scale=1.0),
                 reads=[("rstd", tb), "eps5"], writes=[("rstd", tb)])
            P.op("dve", lambda e, tb=tb, sl=sl: e.reciprocal(out=rstd[:, sl], in_=rstd[:, sl]),
                 reads=[("rstd", tb)], writes=[("rstd", tb)])
        for ch in range(8):
            s = sq[ch % 2]
            P.op("dve", lambda e, ch=ch, s=s: e.tensor_tensor(out=s[:], in0=cv[:, ch, :], in1=mean[:], op=ALU.subtract),
                 reads=[("cv", ch), ("mean", 0), ("mean", 1)], writes=[("sq", ch % 2)])
            P.op("pool", lambda e, ch=ch, s=s: e.tensor_tensor(out=s[:], in0=s[:], in1=rstd[:], op=ALU.mult),
                 reads=[("sq", ch % 2), ("rstd", 0), ("rstd", 1)], writes=[("sq", ch % 2)])
            P.op("act", lambda e, ch=ch, s=s: e.activation(out=s[:], in_=s[:], func=AF.Silu, bias=cvec[:, 16 + ch:17 + ch],
                                                           scale=cvec[:, 8 + ch:9 + ch]),
                 reads=[("sq", ch % 2), "cvec"], writes=[("sq", ch % 2)])
            P.dma("sp", lambda e, ch=ch, s=s: e.dma_start(out=ocT[ch * 128:(ch + 1) * 128, :], in_=s[:]), reads=[("sq", ch % 2)])
        P.emit()
    return nc


def build_retln():
    nc, es, P = new_prog()
    oretT = nc.dram_tensor("oretT", [1024, T], F32, kind="ExternalInput").ap()
    bgT = nc.dram_tensor("bgT", [1024, T], F32, kind="ExternalInput").ap()
    obT = nc.dram_tensor("obT", [1024, T], F32, kind="ExternalOutput").ap()
    with es:
        xt = [[P.sb(f"xt{j}_{c}", [128, T], F32) for c in range(2)] for j in range(4)]
        gt = [P.sb(f"gt{i}", [128, T], F32) for i in range(8)]
        sq = [P.sb(f"sq{i}", [128, T], F32) for i in range(2)]
        mean = P.sb("mean", [128, T], F32)
        rstd = P.sb("rstd", [128, T], F32)
        eps5 = P.sb("eps5", [128, 1], F32)
        ps1 = [P.ps(f"ps1_{i}", [128, 512]) for i in range(2)]
        ps2 = [P.ps(f"ps2_{i}", [128, 512]) for i in range(2)]
        ones32 = consts(P)
        P.op("dve", lambda e: e.memset(eps5[:], 1e-5), writes=["eps5"])
        gi = 0
        for h in range(4):
            j = h
            for c in range(2):
                P.dma("sp", lambda e, h=h, c=c, j=j: e.dma_start(out=xt[j][c][:], in_=oretT[(2 * h + c) * 128:(2 * h + c + 1) * 128, :]),
                      writes=[("xt", j, c)])
                P.op("act", lambda e, j=j, c=c: e.activation(out=sq[c][:], in_=xt[j][c][:], func=AF.Square),
                     reads=[("xt", j, c)], writes=[("sq", c)])
                for tb in range(2):
                    sl = slice(tb * 512, (tb + 1) * 512)
                    P.op("pe", lambda e, j=j, c=c, tb=tb, sl=sl: e.matmul(ps1[tb][:], ones32[:], xt[j][c][:, sl], start=(c == 0), stop=(c == 1)),
                         reads=[("xt", j, c), "ones32"], writes=[("ps1", tb)])
                    P.op("pe", lambda e, c=c, tb=tb, sl=sl: e.matmul(ps2[tb][:], ones32[:], sq[c][:, sl], start=(c == 0), stop=(c == 1)),
                         reads=[("sq", c), "ones32"], writes=[("ps2", tb)])
            for tb in range(2):
                sl = slice(tb * 512, (tb + 1) * 512)
                P.op("act", lambda e, tb=tb, sl=sl: e.activation(out=mean[:, sl], in_=ps1[tb][:], func=AF.Identity, scale=1.0 / 256),
                     reads=[("ps1", tb)], writes=[("mean", tb)])
                P.op("dve", lambda e, tb=tb, sl=sl: e.tensor_tensor(out=rstd[:, sl], in0=mean[:, sl], in1=mean[:, sl], op=ALU.mult),
                     reads=[("mean", tb)], writes=[("rstd", tb)])
                P.op("dve", lambda e, tb=tb, sl=sl: e.scalar_tensor_tensor(out=rstd[:, sl], in0=ps2[tb][:], scalar=1.0 / 256, in1=rstd[:, sl],
                                                                           op0=ALU.mult, op1=ALU.subtract),
                     reads=[("ps2", tb), ("rstd", tb)], writes=[("rstd", tb)])
                P.op("act", lambda e, tb=tb, sl=sl: e.activation(out=rstd[:, sl], in_=rstd[:, sl], func=AF.Ln, bias=eps5[:], scale=1.0),
                     reads=[("rstd", tb), "eps5"], writes=[("rstd", tb)])
                P.op("act", lambda e, tb=tb, sl=sl: e.activation(out=rstd[:, sl], in_=rstd[:, sl], func=AF.Exp, scale=-0.5),
                     reads=[("rstd", tb)], writes=[("rstd", tb)])
            mk = [("mean", 0), ("mean", 1)]
            rk = [("rstd", 0), ("rstd", 1)]
            for c in range(2):
                g = gt[gi % 8]
                gk = ("gt", gi % 8)
                gi += 1
                P.dma("sp", lambda e, h=h, c=c, g=g: e.dma_start(out=g[:], in_=bgT[(2 * h + c) * 128:(2 * h + c + 1) * 128, :]), writes=[gk])
                P.op("act", lambda e, g=g: e.activation(out=g[:], in_=g[:], func=AF.Silu), reads=[gk], writes=[gk])
                x = xt[j][c]
                P.op("dve", lambda e, x=x: e.tensor_tensor(out=x[:], in0=x[:], in1=mean[:], op=ALU.subtract),
                     reads=[("xt", j, c)] + mk, writes=[("xt", j, c)])
                P.op("pool", lambda e, x=x: e.tensor_tensor(out=x[:], in0=x[:], in1=rstd[:], op=ALU.mult),
                     reads=[("xt", j, c)] + rk, writes=[("xt", j, c)])
                P.op("dve", lambda e, x=x, g=g: e.tensor_tensor(out=x[:], in0=x[:], in1=g[:], op=ALU.mult),
                     reads=[("xt", j, c), gk], writes=[("xt", j, c)])
                P.dma("sp", lambda e, h=h, c=c, x=x: e.dma_start(out=obT[(2 * h + c) * 128:(2 * h + c + 1) * 128, :], in_=x[:]),
                      reads=[("xt", j, c)])
        P.emit()
    return nc


def build_ada():
    nc, es, P = new_prog()
    NCOL = 1536
    cTd = nc.dram_tensor("cT", [128, 16], F32, kind="ExternalInput").ap()
    wad = nc.dram_tensor("wa", [2, D, NCOL], F32, kind="ExternalInput").ap()
    bad = nc.dram_tensor("ba", [2, NCOL], F32, kind="ExternalInput").ap()
    modd = nc.dram_tensor("mod", [2, NCOL], F32, kind="ExternalOutput").ap()
    with es:
        ca = P.sb("ca", [128, 16], F32)
        wt = [P.sb(f"wt{i}", [128, 16, 256], F32) for i in range(2)]
        bt = P.sb("bt", [1, 2 * NCOL], F32)
        ot = P.sb("ot", [1, 2 * NCOL], F32)
        ps = [P.ps(f"ps{i}", [128, 512]) for i in range(2)]
        P.dma("sp", lambda e: e.dma_start(out=ca[:], in_=cTd), writes=["ca"])
        P.op("act", lambda e: e.activation(out=ca[:], in_=ca[:], func=AF.Silu), reads=["ca"], writes=["ca"])
        for l in range(2):
            P.dma("sp", lambda e, l=l: e.dma_start(out=bt[0:1, l * NCOL:(l + 1) * NCOL], in_=bad[l:l + 1, :]), writes=[("bt", l)])
        i = 0
        for l in range(2):
            for cb in range(NCOL // 256):
                b = i % 2
                i += 1
                P.dma("sp", lambda e, l=l, cb=cb, b=b: e.dma_start(
                    out=wt[b][:], in_=wad[l, :, cb * 256:(cb + 1) * 256].rearrange("(kc p) c -> p kc c", p=128)), writes=[("wt", b)])
                for kc in range(16):
                    P.op("pe", lambda e, b=b, kc=kc: e.matmul(ps[b][0:1, 0:256], ca[:, kc:kc + 1], wt[b][:, kc, :], start=(kc == 0), stop=(kc == 15)),
                         reads=["ca", ("wt", b)], writes=[("ps", b)])
                o0 = l * NCOL + cb * 256
                P.op("dve", lambda e, b=b, o0=o0: e.tensor_tensor(out=ot[0:1, o0:o0 + 256], in0=ps[b][0:1, 0:256], in1=bt[0:1, o0:o0 + 256], op=ALU.add),
                     reads=[("ps", b), ("bt", l)], writes=[("ot", l)])
        for l in range(2):
            P.dma("sp", lambda e, l=l: e.dma_start(out=modd[l:l + 1, :], in_=ot[0:1, l * NCOL:(l + 1) * NCOL]), reads=[("ot", l)])
        P.emit()
    return nc


NRT = 104
PRE2_ROWS = NRT * 128 + 16


def pre2_tilemap():
    tm = []
    ri = 0
    for p in range(120):
        if p < 56 and p % 7 in (0, 1):
            tm.append(("a" if p % 7 == 0 else "b", p // 7))
        else:
            tm.append(("r", ri))
            ri += 1
    tm.append(("r", ri))
    return tm


def build_pre2():
    nc, es, P = new_prog()
    TH = T + 32
    xT = nc.dram_tensor("xT", [D, T], F32, kind="ExternalInput").ap()
    xhT = nc.dram_tensor("xhT", [D, 32], F32, kind="ExternalInput").ap()
    vecd = nc.dram_tensor("vec", [128, NV], F32, kind="ExternalInput").ap()
    flagd = nc.dram_tensor("flag", [128, 1], F32, kind="ExternalInput").ap()
    cwd = nc.dram_tensor("cw", [128, 8 * 31], F32, kind="ExternalInput").ap()
    cvd = nc.dram_tensor("cvec", [128, 24], F32, kind="ExternalInput").ap()
    w_in = nc.dram_tensor("w_in", [D, INW], F32, kind="ExternalInput").ap()
    projT = nc.dram_tensor("projT", [PRE2_ROWS, T], F32, kind="ExternalOutput").ap()
    ocT = nc.dram_tensor("ocT", [1024, T], F32, kind="ExternalOutput").ap()
    tm = pre2_tilemap()
    HOOK = 62
    with es:
        x32 = P.sb("x32", [128, KC, T], F32)
        xh32 = P.sb("xh32", [128, KC, 32], F32)
        hb = P.sb("hb", [128, KC, T], BF16)
        hbh = P.sb("hbh", [128, KC, 32], BF16)
        vec = P.sb("vec", [128, NV], F32)
        avec = P.sb("avec", [128, 16], F32)
        flag = P.sb("flag", [128, 1], F32)
        cw = P.sb("cw", [128, 8 * 31], F32)
        cvec = P.sb("cvec", [128, 24], F32)
        eps5 = P.sb("eps5", [128, 1], F32)
        sq = [P.sb(f"sq{i}", [128, T], F32) for i in range(2)]
        lt = [P.sb(f"lt{i}", [128, T], F32) for i in range(2)]
        rstd = P.sb("rstd", [128, T], F32)
        rstdh = P.sb("rstdh", [128, 32], F32)
        sqh = P.sb("sqh", [128, 32], F32)
        wbufs = [P.sb(f"wb{i}", [128, 8192], BF16) for i in range(2)]
        ob = [P.sb(f"ob{i}", [128, T], F32) for i in range(2)]
        at = [P.sb(f"at{i}", [128, TH], F32) for i in range(2)]
        bt = [P.sb(f"bt{i}", [128, TH], F32) for i in range(2)]
        pss = [P.ps(f"pss{i}", [128, 512]) for i in range(2)]
        psb = [P.ps(f"psb{i}", [128, 512]) for i in range(4)]
        psh = [P.ps(f"psh{i}", [128, 512]) for i in range(2)]
        ones32 = consts(P)
        P.op("dve", lambda e: e.memset(eps5[:], 1e-5), writes=["eps5"])
        for (t_, d_, k_) in ((vec, vecd, "vec"), (flag, flagd, "flag"), (cw, cwd, "cw"), (cvec, cvd, "cvec")):
            P.dma("sp", lambda e, t_=t_, d_=d_: e.dma_start(out=t_[:], in_=d_), writes=[k_])
        P.dma("sp", lambda e: e.dma_start(out=xh32[:], in_=xhT.rearrange("(kc p) t -> p kc t", p=128)), writes=["xh32"])
        load_x(P, xT, x32)
        rms_to_bf16(P, x32, vec, V_SC1, V_SH1, V_NMIX, hb, ones32, sq, rstd, pss, avec)
        for kc in range(KC):
            P.op("act", lambda e, kc=kc: e.activation(out=sqh[:], in_=xh32[:, kc, :], func=AF.Square), reads=["xh32"], writes=["sqh"])
            P.op("pe", lambda e, kc=kc: e.matmul(psh[0][:, 0:32], ones32[:], sqh[:], start=(kc == 0), stop=(kc == KC - 1)),
                 reads=["sqh", "ones32"], writes=[("psh", 0)])
        P.op("act", lambda e: e.activation(out=rstdh[:], in_=psh[0][:, 0:32], func=AF.Sqrt, bias=epsb[0][:], scale=1.0 / D),
             reads=[("psh", 0), "epsb"], writes=["rstdh"])
        P.op("dve", lambda e: e.reciprocal(out=rstdh[:], in_=rstdh[:]), reads=["rstdh"], writes=["rstdh"])
        for kc in range(KC):
            P.op("dve", lambda e, kc=kc: e.scalar_tensor_tensor(out=sqh[:], in0=xh32[:, kc, :], scalar=avec[:, kc:kc + 1], in1=rstdh[:],
                                                               op0=ALU.mult, op1=ALU.mult), reads=["xh32", "avec", "rstdh"], writes=["sqh"])
            P.op("act", lambda e, kc=kc: e.activation(out=hbh[:, kc, :], in_=sqh[:], func=AF.Identity,
                                                      bias=vec[:, V_SH1 + kc:V_SH1 + kc + 1], scale=1.0),
                 reads=["sqh", "vec"], writes=["hbh"])

        def rhs(kc, tb):
            return hb[:, kc, tb * 512:(tb + 1) * 512], [("hb", kc)]

        def extra(ct, m, wt, ci, slot, wkey):
            kind, idx = tm[ct]
            if kind == "r":
                return
            for kc in range(KC):
                P.op("pe", lambda e, kc=kc: e.matmul(psh[slot][:, 0:32], wt[:, kc, ci * 128:ci * 128 + 128], hbh[:, kc, :],
                                                     start=(kc == 0), stop=(kc == KC - 1)),
                     reads=[wkey, "hbh"], writes=[("psh", slot)])

        def evac(ct, m, tb, pst, pkey):
            kind, idx = tm[ct]
            sl = slice(tb * 512, (tb + 1) * 512)
            if kind == "r":
                o = ob[idx % 2]
                okey = ("ob", idx % 2, tb)
                if ct < HOOK or tb == 1:
                    P.op("act", lambda e: e.activation(out=o[0:m, sl], in_=pst[0:m, :], func=AF.Identity), reads=[pkey], writes=[okey])
                else:
                    P.op("dve", lambda e: e.tensor_copy(out=o[0:m, sl], in_=pst[0:m, :]), reads=[pkey], writes=[okey])
                if tb == 1:
                    P.dma("sp", lambda e: e.dma_start(out=projT[idx * 128:idx * 128 + m, :], in_=o[0:m, :]),
                          reads=[("ob", idx % 2, 0), ("ob", idx % 2, 1)])
                return
            ch = idx
            j = ch % 2
            dst = at[j] if kind == "a" else bt[j]
            dkey = ("at", j) if kind == "a" else ("bt", j)
            P.op("act", lambda e: e.activation(out=dst[:, 32 + tb * 512:32 + (tb + 1) * 512], in_=pst[:], func=AF.Identity),
                 reads=[pkey], writes=[dkey])
            if tb == 1:
                slot = None
                hs_ = (P.psctr - 1) % 2
                P.op("act", lambda e: e.activation(out=dst[:, 0:32], in_=psh[hs_][:, 0:32], func=AF.Identity),
                     reads=[("psh", hs_)], writes=[dkey])
                if kind == "b":
                    P.op("act", lambda e: e.activation(out=bt[j][:], in_=bt[j][:], func=AF.Sigmoid), reads=[("bt", j)], writes=[("bt", j)])
                    P.op("dve", lambda e: e.tensor_scalar(out=at[j][:, 0:32], in0=at[j][:, 0:32], scalar1=flag[:, 0:1], scalar2=None, op0=ALU.mult),
                         reads=[("at", j), "flag"], writes=[("at", j)])
                    P.op("dve", lambda e: e.tensor_tensor(out=at[j][:], in0=at[j][:], in1=bt[j][:], op=ALU.mult),
                         reads=[("at", j), ("bt", j)], writes=[("at", j)])
                    P.op("dve", lambda e: e.tensor_scalar(out=x32[:, ch, :], in0=at[j][:, 2:2 + T], scalar1=cw[:, ch * 31:ch * 31 + 1],
                                                          scalar2=cvec[:, ch:ch + 1], op0=ALU.mult, op1=ALU.add),
                         reads=[("at", j), "cw", "cvec"], writes=[("x32", ch)])
                    for k in range(1, 31):
                        P.op("dve", lambda e, k=k: e.scalar_tensor_tensor(
                            out=x32[:, ch, :], in0=at[j][:, 2 + k:2 + k + T], scalar=cw[:, ch * 31 + k:ch * 31 + k + 1],
                            in1=x32[:, ch, :], op0=ALU.mult, op1=ALU.add),
                            reads=[("at", j), "cw", ("x32", ch)], writes=[("x32", ch)])

        def after(ct):
            if ct != HOOK:
                return
            mean, rs2 = sq[0], sq[1]
            for ch in range(8):
                s = lt[ch % 2]
                P.op("act", lambda e, s=s, ch=ch: e.activation(out=s[:], in_=x32[:, ch, :], func=AF.Square),
                     reads=[("x32", ch)], writes=[("lt", ch % 2)])
                for tb in range(2):
                    P.op("pe", lambda e, tb=tb, ch=ch: e.matmul(pss[tb][:], ones32[:], x32[:, ch, tb * 512:(tb + 1) * 512],
                                                                start=(ch == 0), stop=(ch == 7)),
                         reads=[("x32", ch), "ones32"], writes=[("pss", tb)])
                    P.op("pe", lambda e, tb=tb, ch=ch, s=s: e.matmul(psh[tb][:], ones32[:], s[:, tb * 512:(tb + 1) * 512],
                                                                     start=(ch == 0), stop=(ch == 7)),
                         reads=[("lt", ch % 2), "ones32"], writes=[("psh", tb)])
            for tb in range(2):
                sl = slice(tb * 512, (tb + 1) * 512)
                P.op("act", lambda e, tb=tb, sl=sl: e.activation(out=mean[:, sl], in_=pss[tb][:], func=AF.Identity, scale=1.0 / 1024),
                     reads=[("pss", tb)], writes=[("sq", 0)])
                P.op("dve", lambda e, tb=tb, sl=sl: e.tensor_tensor(out=rs2[:, sl], in0=mean[:, sl], in1=mean[:, sl], op=ALU.mult),
                     reads=[("sq", 0)], writes=[("sq", 1)])
                P.op("dve", lambda e, tb=tb, sl=sl: e.scalar_tensor_tensor(out=rs2[:, sl], in0=psh[tb][:], scalar=1.0 / 1024, in1=rs2[:, sl],
                                                                           op0=ALU.mult, op1=ALU.subtract),
                     reads=[("psh", tb), ("sq", 1)], writes=[("sq", 1)])
                P.op("act", lambda e, tb=tb, sl=sl: e.activation(out=rs2[:, sl], in_=rs2[:, sl], func=AF.Sqrt, bias=eps5[:], scale=1.0),
                     reads=[("sq", 1), "eps5"], writes=[("sq", 1)])
                P.op("dve", lambda e, tb=tb, sl=sl: e.reciprocal(out=rs2[:, sl], in_=rs2[:, sl]), reads=[("sq", 1)], writes=[("sq", 1)])
            for ch in range(8):
                s = lt[ch % 2]
                P.op("dve", lambda e, ch=ch, s=s: e.tensor_tensor(out=s[:], in0=x32[:, ch, :], in1=mean[:], op=ALU.subtract),
                     reads=[("x32", ch), ("sq", 0)], writes=[("lt", ch % 2)])
                P.op("dve", lambda e, ch=ch, s=s: e.tensor_tensor(out=s[:], in0=s[:], in1=rs2[:], op=ALU.mult),
                     reads=[("lt", ch % 2), ("sq", 1)], writes=[("lt", ch % 2)])
                P.op("act", lambda e, ch=ch, s=s: e.activation(out=s[:], in_=s[:], func=AF.Silu, bias=cvec[:, 16 + ch:17 + ch],
                                                               scale=cvec[:, 8 + ch:9 + ch]),
                     reads=[("lt", ch % 2), "cvec"], writes=[("lt", ch % 2)])
                P.dma("sp", lambda e, ch=ch, s=s: e.dma_start(out=ocT[ch * 128:(ch + 1) * 128, :], in_=s[:]), reads=[("lt", ch % 2)])
        gemm(P, w_in, D, INW, 512, rhs, evac, wbufs, psb, extra=extra, after=after)
        P.emit()
    return nc


TA = 8192
NPAIR = TA // 128
NEG = -30000.0
C_ID, C_TRI, C_BLK, C_SEL0, C_SEL1, C_MU, C_MLS, C_OFFD, C_ONES = range(9)
NCONST = 9
RING = 4


def gdn_consts():
    p = np.arange(128)[:, None]
    f = np.arange(128)[None, :]
    same = (p // 64) == (f // 64)
    c = np.zeros((NCONST, 128, 128), np.float32)
    c[C_ID] = (p == f)
    c[C_TRI] = same & (p <= f)
    c[C_BLK] = same
    c[C_SEL0] = (p < 64) & (f >= 0)
    c[C_SEL1] = (p >= 64) & (f >= 0)
    c[C_MU] = np.where(same & (p <= f), 0.0, NEG)
    c[C_MLS] = np.where(same & (p > f), 0.0, NEG)
    c[C_OFFD] = (p != f)
    c[C_ONES] = 1.0
    return np.ascontiguousarray(c.transpose(1, 0, 2).reshape(128, NCONST * 128))


def build_gdn(stage=9, npair=NPAIR):
    nc = bass.Bass("TRN2", target_bir_lowering=False)
    es = ExitStack()
    P = Prog(nc, es)
    qd = nc.dram_tensor("qT", [128, TA + 4], F32, kind="ExternalInput").ap()
    kd = nc.dram_tensor("kT", [128, TA + 4], F32, kind="ExternalInput").ap()
    vd = nc.dram_tensor("vT", [128, TA + 4], F32, kind="ExternalInput").ap()
    zd = nc.dram_tensor("zT", [128, TA], F32, kind="ExternalInput").ap()
    browd = nc.dram_tensor("brow", [128, TA], F32, kind="ExternalInput").ap()
    btokd = nc.dram_tensor("btok", [128, NPAIR], F32, kind="ExternalInput").ap()
    atokd = nc.dram_tensor("atok", [128, NPAIR], F32, kind="ExternalInput").ap()
    cwvd = nc.dram_tensor("cwv", [128, 12], F32, kind="ExternalInput").ap()
    hvd = nc.dram_tensor("hv", [128, 4], F32, kind="ExternalInput").ap()
    cstd = nc.dram_tensor("cst", [128, NCONST * 128], F32, kind="ExternalInput").ap()
    outd = nc.dram_tensor("oaT", [128, TA], F32, kind="ExternalOutput").ap()
    with es:
        cst = P.sb("cst", [128, NCONST * 128], F32)

        def C(i):
            return cst[:, i * 128:(i + 1) * 128]
        identb = P.sb("identb", [128, 128], BF16)
        qnb = P.sb("qnb", [128, TA], BF16)
        knb = P.sb("knb", [128, TA], BF16)
        kbb = P.sb("kbb", [128, TA], BF16)
        vnb = P.sb("vnb", [128, TA], BF16)
        oT = P.sb("oT", [128, TA], F32)
        NB = 1024
        raw = [P.sb(f"raw{i}", [128, NB + 4], F32) for i in range(2)]
        acc = [P.sb(f"acc{i}", [128, NB], F32) for i in range(2)]
        sqs = P.sb("sqs", [128, NB], F32)
        rn = P.sb("rn", [128, NB], F32)
        bsb = P.sb("bsb", [128, NB], F32)
        cwv = P.sb("cwv", [128, 12], F32)
        hv = P.sb("hv", [128, 4], F32)
        small = P.sb("small", [128, 8], F32)
        btok = P.sb("btok", [128, NPAIR], F32)
        atok = P.sb("atok", [128, NPAIR], F32)
        gt = P.sb("gt", [128, NPAIR], F32)
        gc = P.sb("gc", [128, NPAIR], F32)
        edl = P.sb("edl", [128, NPAIR], F32)
        bgc = P.sb("bgc", [128, NPAIR], F32)
        glb = [P.sb(f"glb{h}", [128, NPAIR], F32) for h in range(2)]
        dgt = [P.sb(f"dgt{i}", [128, 128], F32) for i in range(3)]
        tU = [P.sb(f"tU{i}", [128, 128], F32) for i in range(3)]
        tL = [P.sb(f"tL{i}", [128, 128], F32) for i in range(3)]
        EUs = [P.sb(f"EUs{i}", [128, 128], F32) for i in range(3)]
        egc = [P.sb(f"egc{i}", [128, 128], F32) for i in range(3)]
        Lb = [P.sb(f"Lb{i}", [128, 128], BF16) for i in range(6)]
        Ub = [P.sb(f"Ub{i}", [128, 128], BF16) for i in range(6)]
        TTb = [P.sb(f"TTb{i}", [128, 128], BF16) for i in range(6)]
        vb = [P.sb(f"vb{i}", [128, 128], BF16) for i in range(3)]
        kbg = [P.sb(f"kbg{i}", [128, 128], BF16) for i in range(3)]
        attnT = [P.sb(f"attnT{i}", [128, 128], BF16) for i in range(RING)]
        qdec = [P.sb(f"qdec{i}", [128, 128], BF16) for i in range(RING)]
        kdec = [P.sb(f"kdec{i}", [128, 128], BF16) for i in range(RING)]
        ub = [P.sb(f"ub{i}", [128, 128], BF16) for i in range(RING)]
        At = [[P.sb(f"At{i}_{h}", [128, 128], BF16) for h in range(2)] for i in range(RING)]
        wtok = [P.sb(f"wtok{i}", [128, 128], BF16) for i in range(3)]
        S32 = P.sb("S32", [128, 128], F32)
        Sb = P.sb("Sb", [128, 128], BF16)
        bk = [P.ps(f"bk{i}", [128, 512]) for i in range(8)]

        def q4(b, i):
            return bk[b][:, i * 128:(i + 1) * 128]
        pss = [bk[0], bk[1]]
        PDGs = [q4(2 * j, 0) for j in range(3)]
        PKKs = [[q4(2 * j, 1), q4(2 * j, 2), q4(2 * j, 3)] for j in range(3)]
        PTRs = [bk[2 * j + 1][:, 0:256] for j in range(3)]
        PINVs = [[q4(2 * j + 1, 2), q4(2 * j + 1, 3), q4(2 * j, 0)] for j in range(3)]
        PUWs = [[q4(2 * j + 1, 0), q4(2 * j + 1, 1)] for j in range(3)]
        PATs = [[q4(2 * j, 1), q4(2 * j + 1, 2)] for j in range(3)]
        PQPs = [q4(2 * j, 2) for j in range(3)]
        PWS = [q4(6, 0), q4(6, 1)]
        PDS = [q4(6, 2), q4(6, 3)]
        POT = [q4(7, 0), q4(7, 1)]
        PSET = [q4(1, 3), q4(3, 3), q4(5, 3)]
        PSETK = [("bk", 1), ("bk", 3), ("bk", 5)]

        P.dma("sp", lambda e: e.dma_start(out=cst[:], in_=cstd), writes=["cst"])
        P.dma("sp", lambda e: e.dma_start(out=cwv[:], in_=cwvd), writes=["cwv"])
        P.dma("sp", lambda e: e.dma_start(out=hv[:], in_=hvd), writes=["hv"])
        P.dma("sp", lambda e: e.dma_start(out=btok[:], in_=btokd), writes=["btok"])
        P.dma("sp", lambda e: e.dma_start(out=atok[:], in_=atokd), writes=["atok"])
        P.op("dve", lambda e: e.memset(small[:, 0:1], 1e-6), writes=["small"])
        P.op("dve", lambda e: e.memset(small[:, 1:2], 1.0), reads=["small"], writes=["small"])
        P.op("dve", lambda e: e.memset(S32[:], 0.0), writes=["S32"])
        P.op("dve", lambda e: e.memset(Sb[:], 0.0), writes=["Sb"])
        P.op("dve", lambda e: e.tensor_copy(out=identb[:], in_=C(C_ID)), reads=["cst"], writes=["identb"])
        if stage >= 0:
            P.op("act", lambda e: e.activation(out=btok[:], in_=btok[:], func=AF.Sigmoid), reads=["btok"], writes=["btok"])
            P.op("act", lambda e: e.activation(out=small[:, 2:3], in_=hv[:, 0:1], func=AF.Exp), reads=["hv", "small"], writes=["small"])
            P.op("dve", lambda e: e.tensor_scalar(out=small[:, 2:3], in0=small[:, 2:3], scalar1=-1.0, scalar2=None, op0=ALU.mult),
                 reads=["small"], writes=["small"])
            P.op("act", lambda e: e.activation(out=atok[:], in_=atok[:], func=AF.Exp, bias=hv[:, 1:2], scale=1.0),
                 reads=["atok", "hv"], writes=["atok"])
            P.op("act", lambda e: e.activation(out=atok[:], in_=atok[:], func=AF.Ln, bias=small[:, 1:2], scale=1.0),
                 reads=["atok", "small"], writes=["atok"])
            P.op("dve", lambda e: e.tensor_scalar(out=gt[:], in0=atok[:], scalar1=small[:, 2:3], scalar2=None, op0=ALU.mult),
                 reads=["atok", "small"], writes=["gt"])
            P.op("pe", lambda e: e.matmul(PSET[0][:, 0:NPAIR], C(C_TRI), gt[:], start=True, stop=True), reads=["cst", "gt"], writes=[PSETK[0]])
            P.op("pe", lambda e: e.matmul(PSET[1][:, 0:NPAIR], C(C_BLK), gt[:], start=True, stop=True), reads=["cst", "gt"], writes=[PSETK[1]])
            P.op("dve", lambda e: e.tensor_copy(out=gc[:], in_=PSET[0][:, 0:NPAIR]), reads=[PSETK[0]], writes=["gc"])
            P.op("dve", lambda e: e.tensor_tensor(out=edl[:], in0=PSET[1][:, 0:NPAIR], in1=gc[:], op=ALU.subtract),
                 reads=[PSETK[1], "gc"], writes=["edl"])
            P.op("act", lambda e: e.activation(out=edl[:], in_=edl[:], func=AF.Exp), reads=["edl"], writes=["edl"])
            P.op("act", lambda e: e.activation(out=bgc[:], in_=gc[:], func=AF.Exp), reads=["gc"], writes=["bgc"])
            P.op("dve", lambda e: e.tensor_tensor(out=bgc[:], in0=bgc[:], in1=btok[:], op=ALU.mult), reads=["bgc", "btok"], writes=["bgc"])
            for h in range(2):
                P.op("pe", lambda e, h=h: e.matmul(PSET[2][:, 0:NPAIR], C(C_SEL0 + h), gt[:], start=True, stop=True),
                     reads=["cst", "gt"], writes=[PSETK[2]])
                P.op("act", lambda e, h=h: e.activation(out=glb[h][:], in_=PSET[2][:, 0:NPAIR], func=AF.Exp),
                     reads=[PSETK[2]], writes=[("glb", h)])


        rctr = [0]
        for b in range(npair * 128 // NB if stage >= 1 else 0):
            bs = slice(b * NB, (b + 1) * NB)
            P.dma("sp", lambda e, bs=bs: e.dma_start(out=bsb[:], in_=browd[:, bs]), writes=["bsb"])
            P.op("act", lambda e: e.activation(out=bsb[:], in_=bsb[:], func=AF.Sigmoid), reads=["bsb"], writes=["bsb"])
            for i, src in enumerate((qd, kd, vd)):
                j = rctr[0] % 2
                rctr[0] += 1
                P.dma("sp", lambda e, src=src, j=j, b=b: e.dma_start(out=raw[j][:], in_=src[:, b * NB:b * NB + NB + 4]),
                      writes=[("raw", j)])
                a = acc[j]
                P.op("dve", lambda e, a=a, j=j, i=i: e.tensor_scalar(out=a[:], in0=raw[j][:, 1:1 + NB], scalar1=cwv[:, i * 4:i * 4 + 1],
                                                                    scalar2=None, op0=ALU.mult),
                     reads=[("raw", j), "cwv"], writes=[("acc", j)])
                for k in range(1, 4):
                    P.op("dve", lambda e, a=a, j=j, i=i, k=k: e.scalar_tensor_tensor(
                        out=a[:], in0=raw[j][:, 1 + k:1 + k + NB], scalar=cwv[:, i * 4 + k:i * 4 + k + 1], in1=a[:],
                        op0=ALU.mult, op1=ALU.add), reads=[("raw", j), "cwv", ("acc", j)], writes=[("acc", j)])
                if i == 2:
                    P.op("act", lambda e, a=a, bs=bs: e.activation(out=vnb[:, bs], in_=a[:], func=AF.Silu),
                         reads=[("acc", j)], writes=[("vnb", b)])
                    continue
                P.op("act", lambda e, a=a: e.activation(out=a[:], in_=a[:], func=AF.Silu), reads=[("acc", j)], writes=[("acc", j)])
                P.op("act", lambda e, a=a: e.activation(out=sqs[:], in_=a[:], func=AF.Square), reads=[("acc", j)], writes=["sqs"])
                for sbk in range(NB // 512):
                    ss = slice(sbk * 512, (sbk + 1) * 512)
                    pp = pss[sbk % 2]
                    P.op("pe", lambda e, pp=pp, ss=ss: e.matmul(pp[:], C(C_ONES), sqs[:, ss], start=True, stop=True),
                         reads=["cst", "sqs"], writes=[("bk", sbk % 2)])
                    P.op("act", lambda e, pp=pp, ss=ss: e.activation(out=rn[:, ss], in_=pp[:], func=AF.Ln, bias=small[:, 0:1], scale=1.0),
                         reads=[("bk", sbk % 2), "small"], writes=[("rn", sbk)])
                P.op("act", lambda e: e.activation(out=rn[:], in_=rn[:], func=AF.Exp, scale=-0.5), reads=[("rn", s_) for s_ in range(NB // 512)],
                     writes=[("rn", s_) for s_ in range(NB // 512)])
                rnk = [("rn", s_) for s_ in range(NB // 512)]
                if i == 0:
                    P.op("dve", lambda e, a=a, bs=bs: e.scalar_tensor_tensor(out=qnb[:, bs], in0=a[:], scalar=128.0 ** -0.5, in1=rn[:],
                                                                             op0=ALU.mult, op1=ALU.mult),
                         reads=[("acc", j)] + rnk, writes=[("qnb", b)])
                else:
                    P.op("dve", lambda e, a=a: e.tensor_tensor(out=a[:], in0=a[:], in1=rn[:], op=ALU.mult),
                         reads=[("acc", j)] + rnk, writes=[("acc", j)])
                    P.op("act", lambda e, a=a, bs=bs: e.activation(out=knb[:, bs], in_=a[:], func=AF.Identity),
                         reads=[("acc", j)], writes=[("knb", b)])
                    P.op("pool", lambda e, a=a, bs=bs: e.tensor_tensor(out=kbb[:, bs], in0=a[:], in1=bsb[:], op=ALU.mult),
                         reads=[("acc", j), "bsb"], writes=[("kbb", b)])

        cpy = [0]

        def copy_out(out_ap, in_ap, reads, writes):
            cpy[0] += 1
            if cpy[0] % 4 != 0:
                P.op("act", lambda e: e.activation(out=out_ap, in_=in_ap, func=AF.Identity), reads=reads, writes=writes)
            else:
                P.op("dve", lambda e: e.tensor_copy(out=out_ap, in_=in_ap), reads=reads, writes=writes)

        def prep(m):
            j = m % 3
            r = m % RING
            blk = m * 128 // NB
            ps_ = slice(m * 128, (m + 1) * 128)
            gcm = gc[:, m:m + 1]
            PDG_, PKK, PINV, ptr, PUW = PDGs[j], PKKs[j], PINVs[j], PTRs[j], PUWs[j]
            BA, BB = ("bk", 2 * j), ("bk", 2 * j + 1)
            LB = [Lb[j * 2], Lb[j * 2 + 1]]
            UB = [Ub[j * 2], Ub[j * 2 + 1]]
            TB = [TTb[j * 2], TTb[j * 2 + 1]]

            def lk(i):
                return ("Lb", j * 2 + i)

            def uk(i):
                return ("Ub", j * 2 + i)

            def tk(i):
                return ("TTb", j * 2 + i)
            P.op("act", lambda e: e.activation(out=dgt[j][:], in_=C(C_ID), func=AF.Identity, scale=gcm),
                 reads=["cst", "gc"], writes=[("dgt", j)])
            P.op("pe", lambda e: e.matmul(PDG_, C(C_ONES), dgt[j][:], start=True, stop=True),
                 reads=["cst", ("dgt", j)], writes=[BA])
            yield
            P.op("dve", lambda e: e.scalar_tensor_tensor(out=tU[j][:], in0=PDG_, scalar=gcm, in1=C(C_MU), op0=ALU.subtract, op1=ALU.add),
                 reads=[BA, "gc", "cst"], writes=[("tU", j)])
            P.op("dve", lambda e: e.scalar_tensor_tensor(out=tL[j][:], in0=PDG_, scalar=gcm, in1=C(C_MLS), op0=ALU.subtract, op1=ALU.subtract),
                 reads=[BA, "gc", "cst"], writes=[("tL", j)])
            P.op("act", lambda e: e.activation(out=egc[j][:], in_=PDG_, func=AF.Exp), reads=[BA], writes=[("egc", j)])
            P.op("act", lambda e: e.activation(out=tU[j][:], in_=tU[j][:], func=AF.Exp), reads=[("tU", j)], writes=[("tU", j)])
            P.op("act", lambda e: e.activation(out=tL[j][:], in_=tL[j][:], func=AF.Exp, scale=-1.0), reads=[("tL", j)], writes=[("tL", j)])
            P.op("pe", lambda e: e.matmul(PKK[0], knb[:, ps_], kbb[:, ps_], start=True, stop=True),
                 reads=[("knb", blk), ("kbb", blk)], writes=[BA])
            P.op("pe", lambda e: e.matmul(PKK[1], kbb[:, ps_], knb[:, ps_], start=True, stop=True),
                 reads=[("knb", blk), ("kbb", blk)], writes=[BA])
            P.op("pe", lambda e: e.matmul(PKK[2], knb[:, ps_], qnb[:, ps_], start=True, stop=True),
                 reads=[("knb", blk), ("qnb", blk)], writes=[BA])
            P.op("pe", lambda e: e.matmul(ptr[:, 0:128], vnb[:, ps_], identb[:], start=True, stop=True), reads=[("vnb", blk), "identb"], writes=[BB])
            P.op("pe", lambda e: e.matmul(ptr[:, 128:256], knb[:, ps_], identb[:], start=True, stop=True), reads=[("knb", blk), "identb"], writes=[BB])
            yield
            P.op("pool", lambda e: e.tensor_tensor(out=EUs[j][:], in0=tU[j][:], in1=C(C_OFFD), op=ALU.mult),
                 reads=[("tU", j), "cst"], writes=[("EUs", j)])
            P.op("dve", lambda e: e.tensor_scalar(out=vb[j][:], in0=ptr[:, 0:128], scalar1=btok[:, m:m + 1], scalar2=None, op0=ALU.mult),
                 reads=[BB, "btok"], writes=[("vb", j)])
            P.op("dve", lambda e: e.tensor_scalar(out=kbg[j][:], in0=ptr[:, 128:256], scalar1=bgc[:, m:m + 1], scalar2=None, op0=ALU.mult),
                 reads=[BB, "bgc"], writes=[("kbg", j)])
            P.op("dve", lambda e: e.tensor_scalar(out=kdec[r][:], in0=ptr[:, 128:256], scalar1=edl[:, m:m + 1], scalar2=None, op0=ALU.mult),
                 reads=[BB, "edl"], writes=[("kdec", r)])
            P.op("pool", lambda e: e.tensor_tensor(out=qdec[r][:], in0=qnb[:, ps_], in1=egc[j][:], op=ALU.mult),
                 reads=[("qnb", blk), ("egc", j)], writes=[("qdec", r)])
            yield
            P.op("dve", lambda e: e.tensor_tensor(out=LB[0][:], in0=PKK[1], in1=tL[j][:], op=ALU.mult),
                 reads=[BA, ("tL", j)], writes=[lk(0)])
            P.op("dve", lambda e: e.tensor_tensor(out=attnT[r][:], in0=PKK[2], in1=tU[j][:], op=ALU.mult),
                 reads=[BA, ("tU", j)], writes=[("attnT", r)])
            P.op("dve", lambda e: e.tensor_tensor(out=UB[0][:], in0=PKK[0], in1=EUs[j][:], op=ALU.mult),
                 reads=[BA, ("EUs", j)], writes=[uk(0)])
            P.op("pool", lambda e: e.tensor_tensor(out=TB[0][:], in0=C(C_ID), in1=UB[0][:], op=ALU.subtract),
                 reads=[uk(0), "cst"], writes=[tk(0)])
            yield
            cur = 0
            for k in range(5):
                nx = 1 - cur
                P.op("pe", lambda e, cur=cur: e.matmul(PINV[0], UB[cur][:], LB[cur][:], start=True, stop=True),
                     reads=[uk(cur), lk(cur)], writes=[BB])
                if k < 4:
                    P.op("pe", lambda e, cur=cur: e.matmul(PINV[1], LB[cur][:], UB[cur][:], start=True, stop=True),
                         reads=[uk(cur), lk(cur)], writes=[BB])
                yield
                copy_out(LB[nx][:], PINV[0], [BB], [lk(nx)])
                if k < 4:
                    copy_out(UB[nx][:], PINV[1], [BB], [uk(nx)])
                P.op("pe", lambda e, cur=cur: e.matmul(PINV[2], identb[:], TB[cur][:], start=True, stop=False),
                     reads=["identb", tk(cur)], writes=[BA])
                P.op("pe", lambda e, cur=cur, nx=nx: e.matmul(PINV[2], LB[nx][:], TB[cur][:], start=False, stop=True),
                     reads=[lk(nx), tk(cur)], writes=[BA])
                yield
                copy_out(TB[nx][:], PINV[2], [BA], [tk(nx)])
                cur = nx
            TT = TB[cur]
            ttk = tk(cur)
            P.op("pe", lambda e: e.matmul(PUW[0], TT[:], vb[j][:], start=True, stop=True), reads=[ttk, ("vb", j)], writes=[BB])
            P.op("pe", lambda e: e.matmul(PUW[1], TT[:], kbg[j][:], start=True, stop=True), reads=[ttk, ("kbg", j)], writes=[BB])
            yield
            copy_out(ub[r][:], PUW[0], [BB], [("ub", r)])
            copy_out(wtok[j][:], PUW[1], [BB], [("wtok", j)])
            PAT = PATs[j]
            PATK = [BA, BB]
            for h in range(2):
                hs = slice(h * 64, (h + 1) * 64)
                P.op("pe", lambda e, h=h, hs=hs: e.matmul(PAT[h], wtok[j][hs, :], kdec[r][hs, :], start=True, stop=True),
                     reads=[("wtok", j), ("kdec", r)], writes=[PATK[h]])
            P.op("pe", lambda e: e.matmul(PQPs[j], wtok[j][:], attnT[r][:], start=True, stop=True),
                 reads=[("wtok", j), ("attnT", r)], writes=[BA])
            yield
            for h in range(2):
                P.op("act", lambda e, h=h: e.activation(out=At[r][h][:], in_=PAT[h], func=AF.Identity, scale=-1.0),
                     reads=[PATK[h]], writes=[("At", r, h)])
            P.op("dve", lambda e: e.tensor_tensor(out=qdec[r][:], in0=qdec[r][:], in1=PQPs[j], op=ALU.subtract),
                 reads=[("qdec", r), BA], writes=[("qdec", r)])

        def scan(m):
            r = m % RING
            for h in range(2):
                n = 2 * m + h
                hs = slice(h * 64, (h + 1) * 64)
                j = n % 2
                P.op("pe", lambda e, j=j, h=h: e.matmul(PDS[j], At[r][h][:], Sb[:], start=True, stop=False),
                     reads=[("At", r, h), "Sb"], writes=[("bk", 6)])
                P.op("pe", lambda e, j=j, hs=hs: e.matmul(PDS[j], kdec[r][hs, :], ub[r][hs, :], start=False, stop=True),
                     reads=[("kdec", r), ("ub", r)], writes=[("bk", 6)])
                P.op("pe", lambda e, j=j, hs=hs: e.matmul(POT[j][:, 0:64], Sb[:], qdec[r][:, hs], start=True, stop=False),
                     reads=["Sb", ("qdec", r)], writes=[("bk", 7)])
                P.op("pe", lambda e, j=j, hs=hs: e.matmul(POT[j][:, 0:64], ub[r][hs, :], attnT[r][hs, hs], start=False, stop=True),
                     reads=[("ub", r), ("attnT", r)], writes=[("bk", 7)])
                yield
                P.op("dve", lambda e, j=j, h=h: e.scalar_tensor_tensor(out=Sb[:], in0=S32[:], scalar=glb[h][:, m:m + 1], in1=PDS[j],
                                                                       op0=ALU.mult, op1=ALU.add),
                     reads=["S32", ("glb", h), ("bk", 6)], writes=["Sb"])
                P.op("dve", lambda e, j=j, h=h: e.scalar_tensor_tensor(out=S32[:], in0=S32[:], scalar=glb[h][:, m:m + 1], in1=PDS[j],
                                                                       op0=ALU.mult, op1=ALU.add),
                     reads=["S32", ("glb", h), ("bk", 6)], writes=["S32"])
                P.op("act", lambda e, j=j, n=n: e.activation(out=oT[:, n * 64:(n + 1) * 64], in_=POT[j][:, 0:64], func=AF.Identity),
                     reads=[("bk", 7)], writes=[("oT", n * 64 // NB)])
                yield

        KP = int(os.environ.get("KP", 3))
        prep_next, prep_done, scan_next, scan_done = 0, set(), 0, 0
        active = []
        scan_active = False
        while scan_done < npair:
            while (prep_next < npair and sum(1 for a_ in active if a_[0] == "p") < KP and prep_next < scan_done + RING):
                active.append(["p", prep_next, prep(prep_next)])
                prep_next += 1
            if not scan_active and scan_next < npair and scan_next in prep_done:
                active.append(["s", scan_next, scan(scan_next)])
                scan_active = True
                scan_next += 1
            for a_ in list(active):
                try:
                    next(a_[2])
                except StopIteration:
                    active.remove(a_)
                    if a_[0] == "p":
                        prep_done.add(a_[1])
                    else:
                        scan_done += 1
                        scan_active = False

        if os.environ.get("SKIP_POST"):
            P.dma("sp", lambda e: e.dma_start(out=outd[:, 0:1024], in_=cst[:, 0:1024]), reads=["cst"])
        for b in range(npair * 128 // NB if not os.environ.get("SKIP_POST") else 0):
            bs = slice(b * NB, (b + 1) * NB)
            j = b % 2
            P.dma("sp", lambda e, bs=bs, j=j: e.dma_start(out=raw[j][:, 0:NB], in_=zd[:, bs]), writes=[("raw", j)])
            P.op("act", lambda e, j=j: e.activation(out=raw[j][:, 0:NB], in_=raw[j][:, 0:NB], func=AF.Silu), reads=[("raw", j)], writes=[("raw", j)])
            P.op("act", lambda e, bs=bs: e.activation(out=sqs[:], in_=oT[:, bs], func=AF.Square), reads=[("oT", b)], writes=["sqs"])
            for sbk in range(NB // 512):
                ss = slice(sbk * 512, (sbk + 1) * 512)
                pp = pss[sbk % 2]
                P.op("pe", lambda e, pp=pp, ss=ss: e.matmul(pp[:], C(C_ONES), sqs[:, ss], start=True, stop=True),
                     reads=["cst", "sqs"], writes=[("bk", sbk % 2)])
                P.op("act", lambda e, pp=pp, ss=ss: e.activation(out=rn[:, ss], in_=pp[:], func=AF.Ln, bias=small[:, 0:1], scale=1.0 / 128),
                     reads=[("bk", sbk % 2), "small"], writes=[("rn", sbk)])
            rnk = [("rn", s_) for s_ in range(NB // 512)]
            P.op("act", lambda e: e.activation(out=rn[:], in_=rn[:], func=AF.Exp, scale=-0.5), reads=rnk, writes=rnk)
            a = acc[j]
            P.op("dve", lambda e, a=a, bs=bs: e.scalar_tensor_tensor(out=a[:], in0=oT[:, bs], scalar=hv[:, 2:3], in1=rn[:], op0=ALU.mult, op1=ALU.mult),
                 reads=[("oT", b), "hv"] + rnk, writes=[("acc", j)])
            P.op("pool", lambda e, a=a, j=j: e.tensor_tensor(out=a[:], in0=a[:], in1=raw[j][:, 0:NB], op=ALU.mult),
                 reads=[("acc", j), ("raw", j)], writes=[("acc", j)])
            P.dma("sp", lambda e, a=a, bs=bs: e.dma_start(out=outd[:, bs], in_=a[:]), reads=[("acc", j)])
        P.emit()
    return nc


import math

TA = 8192
NT = TA // 128
HT = 32
MAGIC = 12582912.0
C1 = 6.28125
C2 = 2.0 * math.pi - 6.28125
PI_SAFE = 3.1415925


def ret_consts():
    p = np.arange(128)[:, None]
    f = np.arange(128)[None, :]
    c = np.zeros((128, 256), np.float32)
    c[:, 0:128] = (p <= f)
    c[:, 128:256] = (p == f)
    return c


def ret_rc(head):
    gamma = 1.0 - 2.0 ** (-5.0 - head)
    rc = np.zeros((128, 72), np.float32)
    half = 64
    rc[:, 0:64] = (10000.0 ** (-np.arange(half, dtype=np.float32) / half)).astype(np.float32)[None, :]
    p = np.arange(128, dtype=np.float64)
    rc[:, 64] = gamma ** (p + 1)
    rc[:, 65] = 128.0 ** -0.5 * gamma ** (-(p + 1))
    rc[:, 66] = gamma ** 128
    rc[:, 67] = math.pi / 2
    return rc


def build_ret(nt=NT):
    nc = bass.Bass("TRN2", target_bir_lowering=False)
    es = ExitStack()
    P = Prog(nc, es)
    qd = nc.dram_tensor("q_tok", [128, NT, 128], F32, kind="ExternalInput").ap()
    kd = nc.dram_tensor("k_tok", [128, NT, 128], F32, kind="ExternalInput").ap()
    vd = nc.dram_tensor("v_tok", [128, NT, 128], F32, kind="ExternalInput").ap()
    posd = nc.dram_tensor("pos_tok", [128, NT], I32, kind="ExternalInput").ap()
    rcd = nc.dram_tensor("rc", [128, 72], F32, kind="ExternalInput").ap()
    cstd = nc.dram_tensor("cst", [128, 256], F32, kind="ExternalInput").ap()
    outd = nc.dram_tensor("oretT", [128, TA], F32, kind="ExternalOutput").ap()
    nh = (nt + HT - 1) // HT
    with es:
        cst = P.sb("cst", [128, 256], F32)
        rc = P.sb("rc", [128, 72], F32)
        posi = P.sb("posi", [128, NT], I32)
        posf = P.sb("posf", [128, NT], F32)
        identb = P.sb("identb", [128, 128], BF16)
        qh = P.sb("qh", [128, HT, 128], F32)
        kh = P.sb("kh", [128, HT, 128], F32)
        ang = P.sb("ang", [128, HT, 64], F32)
        tt = P.sb("tt", [128, HT, 64], F32)
        cs = P.sb("cs", [128, HT, 64], F32)
        sn = P.sb("sn", [128, HT, 64], F32)
        A = P.sb("A", [128, HT, 64], F32)
        B = P.sb("B", [128, HT, 64], F32)
        qb = P.sb("qb", [128, NT, 128], BF16)
        kb = P.sb("kb", [128, NT, 128], BF16)
        vb = P.sb("vb", [128, NT, 128], BF16)
        oT = P.sb("oT", [128, TA], F32)
        qinT = [P.sb(f"qinT{i}", [128, 128], BF16) for i in range(2)]
        koutT = [P.sb(f"koutT{i}", [128, 128], BF16) for i in range(2)]
        qinT2 = [P.sb(f"qinT2{i}", [128, 128], BF16) for i in range(2)]
        scm = [P.sb(f"scm{i}", [128, 128], BF16) for i in range(2)]
        W32 = P.sb("W32", [128, 128], F32)
        Rb = P.sb("Rb", [128, 128], BF16)
        bkq = P.ps("bkq", [128, 1024], BF16)
        bkk = P.ps("bkk", [128, 1024], BF16)
        bks = P.ps("bks", [128, 512])
        bko = P.ps("bko", [128, 512])
        bkr = P.ps("bkr", [128, 512])

        P.dma("sp", lambda e: e.dma_start(out=cst[:], in_=cstd), writes=["cst"])
        P.dma("sp", lambda e: e.dma_start(out=rc[:], in_=rcd), writes=["rc"])
        P.dma("sp", lambda e: e.dma_start(out=posi[:], in_=posd), writes=["posi"])
        for hq in range(nh * 2):
            P.dma("pool", lambda e, hq=hq: e.dma_start(out=vb[:, hq * 16:(hq + 1) * 16, :], in_=vd[:, hq * 16:(hq + 1) * 16, :]),
                  writes=[("vb", hq // 2)])
        P.op("dve", lambda e: e.tensor_copy(out=posf[:], in_=posi[:]), reads=["posi"], writes=["posf"])
        P.op("dve", lambda e: e.tensor_copy(out=identb[:], in_=cst[:, 128:256]), reads=["cst"], writes=["identb"])
        P.op("dve", lambda e: e.memset(W32[:], 0.0), writes=["W32"])
        P.op("dve", lambda e: e.memset(Rb[:], 0.0), writes=["Rb"])

        def fl(t):
            return t[:].rearrange("p a b -> p (a b)")

        def reduce_to(dst_key):
            P.op("dve", lambda e: e.tensor_scalar(out=fl(tt), in0=fl(ang), scalar1=1.0 / (2 * math.pi), scalar2=MAGIC, op0=ALU.mult, op1=ALU.add),
                 reads=["ang"], writes=["tt"])
            P.op("dve", lambda e: e.tensor_scalar(out=fl(tt), in0=fl(tt), scalar1=-MAGIC, scalar2=None, op0=ALU.add),
                 reads=["tt"], writes=["tt"])
            P.op("dve", lambda e: e.scalar_tensor_tensor(out=fl(A), in0=fl(tt), scalar=-C1, in1=fl(ang), op0=ALU.mult, op1=ALU.add),
                 reads=["tt", "ang"], writes=["A"])
            P.op("dve", lambda e: e.scalar_tensor_tensor(out=fl(A), in0=fl(tt), scalar=-C2, in1=fl(A), op0=ALU.mult, op1=ALU.add),
                 reads=["tt", "A"], writes=["A"])
            P.op("dve", lambda e: e.tensor_scalar(out=fl(A), in0=fl(A), scalar1=-PI_SAFE, scalar2=PI_SAFE, op0=ALU.max, op1=ALU.min),
                 reads=["A"], writes=["A"])

        def stageA(hh):
            m0 = hh * HT
            P.dma("sp", lambda e, m0=m0: e.dma_start(out=qh[:], in_=qd[:, m0:m0 + HT, :]), writes=["qh"])
            P.dma("sp", lambda e, m0=m0: e.dma_start(out=kh[:], in_=kd[:, m0:m0 + HT, :]), writes=["kh"])
            for i in range(HT):
                P.op("pool", lambda e, i=i, m0=m0: e.tensor_scalar(out=ang[:, i, :], in0=rc[:, 0:64], scalar1=posf[:, m0 + i:m0 + i + 1],
                                                                  scalar2=None, op0=ALU.mult),
                     reads=["rc", "posf"], writes=["ang"])
                if i % 8 == 7:
                    yield
            reduce_to("A")
            yield
            P.op("act", lambda e: e.activation(out=fl(sn), in_=fl(A), func=AF.Sin), reads=["A"], writes=["sn"])
            P.op("dve", lambda e: e.scalar_tensor_tensor(out=fl(A), in0=fl(A), scalar=-1.0, in1=fl(A), op0=ALU.mult, op1=ALU.max), reads=["A"], writes=["A"])
            P.op("act", lambda e: e.activation(out=fl(cs), in_=fl(A), func=AF.Sin, scale=-1.0, bias=rc[:, 67:68]), reads=["A", "rc"], writes=["cs"])
            for (src, dst, col, eng) in ((qh, qb, 64, "dve"), (kh, kb, 65, "dve")):
                x1 = src[:, :, 0:64]
                x2 = src[:, :, 64:128]
                skey = "qh" if src is qh else "kh"
                dkey = ("qb", hh) if dst is qb else ("kb", hh)
                P.op("dve", lambda e, x1=x1: e.tensor_tensor(out=A[:], in0=x1, in1=cs[:], op=ALU.mult), reads=[skey, "cs"], writes=["A"])
                P.op("pool", lambda e, x2=x2: e.tensor_tensor(out=B[:], in0=x2, in1=sn[:], op=ALU.mult), reads=[skey, "sn"], writes=["B"])
                P.op("dve", lambda e: e.tensor_tensor(out=A[:], in0=A[:], in1=B[:], op=ALU.subtract), reads=["A", "B"], writes=["A"])
                P.op("dve", lambda e, dst=dst, m0=m0, col=col: e.tensor_scalar(out=dst[:, m0:m0 + HT, 0:64], in0=A[:], scalar1=rc[:, col:col + 1],
                                                                               scalar2=None, op0=ALU.mult),
                     reads=["A", "rc"], writes=[dkey])
                yield
                P.op("dve", lambda e, x1=x1: e.tensor_tensor(out=A[:], in0=x1, in1=sn[:], op=ALU.mult), reads=[skey, "sn"], writes=["A"])
                P.op("pool", lambda e, x2=x2: e.tensor_tensor(out=B[:], in0=x2, in1=cs[:], op=ALU.mult), reads=[skey, "cs"], writes=["B"])
                P.op("dve", lambda e: e.tensor_tensor(out=A[:], in0=A[:], in1=B[:], op=ALU.add), reads=["A", "B"], writes=["A"])
                P.op("dve", lambda e, dst=dst, m0=m0, col=col: e.tensor_scalar(out=dst[:, m0:m0 + HT, 64:128], in0=A[:], scalar1=rc[:, col:col + 1],
                                                                               scalar2=None, op0=ALU.mult),
                     reads=["A", "rc"], writes=[dkey])
                yield

        def partA(m):
            hh = m // HT
            j = m % 2
            P.op("pe", lambda e, m=m: e.transpose(bkq[:, 0:128], qb[:, m, :], identb[:]), reads=[("qb", hh), "identb"], writes=[("bk", 0)])
            P.op("pe", lambda e, m=m: e.transpose(bkk[:, 0:128], kb[:, m, :], identb[:]), reads=[("kb", hh), "identb"], writes=[("bk", 1)])
            P.op("dve", lambda e, j=j: e.tensor_copy(out=qinT[j][:], in_=bkq[:, 0:128]), reads=[("bk", 0)], writes=[("qinT", j)])
            P.op("act", lambda e, j=j: e.activation(out=koutT[j][:], in_=bkk[:, 0:128], func=AF.Identity), reads=[("bk", 1)], writes=[("koutT", j)])
            P.op("pe", lambda e, j=j: e.matmul(bks[:, 0:128], koutT[j][:], qinT[j][:], start=True, stop=True),
                 reads=[("koutT", j), ("qinT", j)], writes=[("bk", 2)])
            P.op("dve", lambda e, j=j: e.tensor_tensor(out=scm[j][:], in0=bks[:, 0:128], in1=cst[:, 0:128], op=ALU.mult),
                 reads=[("bk", 2), "cst"], writes=[("scm", j)])

        def partB(m):
            hh = m // HT
            j = m % 2
            P.op("pe", lambda e, m=m: e.matmul(bkr[:, 0:128], kb[:, m, :], vb[:, m, :], start=True, stop=True),
                 reads=[("kb", hh), ("vb", hh)], writes=[("bk", 4)])
            P.op("pe", lambda e, j=j, m=m: e.matmul(bko[:, 0:128], vb[:, m, :], scm[j][:], start=True, stop=False),
                 reads=[("vb", hh), ("scm", j)], writes=[("bk", 3)])
            P.op("pe", lambda e, j=j: e.matmul(bko[:, 0:128], Rb[:], qinT[j][:], start=False, stop=True),
                 reads=["Rb", ("qinT", j)], writes=[("bk", 3)])
            P.op("dve", lambda e: e.scalar_tensor_tensor(out=W32[:], in0=W32[:], scalar=rc[:, 66:67], in1=bkr[:, 0:128], op0=ALU.mult, op1=ALU.add),
                 reads=["W32", "rc", ("bk", 4)], writes=["W32"])
            P.op("act", lambda e: e.activation(out=Rb[:], in_=W32[:], func=AF.Identity, scale=rc[:, 66:67]), reads=["W32", "rc"], writes=["Rb"])
            P.op("act", lambda e, m=m: e.activation(out=oT[:, m * 128:(m + 1) * 128], in_=bko[:, 0:128], func=AF.Identity),
                 reads=[("bk", 3)], writes=[("oT", m // 8)])
            if m % 8 == 7:
                g = m // 8
                P.dma("sp", lambda e, g=g: e.dma_start(out=outd[:, g * 1024:(g + 1) * 1024], in_=oT[:, g * 1024:(g + 1) * 1024]),
                      reads=[("oT", g)])

        for _ in stageA(0):
            pass
        gens = [stageA(hh) for hh in range(1, nh)]
        partA(0)
        for m in range(nt):
            if gens and (m % HT) % 3 == 0:
                try:
                    next(gens[0])
                except StopIteration:
                    gens.pop(0)
            if m % HT == HT - 2 and gens and (m // HT) + 1 < nh:
                for _ in gens[0]:
                    pass
                gens.pop(0)
            if m + 1 < nt:
                partA(m + 1)
            partB(m)
        P.emit()
    return nc


_sizes = (1024, 1024, 1024, 1024, 8, 8, 512, 512, 1024, 1024, 2048, 6144)
_off = np.concatenate([[0], np.cumsum(_sizes)])
_order = [0, 1, 2, 3, 6, 7, 8, 9, 10, 11, 4, 5]
PERM = np.concatenate([np.arange(_off[i], _off[i + 1]) for i in _order])
_psizes = [_sizes[i] for i in _order]
_poff = np.concatenate([[0], np.cumsum(_psizes)])
P_AQ, P_AK, P_AV, P_AZ, P_BQ, P_BK, P_BV, P_BG, P_CGLU, P_GATE, P_BETA, P_ALPHA = [int(v) for v in _poff[:12]]


def pk(v):
    return np.ascontiguousarray(v.reshape(-1, 128).T)


def make_vec(mod_l, nmix, nmlp, nfin):
    parts = [pk(m) for m in np.split(mod_l, 6)] + [pk(nmix), pk(nmlp), pk(nfin)]
    return np.ascontiguousarray(np.concatenate(parts, axis=1).astype(np.float32))


def make_cgT(cg, c, T=1024):
    out = np.zeros((2048, T + 32), np.float32)
    lo = c * T - 32
    if lo >= 0:
        out[:] = cg[lo:(c + 1) * T].T
    else:
        out[:, 32:] = cg[0:T].T
    return out


def make_convc_params(dw_w, dw_b, ln_w, ln_b):
    cw = np.ascontiguousarray(dw_w.reshape(31, 8, 128).transpose(2, 1, 0).reshape(128, 8 * 31)).astype(np.float32)
    cvec = np.ascontiguousarray(np.concatenate([pk(dw_b), pk(ln_w), pk(ln_b)], axis=1)).astype(np.float32)
    return cw, cvec


def make_gdn_inputs(aq, ak, av, az, abeta, aalpha, conv_w, a_log, dt_bias, norm_w, hd, cst):
    Tn = aq.shape[0]
    hs = slice(hd * 128, (hd + 1) * 128)

    def padT(a):
        o = np.zeros((128, Tn + 4), np.float32)
        o[:, 4:] = a[:, hs].T
        return o
    cwv = np.zeros((128, 12), np.float32)
    for i in range(3):
        cwv[:, i * 4:(i + 1) * 4] = conv_w[:, i * 1024 + hd * 128:i * 1024 + (hd + 1) * 128].T
    hv = np.zeros((128, 4), np.float32)
    hv[:, 0] = a_log[hd]
    hv[:, 1] = dt_bias[hd]
    hv[:, 2] = norm_w
    return {"qT": padT(aq), "kT": padT(ak), "vT": padT(av), "zT": np.ascontiguousarray(az[:, hs].T),
            "brow": np.ascontiguousarray(np.broadcast_to(abeta[:, hd][None, :], (128, Tn))),
            "btok": np.ascontiguousarray(abeta[:, hd].reshape(-1, 128).T),
            "atok": np.ascontiguousarray(aalpha[:, hd].reshape(-1, 128).T),
            "cwv": cwv, "hv": hv, "cst": cst}


def tokmaj(a, cols):
    x = a[:, cols]
    return np.ascontiguousarray(x.reshape(-1, 128, x.shape[1]).transpose(1, 0, 2))


def build_perm2():
    r_tiles = list(range(0, 56)) + list(range(72, 120))
    order = []
    ri = 0
    for ch in range(8):
        order += [56 + ch, 64 + ch]
        order += r_tiles[ri:ri + 5]
        ri += 5
    order += r_tiles[ri:]
    cols = [PERM[t * 128:(t + 1) * 128] for t in order] + [PERM[15360:15376]]
    return np.concatenate(cols)


PERM2 = build_perm2()
P2_GATE = 7168
P2_BETA = 7168 + 6144
P2_ALPHA = P2_BETA + 8


_PROGS = {}


def _prog(name, fn):
    if name not in _PROGS:
        _PROGS[name] = fn()
    return _PROGS[name]


def _run(nc, in_maps):
    res = run_bass_kernel_spmd(nc, in_maps, core_ids=list(range(len(in_maps))))
    return res.results


def _c(a):
    return np.ascontiguousarray(a, dtype=np.float32)


def kernel(x, c, positions, w_ada, b_ada, norm_mix_w, norm_mlp_w, w_in, conv_qkv_w,
           gdn_a_log, gdn_dt_bias, gdn_norm_w, conv_dw_w, conv_dw_b, conv_ln_w, conv_ln_b,
           w_branch_a, w_branch_b, w_branch_c, w_out, w_mlp_in, w_mlp_out, final_norm_w):
    NCORE = 8
    TT = 1024
    x = np.asarray(x, np.float32)[0]
    positions = np.asarray(positions)
    nc = _prog("ada", build_ada)
    ims = [{"cT": pk(np.asarray(c, np.float32)[0]),
            "wa": _c(np.asarray(w_ada)[:, :, j * 1536:(j + 1) * 1536]),
            "ba": _c(np.asarray(b_ada)[:, j * 1536:(j + 1) * 1536])} for j in range(NCORE)]
    res = _run(nc, ims)
    mod = np.concatenate([r["mod"] for r in res], axis=1)
    xT = [_c(x[j * TT:(j + 1) * TT].T) for j in range(NCORE)]
    gcst = gdn_consts()
    rcst = ret_consts()
    pos_tok = np.ascontiguousarray(positions[0].reshape(-1, 128).T.astype(np.int32))
    depth = np.asarray(w_in).shape[0]
    for l in range(depth):
        vec = make_vec(mod[l], np.asarray(norm_mix_w)[l], np.asarray(norm_mlp_w)[l], np.asarray(final_norm_w))
        w_in_p = _c(np.asarray(w_in)[l][:, PERM2])
        cw, cvec = make_convc_params(np.asarray(conv_dw_w)[l], np.asarray(conv_dw_b)[l],
                                     np.asarray(conv_ln_w)[l], np.asarray(conv_ln_b)[l])
        ims = []
        for j in range(NCORE):
            xh = np.zeros((2048, 32), np.float32)
            if j > 0:
                xh[:] = xT[j - 1][:, TT - 32:]
            ims.append({"xT": xT[j], "xhT": xh, "vec": vec, "flag": np.full((128, 1), 1.0 if j > 0 else 0.0, np.float32),
                        "cw": cw, "cvec": cvec, "w_in": w_in_p})
        res = _run(_prog("pre2", build_pre2), ims)
        projT = np.concatenate([r["projT"] for r in res], axis=1)
        ocT = [r["ocT"] for r in res]
        del res, w_in_p, ims
        cqw = np.asarray(conv_qkv_w)[l]
        ims = []
        for hd in range(NCORE):
            def padT(r0):
                o = np.zeros((128, 8192 + 4), np.float32)
                o[:, 4:] = projT[r0 + hd * 128:r0 + (hd + 1) * 128]
                return o
            cwv = np.zeros((128, 12), np.float32)
            for i in range(3):
                cwv[:, i * 4:(i + 1) * 4] = cqw[:, i * 1024 + hd * 128:i * 1024 + (hd + 1) * 128].T
            hv = np.zeros((128, 4), np.float32)
            hv[:, 0] = np.asarray(gdn_a_log)[l][hd]
            hv[:, 1] = np.asarray(gdn_dt_bias)[l][hd]
            hv[:, 2] = np.asarray(gdn_norm_w)[l]
            brow = projT[P2_BETA + hd]
            arow = projT[P2_ALPHA + hd]
            ims.append({"qT": padT(P_AQ), "kT": padT(P_AK), "vT": padT(P_AV),
                        "zT": _c(projT[P_AZ + hd * 128:P_AZ + (hd + 1) * 128]),
                        "brow": _c(np.broadcast_to(brow[None, :], (128, 8192))),
                        "btok": _c(brow.reshape(-1, 128).T), "atok": _c(arow.reshape(-1, 128).T),
                        "cwv": cwv, "hv": hv, "cst": gcst})
        res = _run(_prog("gdn", build_gdn), ims)
        oaT = np.concatenate([r["oaT"] for r in res], axis=0)
        ims = []
        for j in range(NCORE):
            head, half = j // 2, j % 2

            def tokm(r0):
                a = projT[r0:r0 + 128]
                return _c(a.reshape(128, 64, 128).transpose(2, 1, 0))
            ims.append({"q_tok": tokm(P_BQ + head * 128), "k_tok": tokm(P_BK + head * 128),
                        "v_tok": tokm(P_BV + head * 256 + half * 128), "pos_tok": pos_tok,
                        "rc": ret_rc(head), "cst": rcst})
        res = _run(_prog("ret", build_ret), ims)
        oretT = np.concatenate([r["oretT"] for r in res], axis=0)
        ims = [{"oretT": _c(oretT[:, j * TT:(j + 1) * TT]), "bgT": _c(projT[P_BG:P_BG + 1024, j * TT:(j + 1) * TT])}
               for j in range(NCORE)]
        res = _run(_prog("retln", build_retln), ims)
        obT = [r["obT"] for r in res]
        wba, wbb, wbc = _c(np.asarray(w_branch_a)[l]), _c(np.asarray(w_branch_b)[l]), _c(np.asarray(w_branch_c)[l])
        wo = _c(np.asarray(w_out)[l])
        ims = [{"xT": xT[j], "vec": vec, "oaT": _c(oaT[:, j * TT:(j + 1) * TT]), "obT": obT[j], "ocT": ocT[j],
                "gT": _c(projT[P2_GATE:P2_GATE + 6144, j * TT:(j + 1) * TT]),
                "wba": wba, "wbb": wbb, "wbc": wbc, "w_out": wo} for j in range(NCORE)]
        res = _run(_prog("merge", build_merge), ims)
        x1T = [r["x1T"] for r in res]
        del projT
        final = (l == depth - 1)
        w1, w2 = _c(np.asarray(w_mlp_in)[l]), _c(np.asarray(w_mlp_out)[l])
        ims = [{"xT": x1T[j], "vec": vec, "w1": w1, "w2": w2} for j in range(NCORE)]
        res = _run(_prog("mlpF" if final else "mlp", (lambda: build_mlp(True)) if final else (lambda: build_mlp(False))), ims)
        xT = [r["x2T"] for r in res]
    out = np.concatenate([t.T for t in xT], axis=0)[None]
    return np.ascontiguousarray(out, dtype=np.float32)
```
